# Optimizing a Trainium2 kernel written in Bass

```python
import math
import jax, jax.numpy as jnp
from jax import lax
import numpy as np

D_MODEL = 1024
BATCH = 8
SEQ = 2048
DEPTH = 1
DEC_BATCH = 128
DEC_SEQ = 4
PAST_LEN = 2048
PAGE_SIZE = 128

NSA_HEADS = 8
NSA_KV = 2
NSA_HD = 64
NSA_REP = NSA_HEADS // NSA_KV
CMP_LEN = 32
CMP_STRIDE = 16
CMP_HID = 64
SEL_BLOCK = 64
SEL_TOPN = 8
WINDOW = 512
Q_BLOCK = 128
HG_HEADS = 4
HG_DK = 128
HG_DV = 128
HG_CHUNK = 64
N_BUCKETS = 32
MAX_DIST = 128
EPS = 1e-6
NEG = -1e30
FORCED = 1e9

A_WIDTH = NSA_HEADS * NSA_HD
A_KV = NSA_KV * NSA_HD
B_WIDTH = HG_HEADS * HG_DV
B_KEY = HG_HEADS * HG_DK
SPLIT_SIZES = (A_WIDTH, 6 * A_KV, 3 * NSA_HEADS, A_WIDTH, B_KEY, B_KEY, B_WIDTH, B_WIDTH, D_MODEL, D_MODEL)
IN_COLS = sum(SPLIT_SIZES)

kernel_name = 'nsa_hgrn2_gated_hybrid_step'


def rmsnorm(x, g):
    x32 = x.astype(jnp.float32)
    return x32 * lax.rsqrt(jnp.mean(x32 * x32, axis=-1, keepdims=True) + EPS) * g.astype(jnp.float32)


def t5_bucket(rel):
    n = jnp.maximum(rel, 0)
    exact = N_BUCKETS // 2
    nf = jnp.maximum(n, 1).astype(jnp.float32)
    large = exact + (jnp.log(nf / exact) / math.log(MAX_DIST / exact) * (N_BUCKETS - exact)).astype(jnp.int32)
    large = jnp.minimum(large, N_BUCKETS - 1)
    return jnp.where(n < exact, n, large)


def masked_softmax(s, mask):
    s = jnp.where(mask, s.astype(jnp.float32), NEG)
    p = jax.nn.softmax(s, axis=-1)
    return jnp.where(mask, p, 0.0)


def compress(kv, pos, w1, w2):
    B, L = kv.shape[:2]
    nc = (L - CMP_LEN) // CMP_STRIDE + 1
    idx = np.arange(nc)[:, None] * CMP_STRIDE + np.arange(CMP_LEN)[None, :]
    blk = kv[:, idx] + pos[:, None, :]
    blk = jnp.swapaxes(blk, 2, 3).reshape(B, nc, NSA_KV, CMP_LEN * NSA_HD)
    out = jax.nn.silu(blk @ w1) @ w2
    end = jnp.asarray(np.arange(nc) * CMP_STRIDE + CMP_LEN - 1, jnp.int32)
    return out, end


def cover_matrix(nc, nb):
    cs = np.arange(nc) * CMP_STRIDE
    ce = cs + CMP_LEN
    bs = np.arange(nb) * SEL_BLOCK
    be = bs + SEL_BLOCK
    ov = np.clip(np.minimum(ce[:, None], be[None, :]) - np.maximum(cs[:, None], bs[None, :]), 0, None) / CMP_STRIDE
    return jnp.asarray(ov, jnp.float32)


def nsa_context(kc, vc, ks, vs, pos_k, w1_k, w2_k, pos_v, w1_v, w2_v):
    B, L = kc.shape[:2]
    k_cmp, cend = compress(kc, pos_k, w1_k, w2_k)
    v_cmp, _ = compress(vc, pos_v, w1_v, w2_v)
    pad = (-L) % SEL_BLOCK
    nb = (L + pad) // SEL_BLOCK
    padw = ((0, 0), (0, pad), (0, 0), (0, 0))
    ks_blk = jnp.pad(ks, padw).reshape(B, nb, SEL_BLOCK, NSA_KV, NSA_HD)
    vs_blk = jnp.pad(vs, padw).reshape(B, nb, SEL_BLOCK, NSA_KV, NSA_HD)
    return (k_cmp, v_cmp, cend, cover_matrix(k_cmp.shape[1], nb), ks_blk, vs_blk)


def nsa_attend(q, gates, qpos, kc, vc, cmp_end, cover, ks_blk, vs_blk, kw, vw, kwpos, rel_bias):
    B, T = q.shape[:2]
    qg = q.reshape(B, T, NSA_KV, NSA_REP, NSA_HD).transpose(0, 2, 3, 1, 4) * (NSA_HD ** -0.5)
    table = rel_bias.T.reshape(NSA_KV, NSA_REP, N_BUCKETS)
    sc = jnp.einsum('bgrtd,bngd->bgrtn', qg, kc) + table[:, :, t5_bucket(qpos[:, None] - cmp_end[None, :])]
    pc = masked_softmax(sc, cmp_end[None, :] <= qpos[:, None])
    o_cmp = jnp.einsum('bgrtn,bngd->bgrtd', pc, vc)
    imp = jnp.einsum('bgrtn,nj->bgtj', pc, cover)
    nb = ks_blk.shape[1]
    j = jnp.arange(nb)[None, :]
    cur = (qpos // SEL_BLOCK)[:, None]
    valid = j * SEL_BLOCK <= qpos[:, None]
    forced = (j == 0) | (j == cur) | (j == cur - 1)
    score = jnp.where(valid, jnp.where(forced, FORCED, imp), -1.0)
    _, idx = lax.top_k(score, min(SEL_TOPN, nb))
    n_sel = idx.shape[-1]
    sel_ok = idx * SEL_BLOCK <= qpos[None, None, :, None]
    b_ix = jnp.arange(B)[:, None, None, None]
    g_ix = jnp.arange(NSA_KV)[None, :, None, None]
    ksg = ks_blk.transpose(0, 3, 1, 2, 4)[b_ix, g_ix, idx]
    vsg = vs_blk.transpose(0, 3, 1, 2, 4)[b_ix, g_ix, idx]
    kpos = idx[..., None] * SEL_BLOCK + jnp.arange(SEL_BLOCK)
    ms = (kpos <= qpos[None, None, :, None, None]) & sel_ok[..., None]
    bk = t5_bucket(qpos[None, None, :, None, None] - kpos)
    bias_s = table[jnp.arange(NSA_KV).reshape(1, NSA_KV, 1, 1, 1, 1),
                   jnp.arange(NSA_REP).reshape(1, 1, NSA_REP, 1, 1, 1), bk[:, :, None]]
    ss = jnp.einsum('bgrtd,bgtnsd->bgrtns', qg, ksg) + bias_s
    nk = n_sel * SEL_BLOCK
    ps = masked_softmax(ss.reshape(B, NSA_KV, NSA_REP, T, nk), ms.reshape(B, NSA_KV, 1, T, nk))
    o_sel = jnp.einsum('bgrtk,bgtkd->bgrtd', ps, vsg.reshape(B, NSA_KV, T, nk, NSA_HD))
    rel_w = qpos[:, None] - kwpos[None, :]
    mw = (rel_w >= 0) & (rel_w < WINDOW) & (kwpos[None, :] >= 0)
    sw = jnp.einsum('bgrtd,blgd->bgrtl', qg, kw) + table[:, :, t5_bucket(rel_w)]
    o_win = jnp.einsum('bgrtl,blgd->bgrtd', masked_softmax(sw, mw), vw)
    g = jax.nn.sigmoid(gates.astype(jnp.float32)).reshape(B, T, 3, NSA_KV, NSA_REP).transpose(2, 0, 3, 4, 1)[..., None]
    o = g[0] * o_cmp + g[1] * o_sel + g[2] * o_win
    return o.transpose(0, 3, 1, 2, 4).reshape(B, T, A_WIDTH)


def hgrn2_scan(q, k, v, g, S0):
    B, T, H, _ = q.shape
    C = math.gcd(T, HG_CHUNK)
    nC = T // C
    def to_chunks(a):
        return a.reshape(B, nC, C, H, a.shape[-1]).transpose(1, 0, 3, 2, 4)
    tri = jnp.tril(jnp.ones((C, C), bool))[:, :, None]
    def step(S, xs):
        qc, kc, vc, gc = xs
        G = jnp.cumsum(gc, axis=2)
        o_inter = jnp.einsum('bhtd,bhde->bhte', qc * jnp.exp(G), S)
        D = jnp.where(tri, G[:, :, :, None, :] - G[:, :, None, :, :], -jnp.inf)
        A = jnp.einsum('bhtd,bhtsd,bhsd->bhts', qc, jnp.exp(D), kc)
        o = o_inter + jnp.einsum('bhts,bhse->bhte', A, vc)
        Gl = G[:, :, -1, :]
        S = jnp.exp(Gl)[..., None] * S + jnp.einsum('bhsd,bhse->bhde', kc * jnp.exp(Gl[:, :, None, :] - G), vc)
        return S, o
    S, o = lax.scan(step, S0, (to_chunks(q), to_chunks(k), to_chunks(v), to_chunks(g)))
    return o.transpose(1, 0, 3, 2, 4).reshape(B, T, H, -1), S


def hgrn2_branch(qB, fB, iB, zB, S0, lb, norm_g):
    B, T, _ = qB.shape
    shp = (B, T, HG_HEADS, HG_DK)
    lbh = lb.reshape(HG_HEADS, HG_DK)
    f = lbh + (1.0 - lbh) * jax.nn.sigmoid(fB.astype(jnp.float32).reshape(shp))
    o, S = hgrn2_scan(qB.astype(jnp.float32).reshape(shp), 1.0 - f,
                      iB.astype(jnp.float32).reshape(B, T, HG_HEADS, HG_DV), jnp.log(f), S0)
    o = o * lax.rsqrt(jnp.mean(o * o, axis=-1, keepdims=True) + EPS)
    return o.reshape(B, T, B_WIDTH) * norm_g * jax.nn.silu(zB), S


def project_inputs(x, norm_g, w_in):
    offs = np.cumsum(SPLIT_SIZES)[:-1].tolist()
    return jnp.split(rmsnorm(x, norm_g) @ w_in, offs, axis=-1)


def merge_out(x, oA, oB, mA, mB, w_ba, w_bb, w_out):
    y = jax.nn.sigmoid(mA) * (oA @ w_ba) + jax.nn.sigmoid(mB) * (oB @ w_bb)
    return x + y @ w_out


def layer_prompt(x, norm_g, w_in, pos_k, w1_k, w2_k, pos_v, w1_v, w2_v, rel_bias, lb, hg_norm_g, w_ba, w_bb, w_out):
    B, T, _ = x.shape
    qA, kvA, gA, zA, qB, fB, iB, zB, mA, mB = project_inputs(x, norm_g, w_in)
    q = qA.reshape(B, T, NSA_HEADS, NSA_HD)
    kc, vc, ks, vs, kw, vw = [a.reshape(B, T, NSA_KV, NSA_HD) for a in jnp.split(kvA, 6, axis=-1)]
    ctx = nsa_context(kc, vc, ks, vs, pos_k, w1_k, w2_k, pos_v, w1_v, w2_v)
    padw = ((0, 0), (WINDOW, 0), (0, 0), (0, 0))
    kw_pad = jnp.pad(kw, padw)
    vw_pad = jnp.pad(vw, padw)
    nq = T // Q_BLOCK
    qb = jnp.swapaxes(q.reshape(B, nq, Q_BLOCK, NSA_HEADS, NSA_HD), 0, 1)
    gb = jnp.swapaxes(gA.reshape(B, nq, Q_BLOCK, 3 * NSA_HEADS), 0, 1)
    def one_block(args):
        q_i, g_i, b = args
        start = b * Q_BLOCK
        qpos = start + jnp.arange(Q_BLOCK)
        kwb = lax.dynamic_slice_in_dim(kw_pad, start, WINDOW + Q_BLOCK, axis=1)
        vwb = lax.dynamic_slice_in_dim(vw_pad, start, WINDOW + Q_BLOCK, axis=1)
        kwpos = start - WINDOW + jnp.arange(WINDOW + Q_BLOCK)
        return nsa_attend(q_i, g_i, qpos, *ctx, kwb, vwb, kwpos, rel_bias)
    oA = lax.map(one_block, (qb, gb, jnp.arange(nq)))
    oA = jnp.swapaxes(oA, 0, 1).reshape(B, T, A_WIDTH) * jax.nn.silu(zA)
    S0 = jnp.zeros((B, HG_HEADS, HG_DK, HG_DV), jnp.float32)
    oB, S = hgrn2_branch(qB, fB, iB, zB, S0, lb, hg_norm_g)
    h = merge_out(x, oA, oB, mA, mB, w_ba, w_bb, w_out)
    wb = min(WINDOW, T)
    return h, (kc, vc, ks, vs, kw[:, T - wb:], vw[:, T - wb:], S)


def layer_sample(x, ck, cv, sk, sv, wk, wv, S0, page_table, norm_g, w_in, pos_k, w1_k, w2_k, pos_v, w1_v, w2_v,
                 rel_bias, lb, hg_norm_g, w_ba, w_bb, w_out):
    B, T, _ = x.shape
    qA, kvA, gA, zA, qB, fB, iB, zB, mA, mB = project_inputs(x, norm_g, w_in)
    q = qA.reshape(B, T, NSA_HEADS, NSA_HD)
    kc, vc, ks, vs, kw, vw = [a.reshape(B, T, NSA_KV, NSA_HD) for a in jnp.split(kvA, 6, axis=-1)]
    def with_past(cache, new):
        past = cache[page_table].reshape(B, -1, NSA_KV, NSA_HD)
        return jnp.concatenate([past, new], axis=1)
    ctx = nsa_context(with_past(ck, kc), with_past(cv, vc), with_past(sk, ks), with_past(sv, vs),
                      pos_k, w1_k, w2_k, pos_v, w1_v, w2_v)
    wb = wk.shape[1]
    kw_all = jnp.concatenate([wk, kw], axis=1)
    vw_all = jnp.concatenate([wv, vw], axis=1)
    kwpos = PAST_LEN - wb + jnp.arange(wb + T)
    qpos = PAST_LEN + jnp.arange(T)
    oA = nsa_attend(q, gA, qpos, *ctx, kw_all, vw_all, kwpos, rel_bias) * jax.nn.silu(zA)
    oB, S = hgrn2_branch(qB, fB, iB, zB, S0.astype(jnp.float32), lb, hg_norm_g)
    h = merge_out(x, oA, oB, mA, mB, w_ba, w_bb, w_out)
    return h, (kc, vc, ks, vs, kw_all[:, T:], vw_all[:, T:], S)


def setup_inputs(seed: int = 0) -> dict:
    key = jax.random.key(seed)
    k = jax.random.split(key, 32)
    n_pages = PAST_LEN // PAGE_SIZE
    n_pool = (DEC_BATCH * n_pages * 5) // 4
    win_buf = min(WINDOW, PAST_LEN)
    def nrm(kk, shape, s):
        return jax.random.normal(kk, shape, jnp.float32) * s
    pshape = (DEPTH, n_pool, PAGE_SIZE, NSA_KV, NSA_HD)
    wshape = (DEPTH, DEC_BATCH, win_buf, NSA_KV, NSA_HD)
    page_table = jax.random.permutation(k[9], n_pool)[:DEC_BATCH * n_pages].reshape(DEC_BATCH, n_pages).astype(jnp.int32)
    return {
        'x_prompt': nrm(k[0], (BATCH, SEQ, D_MODEL), 1.0),
        'x_sample': nrm(k[1], (DEC_BATCH, DEC_SEQ, D_MODEL), 1.0),
        'cache_cmp_k': nrm(k[2], pshape, 1.0),
        'cache_cmp_v': nrm(k[3], pshape, 1.0),
        'cache_sel_k': nrm(k[4], pshape, 1.0),
        'cache_sel_v': nrm(k[5], pshape, 1.0),
        'state_win_k': nrm(k[6], wshape, 1.0),
        'state_win_v': nrm(k[7], wshape, 1.0),
        'state_hgrn': nrm(k[8], (DEPTH, DEC_BATCH, HG_HEADS, HG_DK, HG_DV), 1.0),
        'page_table': page_table,
        'norm_g': 1.0 + nrm(k[10], (DEPTH, D_MODEL), 0.02),
        'w_in': nrm(k[11], (DEPTH, D_MODEL, IN_COLS), D_MODEL ** -0.5),
        'cmp_pos_k': nrm(k[12], (DEPTH, CMP_LEN, NSA_HD), 0.5),
        'cmp_w1_k': nrm(k[13], (DEPTH, CMP_LEN * NSA_HD, CMP_HID), (CMP_LEN * NSA_HD) ** -0.5),
        'cmp_w2_k': nrm(k[14], (DEPTH, CMP_HID, NSA_HD), 1.5 * CMP_HID ** -0.5),
        'cmp_pos_v': nrm(k[15], (DEPTH, CMP_LEN, NSA_HD), 0.5),
        'cmp_w1_v': nrm(k[16], (DEPTH, CMP_LEN * NSA_HD, CMP_HID), (CMP_LEN * NSA_HD) ** -0.5),
        'cmp_w2_v': nrm(k[17], (DEPTH, CMP_HID, NSA_HD), 1.5 * CMP_HID ** -0.5),
        'rel_bias': nrm(k[18], (N_BUCKETS, NSA_HEADS), 0.5),
        'hg_lower': nrm(k[19], (DEPTH + 1, B_KEY), 1.0),
        'hg_norm_g': 1.0 + nrm(k[20], (DEPTH, B_WIDTH), 0.02),
        'w_branch_a': nrm(k[21], (DEPTH, A_WIDTH, D_MODEL), A_WIDTH ** -0.5),
        'w_branch_b': nrm(k[22], (DEPTH, B_WIDTH, D_MODEL), B_WIDTH ** -0.5),
        'w_out': nrm(k[23], (DEPTH, D_MODEL, D_MODEL), D_MODEL ** -0.5),
        'final_g': 1.0 + nrm(k[24], (D_MODEL,), 0.02),
    }


def reference(x_prompt, x_sample, cache_cmp_k, cache_cmp_v, cache_sel_k, cache_sel_v, state_win_k, state_win_v,
              state_hgrn, page_table, norm_g, w_in, cmp_pos_k, cmp_w1_k, cmp_w2_k, cmp_pos_v, cmp_w1_v, cmp_w2_v,
              rel_bias, hg_lower, hg_norm_g, w_branch_a, w_branch_b, w_out, final_g):
    lower = jnp.cumsum(jax.nn.softmax(hg_lower.astype(jnp.float32), axis=0), axis=0)
    hp, hs = x_prompt, x_sample
    p_list, s_list = [], []
    for l in range(DEPTH):
        hp, st_p = layer_prompt(hp, norm_g[l], w_in[l], cmp_pos_k[l], cmp_w1_k[l], cmp_w2_k[l], cmp_pos_v[l],
                                cmp_w1_v[l], cmp_w2_v[l], rel_bias, lower[l], hg_norm_g[l],
                                w_branch_a[l], w_branch_b[l], w_out[l])
        hs, st_s = layer_sample(hs, cache_cmp_k[l], cache_cmp_v[l], cache_sel_k[l], cache_sel_v[l],
                                state_win_k[l], state_win_v[l], state_hgrn[l], page_table,
                                norm_g[l], w_in[l], cmp_pos_k[l], cmp_w1_k[l], cmp_w2_k[l], cmp_pos_v[l],
                                cmp_w1_v[l], cmp_w2_v[l], rel_bias, lower[l], hg_norm_g[l],
                                w_branch_a[l], w_branch_b[l], w_out[l])
        p_list.append(st_p)
        s_list.append(st_s)
    p_cmp_k, p_cmp_v, p_sel_k, p_sel_v, p_win_k, p_win_v, p_hgrn = [jnp.stack(s, axis=0) for s in zip(*p_list)]
    s_cmp_k, s_cmp_v, s_sel_k, s_sel_v, s_win_k, s_win_v, s_hgrn = [jnp.stack(s, axis=0) for s in zip(*s_list)]
    y_prompt = rmsnorm(hp, final_g).astype(x_prompt.dtype)
    y_sample = rmsnorm(hs, final_g).astype(x_sample.dtype)
    return (y_prompt, y_sample, p_cmp_k, p_cmp_v, p_sel_k, p_sel_v, p_win_k, p_win_v, p_hgrn,
            s_cmp_k, s_cmp_v, s_sel_k, s_sel_v, s_win_k, s_win_v, s_hgrn)
```

```python
import numpy as np
import contextlib
import concourse.bass as bass
import concourse.mybir as mybir
from concourse.bass_utils import run_bass_kernel_spmd

F32 = mybir.dt.float32; BF16 = mybir.dt.bfloat16; I32 = mybir.dt.int32
AF = mybir.ActivationFunctionType
ALU = mybir.AluOpType
AX = mybir.AxisListType

SAME_ENGINE_SYNC = True
NEG = -30000.0
CAST_SPLIT = True
NTOK = 2112
NT = 17
D = 1024
KC = 8
EPS = 1e-6
C_Q = 0; C_KV = 512; C_G = 1280; C_ZA = 1304; C_QB = 1816; C_FB = 2328; C_IB = 2840; C_ZB = 3352; C_MA = 3864; C_MB = 4888
NCOL = 5912
TW = 768
TOFF = 160


class Tok:
    __slots__ = ("sem", "val", "eng", "slot")

    def __init__(self, sem, val, eng=None, slot=None):
        self.sem = sem; self.val = val; self.eng = eng; self.slot = slot


class Buf:
    __slots__ = ("w", "r", "name")

    def __init__(self, name=""):
        self.w = None; self.r = []; self.name = name


class Eng:
    def __init__(self, ctx, name, eng, is_pe=False):
        self.ctx = ctx; self.name = name; self.e = eng
        self.sem = ctx.new_sem("e_" + name)
        self.cnt = 0
        self.seen = {}
        self.is_pe = is_pe
        self.ninstr = 0

    def wait(self, tok):
        if tok is None:
            return
        if tok.eng is self:
            if self.is_pe or not SAME_ENGINE_SYNC:
                return
        key = id(tok.sem)
        val = tok.val
        if tok.slot is not None:
            val = tok.slot.cnt
            tok.slot.waited_max = max(tok.slot.waited_max, val)
        if self.seen.get(key, 0) >= val:
            return
        self.e.wait_ge(tok.sem, val)
        self.seen[key] = val

    def wait_many(self, toks):
        best = {}
        for t in toks:
            if t is None:
                continue
            k = id(t.sem)
            if k not in best or best[k].val < t.val:
                best[k] = t
        for t in best.values():
            self.wait(t)

    def deps(self, reads, writes):
        toks = [b.w for b in reads]
        for b in writes:
            toks.append(b.w)
            toks.extend(b.r)
        self.wait_many(toks)

    def op(self, fn, reads=(), writes=(), mark=True, **kw):
        self.deps(reads, writes)
        ins = fn(**kw)
        self.ninstr += 1
        if not mark:
            return None
        self.cnt += 1
        ins.then_inc(self.sem, 1)
        tok = Tok(self.sem, self.cnt, self)
        for b in reads:
            b.r.append(tok)
            if len(b.r) > 16:
                b.r = b.r[-16:]
        for b in writes:
            b.w = tok; b.r = []
        return tok


class DmaSlot:
    def __init__(self, ctx, name):
        self.sem = ctx.new_sem("d_" + name); self.cnt = 0; self.waited_max = 0

    def pre_issue(self, q):
        if self.waited_max:
            q.wait(Tok(self.sem, self.waited_max, None, self))

    def issued(self, ins):
        self.cnt += 16
        ins.then_inc(self.sem, 16)
        return Tok(self.sem, self.cnt, None, self)


class Scope:
    def __init__(self, C):
        self.C = C; self.es = contextlib.ExitStack()

    def sbuf(self, name, shape, dt):
        return self.es.enter_context(self.C.nc.sbuf_tensor(name, list(shape), dt))

    def psum(self, name, shape, dt=F32):
        return self.es.enter_context(self.C.nc.psum_tensor(name, list(shape), dt))

    def close(self):
        self.C.barrier()
        self.es.close()


class Ctx:
    def __init__(self, nc):
        self.nc = nc
        self.es = contextlib.ExitStack()
        self.pe = Eng(self, "pe", nc.tensor, is_pe=True)
        self.act = Eng(self, "act", nc.scalar)
        self.dve = Eng(self, "dve", nc.vector)
        self.pool = Eng(self, "pool", nc.gpsimd)
        self.sp = Eng(self, "sp", nc.sync)
        self.engs = (self.pe, self.act, self.dve, self.pool, self.sp)
        self.slots = []
        self.ndma = 0

    def new_sem(self, name):
        return self.es.enter_context(self.nc.semaphore(name))

    def sbuf(self, name, shape, dt):
        return self.es.enter_context(self.nc.sbuf_tensor(name, list(shape), dt))

    def scope(self):
        return Scope(self)

    def slot(self, name):
        s = DmaSlot(self, name); self.slots.append(s); return s

    def dma(self, q, slot, out, in_, reads=(), writes=(), **kw):
        q.deps(reads, writes)
        slot.pre_issue(q)
        ins = q.e.dma_start(out=out, in_=in_, **kw)
        self.ndma += 1
        tok = slot.issued(ins)
        for b in reads:
            b.r.append(tok)
            if len(b.r) > 16:
                b.r = b.r[-16:]
        for b in writes:
            b.w = tok; b.r = []
        return tok

    def barrier(self):
        for e in self.engs:
            for f in self.engs:
                if f is not e and f.cnt and f is not self.sp:
                    e.wait(Tok(f.sem, f.cnt, f))
            for s in self.slots:
                if s.cnt:
                    e.wait(Tok(s.sem, s.cnt, None, s))

    def finish(self):
        self.barrier()
        self.es.close()


def sub(ap, p0, p1):
    return ap[p0:p1]


def rows(t):
    return 128 if t < 16 else 64


def dram_bcast(ap2d, nparts):
    n = ap2d.shape[-1]
    return bass.AP(ap2d.tensor, ap2d.offset, [[0, nparts], [1, n]])


def build(phases=("A", "B", "T", "C", "D", "E", "F")):
    nc = bass.Bass("TRN2", target_bir_lowering=False)
    C = Ctx(nc)

    def din(name, shape, dt=F32):
        return nc.dram_tensor(name, list(shape), dt, kind="ExternalInput").ap()

    def dout(name, shape):
        return nc.dram_tensor(name, list(shape), F32, kind="ExternalOutput").ap()

    def dscr(name, shape, dt=F32):
        return nc.dram_tensor(name, list(shape), dt, kind="Internal").ap()

    x_all = din("x_all", [NTOK, D])
    w_in = din("w_in", [D, NCOL])
    norm_g = din("norm_g", [1, D])
    final_g = din("final_g", [1, D])
    rel_bias = din("rel_bias", [32, 8])
    pos_kv = [din("pos_k", [32, 64]), din("pos_v", [32, 64])]
    w1_kv = [din("w1_k", [2048, 64]), din("w1_v", [2048, 64])]
    w2_kv = [din("w2_k", [64, 64]), din("w2_v", [64, 64])]
    hg_lower = din("hg_lower", [2, 512])
    hg_norm_g = din("hg_norm_g", [1, 512])
    w_ba = din("w_ba", [512, D])
    w_bb = din("w_bb", [512, D])
    w_out = din("w_out", [D, D])
    caches = [din(n, [2560 * 8, 2048]) for n in ("c_ck", "c_cv", "c_sk", "c_sv")]
    st_win = [din("st_wk", [16, 512, 128]), din("st_wv", [16, 512, 128])]
    st_hg = din("st_hg", [16, 4, 128, 128])
    ptab = din("ptab", [1, 256], I32)
    c_ohx = din("c_ohx", [33, TW])
    c_cover = din("c_cover", [127, 32])
    c_selc = din("c_selc", [128, 2, 16, 32])
    c_tri = din("c_tri", [128, 64])
    c_pm8 = din("c_pm8", [128, 1])

    o_y = dout("o_y", [NTOK, D])
    o_pkv = dout("o_pkv", [6, 2048, 128])
    o_pwin = dout("o_pwin", [2, 512, 128])
    o_phg = dout("o_phg", [4, 128, 128])
    o_skv = dout("o_skv", [6, 64, 128])
    o_swin = dout("o_swin", [2, 16, 512, 128])
    o_shg = dout("o_shg", [16, 4, 128, 128])

    xnT = C.sbuf("xnT", [128, KC, NTOK], BF16)
    identb = C.sbuf("identb", [128, 128], BF16)
    identf = C.sbuf("identf", [128, 128], F32)
    wst = [C.sbuf("wst%d" % i, [128, KC, 256], F32) for i in range(2)]
    Bwst = [Buf() for _ in range(2)]
    sl_w = [C.slot("w%d" % i) for i in range(2)]
    sl_misc = C.slot("misc")
    sl_st = [C.slot("st%d" % i) for i in range(2)]
    wring = [0]

    Bid = Buf()
    C.pool.op(nc.gpsimd.memset, writes=[Bid], ap=identf[:], constant=1.0)
    C.pool.op(nc.gpsimd.affine_select, reads=[Bid], writes=[Bid], out=identf[:], in_=identf[:],
              pattern=[[-1, 128]], base=0, channel_multiplier=1, compare_op=ALU.is_equal, fill=0.0)
    C.dve.op(nc.vector.tensor_copy, reads=[Bid], writes=[Bid], out=identb[:], in_=identf[:])

    def load_w(dst, Bdst, src2d, ncols, kcn=KC, cast_eng=None):
        for c0 in range(0, ncols, 256):
            n = min(256, ncols - c0)
            s = wring[0] % 2; wring[0] += 1
            C.dma(C.sp, sl_w[s], wst[s][:, 0:kcn, 0:n],
                  src2d[:, c0:c0 + n].rearrange("(k p) c -> p k c", p=128), writes=[Bwst[s]])
            if cast_eng is None and s == 1 and CAST_SPLIT:
                C.act.op(nc.scalar.copy, reads=[Bwst[s]], writes=[Bdst], out=dst[:, 0:kcn, c0:c0 + n], in_=wst[s][:, 0:kcn, 0:n])
            else:
                e = cast_eng or C.pool
                e.op(e.e.tensor_copy, reads=[Bwst[s]], writes=[Bdst], out=dst[:, 0:kcn, c0:c0 + n], in_=wst[s][:, 0:kcn, 0:n])

    oAT = C.sbuf("oAT", [128, 4, NTOK], BF16)
    oBT = C.sbuf("oBT", [128, 4, NTOK], BF16)
    SAB2 = C.scope()
    BT = [[SAB2.sbuf("BT%d%d" % (g, d), [128, 4, 128], BF16) for d in range(2)] for g in range(2)]
    Mcmp = [SAB2.sbuf("Mcmp%d" % g, [16, 4, 128], BF16) for g in range(2)]
    Wm4 = SAB2.sbuf("Wm4", [128, 4, 128], BF16)
    Iw = SAB2.sbuf("Iw", [16, 144], BF16)
    W1bd = [SAB2.sbuf("W1bd%d" % j, [128, 32, 128], BF16) for j in range(2)]
    cvec = SAB2.sbuf("cvec", [128, 2], F32)
    W2k = SAB2.sbuf("W2k", [128, 2, 64], BF16)
    W2v = SAB2.sbuf("W2v", [128, 128], BF16)
    Kcaug = [SAB2.sbuf("Kcaug%d" % g, [97, 128], BF16) for g in range(2)]
    Vc = SAB2.sbuf("Vc", [128, 2, 97], BF16)
    gat = SAB2.sbuf("gat", [128, NT, 24], F32)
    rbcol = SAB2.sbuf("rbcol", [128, 8], F32)
    qS = [SAB2.sbuf("qS%d" % g, [97, 4, 64], BF16) for g in range(2)]
    KsS = [SAB2.sbuf("KsS%d" % g, [97, 64], BF16) for g in range(2)]
    KwS = [SAB2.sbuf("KwS%d" % g, [97, 64], BF16) for g in range(2)]
    gS = SAB2.sbuf("gS", [16, 16, 3, 2], F32)
    _w1flat = wst[1][:, :, :].rearrange("p a b -> p (a b)")
    bspst = _w1flat[:, 0:512].rearrange("p (a b c) -> p a b c", a=16, b=8)
    mcst = _w1flat[:, 512:544].rearrange("p (a b) -> p a b", a=8)
    idxraw = SAB2.sbuf("idxraw", [128, 16], I32); pm8f = SAB2.sbuf("pm8f", [128, 1], F32)
    BDS = Buf()
    sl_ds = C.slot("dset")
    SAB1 = C.scope()
    qaug = [SAB1.sbuf("qaug%d" % g, [97, 4 * NTOK], BF16) for g in range(2)]
    Ksaug = [SAB1.sbuf("Ksaug%d" % g, [97, NTOK], BF16) for g in range(2)]
    Kwaug = [SAB1.sbuf("Kwaug%d" % g, [97, NTOK], BF16) for g in range(2)]
    kcvT = [SAB1.sbuf("kcT", [128, 2048], BF16), SAB1.sbuf("vcT", [128, 2048], BF16)]
    Vs = SAB1.sbuf("Vs", [128, NT, 2, 65], BF16)
    Vw = SAB1.sbuf("Vw", [128, NT, 2, 65], BF16)
    selc = SAB1.sbuf("selc", [128, 2, 16, 32], F32)
    Baug = Buf()

    if "A" in phases:
        S = C.scope()
        gbc = S.sbuf("gbc", [128, D], F32); Bgbc = Buf()
        xs = [S.sbuf("xs%d" % i, [128, D], F32) for i in range(2)]; Bxs = [Buf(), Buf()]
        xnb = [S.sbuf("xnb%d" % i, [128, D], BF16) for i in range(2)]; Bxnb = [Buf(), Buf()]
        junk = S.sbuf("junk", [128, D], BF16); Bjunk = Buf()
        ss = S.sbuf("ss", [128, NT], F32); sd = S.sbuf("sd", [128, NT], F32); rstd = S.sbuf("rstd", [128, NT], F32)
        pT = [S.psum("pT%d" % i, [128, KC, 128], BF16) for i in range(2)]; BpT = [Buf(), Buf()]
        sl_x = [C.slot("x0"), C.slot("x1")]
        C.dma(C.sp, sl_misc, gbc[:], dram_bcast(norm_g, 128), writes=[Bgbc])
        for g in range(2):
            C.pool.op(nc.gpsimd.memset, writes=[Baug], ap=qaug[g][64:96, :], constant=0.0)
            C.pool.op(nc.gpsimd.memset, writes=[Baug], ap=qaug[g][96:97, :], constant=1.0)
            C.pool.op(nc.gpsimd.memset, writes=[Baug], ap=Kwaug[g][64:96, :], constant=0.0)
            C.pool.op(nc.gpsimd.memset, writes=[Baug], ap=Kwaug[g][96:97, :], constant=1.0)
            C.pool.op(nc.gpsimd.memset, writes=[Baug], ap=Ksaug[g][96:97, :], constant=1.0)
            C.pool.op(nc.gpsimd.memset, writes=[Baug], ap=Ksaug[g][64:96, 0:2048], constant=1.0)
            C.pool.op(nc.gpsimd.memset, writes=[Baug], ap=Ksaug[g][64:96, 2048:NTOK], constant=0.0)
            C.pool.op(nc.gpsimd.affine_select, writes=[Baug], out=Ksaug[g][64:96, 0:2048], in_=Ksaug[g][64:96, 0:2048],
                      pattern=[[1, 2048]], base=0, channel_multiplier=-64, compare_op=ALU.is_ge, fill=0.0)
            C.pool.op(nc.gpsimd.affine_select, writes=[Baug], out=Ksaug[g][64:96, 0:2048], in_=Ksaug[g][64:96, 0:2048],
                      pattern=[[-1, 2048]], base=63, channel_multiplier=64, compare_op=ALU.is_ge, fill=0.0)
        C.pool.op(nc.gpsimd.memset, writes=[Baug], ap=Vs[:].rearrange("p a b c -> p (a b c)"), constant=1.0)
        C.pool.op(nc.gpsimd.memset, writes=[Baug], ap=Vw[:].rearrange("p a b c -> p (a b c)"), constant=1.0)
        C.dma(C.sp, sl_misc, rbcol[96:97, :], rel_bias[31:32, :], writes=[Baug])
        for h in range(8):
            g, r = divmod(h, 4)
            C.dve.op(nc.vector.tensor_scalar, reads=[Baug], writes=[Baug], out=qaug[g][96:97, r * NTOK:(r + 1) * NTOK],
                     in0=qaug[g][96:97, r * NTOK:(r + 1) * NTOK], scalar1=rbcol[96:97, h:h + 1], scalar2=None, op0=ALU.mult)
        Bt = [Buf() for _ in range(NT)]
        for t in range(NT):
            r = rows(t); s = t % 2
            C.dma(C.sp, sl_x[s], xs[s][0:r, :], x_all[t * 128:t * 128 + r, :], writes=[Bxs[s]])
            C.act.op(nc.scalar.activation, reads=[Bxs[s]], writes=[Bjunk, Bt[t]], out=junk[0:r, :], in_=xs[s][0:r, :],
                     func=AF.Square, accum_out=ss[0:r, t:t + 1])
            C.act.op(nc.scalar.activation, reads=[Bt[t]], writes=[Bt[t]], out=sd[0:r, t:t + 1], in_=ss[0:r, t:t + 1],
                     func=AF.Sqrt, scale=1.0 / D, bias=EPS)
            C.dve.op(nc.vector.reciprocal, reads=[Bt[t]], writes=[Bt[t]], out=rstd[0:r, t:t + 1], in_=sd[0:r, t:t + 1])
            C.dve.op(nc.vector.scalar_tensor_tensor, reads=[Bxs[s], Bt[t], Bgbc], writes=[Bxnb[s]], out=xnb[s][0:r, :],
                     in0=xs[s][0:r, :], scalar=rstd[0:r, t:t + 1], in1=gbc[0:r, :], op0=ALU.mult, op1=ALU.mult)
            for kc in range(KC):
                C.pe.op(nc.tensor.transpose, reads=[Bxnb[s], Bid], writes=[BpT[s]], mark=(kc == KC - 1),
                        out=pT[s][:, kc, 0:r], in_=xnb[s][0:r, kc * 128:(kc + 1) * 128], identity=identb[0:r, 0:r])
            e = C.act if t % 2 == 0 else C.dve
            fn = nc.scalar.copy if t % 2 == 0 else nc.vector.tensor_copy
            e.op(fn, reads=[BpT[s]], writes=[Bt[t]], out=xnT[:, :, t * 128:t * 128 + r], in_=pT[s][:, :, 0:r])
        S.close()

    if "B" in phases:
        S = C.scope()
        wkv = S.sbuf("wkv", [128, KC, 792], BF16); Bwkv = Buf()
        wch = [S.sbuf("wch%d" % i, [128, KC, 256], BF16) for i in range(2)]; Bwch = [Buf(), Buf()]
        stg = [S.sbuf("stg%d" % i, [128, 768], F32) for i in range(2)]; Bstg = [Buf(), Buf()]
        ps = [S.psum("psB%d" % i, [128, 512], F32) for i in range(4)]; Bps = [Buf() for _ in range(4)]
        pring = [0]
        load_w(wkv, Bwkv, w_in[:, C_KV:C_KV + 792], 792)
        Bg = Buf()
        for t in range(NT):
            r = rows(t); s = t % 2
            for gi, (c0, n) in enumerate(((0, 256), (256, 256), (512, 256), (768, 24))):
                b = pring[0] % 4; pring[0] += 1
                for kc in range(KC):
                    C.pe.op(nc.tensor.matmul, reads=[Bwkv], writes=[Bps[b]], mark=(kc == KC - 1), out=ps[b][0:r, 0:n],
                            lhsT=xnT[:, kc, t * 128:t * 128 + r], rhs=wkv[:, kc, c0:c0 + n], start=(kc == 0), stop=(kc == KC - 1))
                if gi < 3:
                    if gi % 2 == 0:
                        C.act.op(nc.scalar.copy, reads=[Bps[b]], writes=[Bstg[s]], out=stg[s][0:r, c0:c0 + n], in_=ps[b][0:r, 0:n])
                    else:
                        C.dve.op(nc.vector.tensor_copy, reads=[Bps[b]], writes=[Bstg[s]], out=stg[s][0:r, c0:c0 + n], in_=ps[b][0:r, 0:n])
                else:
                    C.act.op(nc.scalar.activation, reads=[Bps[b]], writes=[Bg], out=gat[0:r, t, :], in_=ps[b][0:r, 0:24], func=AF.Sigmoid)
            C.pool.op(nc.gpsimd.tensor_copy, reads=[Bstg[s]], writes=[Baug], out=Vs[0:r, t, :, 0:64],
                      in_=stg[s][0:r, 384:512].rearrange("p (g d) -> p g d", g=2))
            C.pool.op(nc.gpsimd.tensor_copy, reads=[Bstg[s]], writes=[Baug], out=Vw[0:r, t, :, 0:64],
                      in_=stg[s][0:r, 640:768].rearrange("p (g d) -> p g d", g=2))
            if t < 16:
                C.dma(C.sp, sl_st[s], o_pkv[:, t * 128:(t + 1) * 128, :].rearrange("j p c -> p j c"),
                      stg[s][:, :].rearrange("p (j c) -> p j c", j=6), reads=[Bstg[s]])
                if t >= 12:
                    C.dma(C.sp, sl_st[s], o_pwin[:, (t - 12) * 128:(t - 11) * 128, :].rearrange("j p c -> p j c"),
                          stg[s][:, 512:768].rearrange("p (j c) -> p j c", j=2), reads=[Bstg[s]])
            else:
                C.dma(C.sp, sl_st[s], o_skv[:, :, :].rearrange("j p c -> p j c"),
                      stg[s][0:64, :].rearrange("p (j c) -> p j c", j=6), reads=[Bstg[s]])
                for j in range(2):
                    for bb in range(16):
                        C.dma(C.sp, sl_st[s], o_swin[j, bb, 508:512, :], stg[s][4 * bb:4 * bb + 4, 512 + 128 * j:640 + 128 * j], reads=[Bstg[s]])
        def featproj(wt, Bw, M, evac):
            for st in range(5):
                n = 512 if st < 4 else 64
                b = pring[0] % 4; pring[0] += 1
                for kc in range(KC):
                    C.pe.op(nc.tensor.matmul, reads=[Bw], writes=[Bps[b]], mark=(kc == KC - 1), out=ps[b][0:M, 0:n],
                            lhsT=wt[:, kc, :], rhs=xnT[:, kc, st * 512:st * 512 + n], start=(kc == 0), stop=(kc == KC - 1))
                evac(st, n, ps[b], Bps[b])
        ev = [0]

        def evac_to(dst_fn, scale=None):
            def f(st, n, p, Bp):
                dst = dst_fn(st, n)
                if dst is None:
                    return
                M = dst.shape[0]
                ev[0] += 1
                if ev[0] % 2 == 0:
                    C.act.op(nc.scalar.activation, reads=[Bp], writes=[Baug], out=dst, in_=p[0:M, 0:n], func=AF.Copy,
                             scale=(scale if scale is not None else 1.0))
                else:
                    C.dve.op(nc.vector.tensor_scalar, reads=[Bp], writes=[Baug], out=dst, in0=p[0:M, 0:n],
                             scalar1=(scale if scale is not None else 1.0), scalar2=None, op0=ALU.mult)
            return f
        wi = [0]

        def next_w(c0, n):
            s = wi[0] % 2; wi[0] += 1
            load_w(wch[s], Bwch[s], w_in[:, c0:c0 + n], n)
            return wch[s], Bwch[s]
        for piece in range(2):
            wt, Bw = next_w(C_Q + piece * 256, 256)
            for hh in range(4):
                h = piece * 4 + hh; g, r = divmod(h, 4)
                featproj(wt[:, :, hh * 64:(hh + 1) * 64], Bw, 64,
                         evac_to(lambda st, n, g=g, r=r: qaug[g][0:64, r * NTOK + st * 512:r * NTOK + st * 512 + n], scale=0.125))
        wt, Bw = next_w(C_KV, 256)
        for j in range(2):
            featproj(wt[:, :, j * 128:(j + 1) * 128], Bw, 128,
                     evac_to(lambda st, n, j=j: (kcvT[j][:, st * 512:st * 512 + n] if st < 4 else None)))
        wt, Bw = next_w(C_KV + 256, 128)
        for g in range(2):
            featproj(wt[:, :, g * 64:(g + 1) * 64], Bw, 64, evac_to(lambda st, n, g=g: Ksaug[g][0:64, st * 512:st * 512 + n]))
        wt, Bw = next_w(C_KV + 512, 128)
        for g in range(2):
            featproj(wt[:, :, g * 64:(g + 1) * 64], Bw, 64, evac_to(lambda st, n, g=g: Kwaug[g][0:64, st * 512:st * 512 + n]))
        S.close()

    G_scr = dscr("G_scr", [8, 128, TW])
    sl_t = C.slot("tbl")

    def toep(h, off, pstep, nparts, n):
        return bass.AP(G_scr.tensor, h * 128 * TW + off, [[pstep, nparts], [1, n]])

    if "T" in phases:
        S = C.scope()
        rbx = S.sbuf("rbx", [33, 8], F32); ohx = S.sbuf("ohx", [33, TW], F32); lh = S.sbuf("lh", [33, 8, 128], F32)
        gb = [S.sbuf("gb%d" % i, [128, TW], F32) for i in range(2)]; Bgb = [Buf(), Buf()]
        tz = wst[1][:, :, :].rearrange("p a (b c) -> p a b c", b=2); tzc = S.sbuf("tzc", [16, 8, 128], F32)
        w1s2 = [wst[jj][:, :, :].rearrange("p a (b c) -> p (a b) c", b=4) for jj in range(2)]; posT = S.sbuf("posT", [128, 32], F32); posTb = S.sbuf("posTb", [128, 32], BF16)
        w2s = S.sbuf("w2s", [128, 2, 64], F32); cvs = S.sbuf("cvs", [128, 32], F32)
        psT = [S.psum("psT%d" % i, [128, 512], F32) for i in range(2)]; BpsT = [Buf(), Buf()]
        Bt_ = Buf(); Bw = Buf(); sl_w1 = [C.slot("w1a"), C.slot("w1b")]
        for g in range(2):
            C.pool.op(nc.gpsimd.memset, writes=[Bw], ap=Kcaug[g][64:96, :], constant=0.0)
            C.pool.op(nc.gpsimd.memset, writes=[Bw], ap=Kcaug[g][96:97, :], constant=1.0)
        C.pool.op(nc.gpsimd.memset, writes=[Bw], ap=Vc[:].rearrange("p a b -> p (a b)"), constant=1.0)
        C.pool.op(nc.gpsimd.memset, writes=[Bw], ap=W2k[:].rearrange("p a b -> p (a b)"), constant=0.0)
        C.pool.op(nc.gpsimd.memset, writes=[Bw], ap=W2v[:], constant=0.0)
        Bcs = Buf()
        cov = S.sbuf("cov", [128, 32], F32)
        C.dma(C.sp, sl_t, cov[0:127, :], c_cover, writes=[Bcs])
        for g in range(2):
            C.dve.op(nc.vector.tensor_copy, reads=[Bcs], writes=[Bw], out=Vc[0:127, g, 65:97], in_=cov[0:127, :])
        Bs = Buf(); Bs2 = [Buf(), Buf()]
        for j in range(2):
            for half in range(2):
                C.dma(C.sp, sl_w1[j], w1s2[j][half * 64:(half + 1) * 64, :, :], w1_kv[j].rearrange("(l d) h -> d l h", d=64), writes=[Bs2[j]])
        for j in range(2):
            w1s = w1s2[j]
            for half in range(2):
                with nc.allow_non_contiguous_dma(reason="tiny transposed load of the compression position table"):
                    C.dma(C.sp, sl_t, posT[half * 64:(half + 1) * 64, :], pos_kv[j].rearrange("l d -> d l"), writes=[Bs])
                C.dma(C.sp, sl_t, w2s[half * 64:(half + 1) * 64, j, :], w2_kv[j], writes=[Bs])
            C.pool.op(nc.gpsimd.memset, writes=[Bw], ap=W1bd[j][:].rearrange("p a b -> p (a b)"), constant=0.0)
            C.dve.op(nc.vector.tensor_copy, reads=[Bs2[j]], writes=[Bw], out=W1bd[j][0:64, :, 0:64], in_=w1s[0:64, :, :])
            C.dve.op(nc.vector.tensor_copy, reads=[Bs2[j]], writes=[Bw], out=W1bd[j][64:128, :, 64:128], in_=w1s[64:128, :, :])
            C.dve.op(nc.vector.tensor_copy, reads=[Bs], writes=[Bw], out=posTb[:], in_=posT[:])
            if j == 0:
                for g in range(2):
                    C.dve.op(nc.vector.tensor_copy, reads=[Bs], writes=[Bw], out=W2k[g * 64:(g + 1) * 64, g, :], in_=w2s[g * 64:(g + 1) * 64, 0, :])
            else:
                for g in range(2):
                    C.dve.op(nc.vector.tensor_copy, reads=[Bs], writes=[Bw], out=W2v[g * 64:(g + 1) * 64, g * 64:(g + 1) * 64],
                             in_=w2s[g * 64:(g + 1) * 64, 1, :])
            for l in range(32):
                C.pe.op(nc.tensor.matmul, reads=[Bw], writes=[BpsT[0]], mark=(l == 31), out=psT[0][:, 0:1], lhsT=W1bd[j][:, l, :],
                        rhs=posTb[:, l:l + 1], start=(l == 0), stop=(l == 31))
            C.dve.op(nc.vector.tensor_copy, reads=[BpsT[0]], writes=[Bw], out=cvec[:, j:j + 1], in_=psT[0][:, 0:1])
        C.pool.op(nc.gpsimd.memset, writes=[Bt_], ap=rbx[32:33, :], constant=1.0)
        C.dma(C.sp, sl_t, rbx[0:32, :], rel_bias, writes=[Bt_])
        C.dma(C.sp, sl_t, ohx[:], c_ohx, writes=[Bt_])
        C.dma(C.sp, sl_t, selc[:].rearrange("p a b c -> p (a b c)"), c_selc.rearrange("p a b c -> p (a b c)"), writes=[Bt_])
        C.dve.op(nc.vector.tensor_copy, reads=[Bt_], writes=[Bt_], out=lh[:],
                 in_=bass.AP(rbx[:].tensor, rbx[:].offset, [list(rbx[:].ap[0]), [1, 8], [0, 128]]))
        BGs = []
        sl_gb = [C.slot('gb0'), C.slot('gb1')]
        for h in range(8):
            s = h % 2
            for ci, (c0, n) in enumerate(((0, 512), (512, 256))):
                C.pe.op(nc.tensor.matmul, reads=[Bt_], writes=[BpsT[ci]], out=psT[ci][:, 0:n], lhsT=lh[:, h, :], rhs=ohx[:, c0:c0 + n],
                        start=True, stop=True)
                if ci == 0:
                    C.act.op(nc.scalar.copy, reads=[BpsT[ci]], writes=[Bgb[s]], out=gb[s][:, c0:c0 + n], in_=psT[ci][:, 0:n])
                else:
                    C.dve.op(nc.vector.tensor_copy, reads=[BpsT[ci]], writes=[Bgb[s]], out=gb[s][:, c0:c0 + n], in_=psT[ci][:, 0:n])
            BGh = Buf()
            C.dma(C.sp, sl_gb[s], G_scr[h], gb[s][:], reads=[Bgb[s]], writes=[BGh]); BGs.append(BGh)
        Btz = Buf()
        sl_tz = C.slot("tz")
        C.sp.wait_many(Bs2[1].r + [Bs2[1].w])
        for h in range(8):
            for d in range(2):
                tok = C.dma(C.sp, sl_tz, tz[:, h, d, :], toep(h, TOFF + 128 * d, TW - 1, 128, 128), reads=BGs)
            tok = C.dma(C.sp, sl_tz, tzc[:, h, :], toep(h, TOFF + 97, TW - 16, 16, 128), reads=BGs)
        Btz.w = tok
        for g in range(2):
            for d in range(2):
                C.dve.op(nc.vector.tensor_copy, reads=[Btz], writes=[Bw], out=BT[g][d][:], in_=tz[:, 4 * g:4 * g + 4, d, :])
            C.dve.op(nc.vector.tensor_copy, reads=[Btz], writes=[Bw], out=Mcmp[g][:], in_=tzc[:, 4 * g:4 * g + 4, :])
        C.pool.op(nc.gpsimd.memset, writes=[Bw], ap=Wm4[:].rearrange("p a b -> p (a b)"), constant=0.0)
        C.pool.op(nc.gpsimd.affine_select, writes=[Bw], out=Wm4[:], in_=Wm4[:], pattern=[[0, 4], [-1, 128]], base=0,
                  channel_multiplier=1, compare_op=ALU.is_gt, fill=NEG)
        C.pool.op(nc.gpsimd.memset, writes=[Bw], ap=Iw[:], constant=1.0)
        C.pool.op(nc.gpsimd.affine_select, writes=[Bw], out=Iw[:], in_=Iw[:], pattern=[[1, 144]], base=-120,
                  channel_multiplier=-1, compare_op=ALU.is_equal, fill=0.0)
        S.close()

    gs_scr = dscr("gs_scr", [4, 64, 6])
    if "D" in phases:
        Bm0 = Buf()
        C.pool.op(nc.gpsimd.memset, writes=[Bm0], ap=_w1flat[:, 0:544], constant=0.0)
        C.sp.wait(Bm0.w)
        tok = None
        with nc.allow_non_contiguous_dma(reason="tiny setup shuffles (page table spread, gate rows)"):
            for r8 in range(8):
                tok = C.dma(C.sp, sl_ds, idxraw[r8:128:8, :], bass.AP(ptab.tensor, 0, [[1, 16], [16, 16]]))
            tok = C.dma(C.sp, sl_ds, pm8f[:], c_pm8)
            for h in range(8):
                tok = C.dma(C.sp, sl_ds, mcst[96:127, h, :], toep(h, TOFF + 2017 - 16 * 96, TW - 16, 31, 4))
            for tau in range(16):
                for h in range(8):
                    tok = C.dma(C.sp, sl_ds, bspst[120:128, tau, h, :], toep(h, TOFF + 128 - tau, TW - 16, 8, 4))
            Bgs = Buf()
            for r in range(4):
                C.dma(C.sp, sl_ds, gs_scr[r], gat[0:64, 16, r:24:4], writes=[Bgs])
            for r in range(4):
                tok = C.dma(C.sp, sl_ds, gS[4 * r:4 * r + 4, :, :, :].rearrange("p b a g -> p b (a g)"),
                            bass.AP(gs_scr.tensor, r * 384, [[6, 4], [24, 16], [1, 6]]), reads=[Bgs])
        BDS.w = tok

    def d_setup_post(idx, idxf, Mcs, Bsp):
        C.dve.op(nc.vector.tensor_copy, reads=[BDS], writes=[BDS], out=idxf[:], in_=idxraw[:])
        C.dve.op(nc.vector.tensor_scalar, reads=[BDS], writes=[BDS], out=idxf[:], in0=idxf[:], scalar1=8.0, scalar2=pm8f[:, 0:1], op0=ALU.mult, op1=ALU.add)
        C.dve.op(nc.vector.tensor_copy, reads=[BDS], writes=[BDS], out=idx[:], in_=idxf[:])
        for g in range(2):
            C.dve.op(nc.vector.tensor_copy, reads=[BDS], writes=[BDS], out=Mcs[g][:], in_=mcst[:, 4 * g:4 * g + 4, :])
            C.dve.op(nc.vector.tensor_copy, reads=[BDS], writes=[BDS], out=Bsp[g][:], in_=bspst[:, :, 4 * g:4 * g + 4, :])

    def compress(S_ps, BS_ps, rhs_fn, Bsrc, hid, Bhid, Kdst, Vdst, Bdst):
        for j in range(2):
            p = S_ps[j]; Bp = BS_ps[j]
            for l in range(32):
                C.pe.op(nc.tensor.matmul, reads=[Bsrc], writes=[Bp], mark=(l == 31), out=p[:, 0:127], lhsT=W1bd[j][:, l, :],
                        rhs=rhs_fn(j, l), start=(l == 0), stop=(l == 31))
            C.act.op(nc.scalar.activation, reads=[Bp], writes=[Bhid[j]], out=hid[j][:, 0:127], in_=p[:, 0:127], func=AF.Silu,
                     bias=cvec[:, j:j + 1], scale=1.0)
        for g in range(2):
            p = S_ps[g]; Bp = BS_ps[g]
            C.pe.op(nc.tensor.matmul, reads=[Bhid[0]], writes=[Bp], out=p[0:64, 0:127], lhsT=W2k[:, g, :], rhs=hid[0][:, 0:127],
                    start=True, stop=True)
            C.dve.op(nc.vector.tensor_copy, reads=[Bp], writes=[Bdst], out=Kdst[g], in_=p[0:64, 0:127])
        p = S_ps[0]; Bp = BS_ps[0]
        C.pe.op(nc.tensor.matmul, reads=[Bhid[1]], writes=[Bp], out=p[0:127, 0:128], lhsT=hid[1][:, 0:127], rhs=W2v[:, :],
                start=True, stop=True)
        C.act.op(nc.scalar.copy, reads=[Bp], writes=[Bdst], out=Vdst, in_=p[0:127, 0:128].rearrange("p (g d) -> p g d", g=2))

    if "C" in phases:
        S = C.scope()
        psS = [S.psum("psS%d" % i, [128, 512], F32) for i in range(3)]; BpsS = [Buf() for _ in range(3)]
        psO = [S.psum("psO%d" % i, [128, 512], F32) for i in range(4)]; BpsO = [Buf() for _ in range(4)]
        psMX = S.psum("psMX", [128, 512], F32)
        psM = psMX[:, 0:128]; BpsM = Buf()
        psX = psMX[:, 128:384].bitcast(BF16).rearrange("p (j t) -> p j t", t=128); BpsX = Buf()
        hid = [S.sbuf("hid%d" % j, [128, 128], BF16) for j in range(2)]; Bhid = [Buf(), Buf()]
        Pb = [S.sbuf("Pb%d" % i, [128, 512], BF16) for i in range(4)]; BPb = [Buf() for _ in range(4)]
        oacc = S.sbuf("oacc", [128, 512], F32); Boacc = Buf()
        oab = S.sbuf("oab", [128, 512], BF16); Boab = Buf()
        sm = S.sbuf("sm", [128, 64], F32)
        imp = S.sbuf("imp", [128, 32], F32); sc = S.sbuf("sc", [128, 32], F32); mb = [S.sbuf("mb%d" % g, [128, 32], BF16) for g in range(2)]; Bmb = [Buf(), Buf()]
        Bsel = Buf(); BMB = [Buf(), Buf()]; Bc = Buf(); BoAT = Buf()
        compress(psS, BpsS, lambda j, l: kcvT[j][:, l:l + 16 * 126 + 1:16], Bc, hid, Bhid,
                 [Kcaug[g][0:64, 0:127] for g in range(2)], Vc[0:127, :, 0:64], Bc)
        sring = [0]; pring2 = [0]
        q3 = [qaug[g][:, :].rearrange("p (r t) -> p r t", r=4) for g in range(2)]

        def qk_exp(lhsT, K, M, qt, g, extra, reads):
            b = sring[0] % 3; sring[0] += 1
            C.pe.op(nc.tensor.matmul, reads=reads, writes=[BpsS[b]], mark=(extra is None), out=psS[b][0:M, :], lhsT=lhsT,
                    rhs=q3[g][0:K, :, qt * 128:(qt + 1) * 128], start=True, stop=(extra is None))
            if extra is not None:
                el, er = extra
                C.pe.op(nc.tensor.matmul, reads=reads, writes=[BpsS[b]], out=psS[b][0:M, :], lhsT=el, rhs=er, start=False, stop=True)
            pi = pring2[0] % 4; pring2[0] += 1
            C.act.op(nc.scalar.activation, reads=[BpsS[b]], writes=[BPb[pi]], out=Pb[pi][0:M, :], in_=psS[b][0:M, :], func=AF.Exp)
            return Pb[pi], BPb[pi]

        def combine(bi, W, qt, g, br, first):
            O = psO[bi]
            O3 = O[:, 0:4 * W].rearrange("p (r w) -> p r w", r=4)
            C.dve.op(nc.vector.tensor_scalar, reads=[BpsO[bi]], writes=[Bsel], out=sm[:, 0:4], in0=O3[:, :, 64], scalar1=1e-30,
                     scalar2=None, op0=ALU.add)
            C.dve.op(nc.vector.reciprocal, reads=[Bsel], writes=[Bsel], out=sm[:, 0:4], in_=sm[:, 0:4])
            C.dve.op(nc.vector.tensor_tensor, reads=[Bsel], writes=[Bsel], out=sm[:, 4:8], in0=sm[:, 0:4],
                     in1=gat[:, qt, br * 8 + g * 4:br * 8 + g * 4 + 4], op=ALU.mult)
            for r in range(4):
                dst = oacc[:, (g * 4 + r) * 64:(g * 4 + r + 1) * 64]
                if first:
                    C.dve.op(nc.vector.tensor_scalar, reads=[BpsO[bi], Bsel], writes=[Boacc], out=dst, in0=O[:, r * W:r * W + 64],
                             scalar1=sm[:, 4 + r:5 + r], scalar2=None, op0=ALU.mult)
                else:
                    C.dve.op(nc.vector.scalar_tensor_tensor, reads=[BpsO[bi], Bsel], writes=[Boacc], out=dst, in0=O[:, r * W:r * W + 64],
                             scalar=sm[:, 4 + r:5 + r], in1=dst, op0=ALU.mult, op1=ALU.add)

        SKEW = 2
        tiles = []

        def sel_part1(qt, g, bi):
            O3 = psO[bi][:, 0:388].rearrange("p (r w) -> p r w", r=4)
            C.dve.op(nc.vector.tensor_scalar, reads=[BpsO[bi]], writes=[Bsel], out=sm[:, 8:12], in0=O3[:, :, 64], scalar1=1e-30,
                     scalar2=None, op0=ALU.add)
            C.dve.op(nc.vector.reciprocal, reads=[Bsel], writes=[Bsel], out=sm[:, 8:12], in_=sm[:, 8:12])
            for r in range(4):
                if r == 0:
                    C.dve.op(nc.vector.tensor_scalar, reads=[BpsO[bi], Bsel], writes=[Bsel], out=imp[:], in0=psO[bi][:, 65:97],
                             scalar1=sm[:, 8:9], scalar2=None, op0=ALU.mult)
                else:
                    C.dve.op(nc.vector.scalar_tensor_tensor, reads=[BpsO[bi], Bsel], writes=[Bsel], out=imp[:],
                             in0=psO[bi][:, r * 97 + 65:r * 97 + 97], scalar=sm[:, 8 + r:9 + r], in1=imp[:], op0=ALU.mult, op1=ALU.add)
            C.dve.op(nc.vector.tensor_tensor, reads=[Bsel], writes=[Bsel], out=sc[:], in0=imp[:], in1=selc[:, 0, qt, :], op=ALU.mult)
            C.dve.op(nc.vector.tensor_tensor, reads=[Bsel], writes=[Bsel], out=sc[:], in0=sc[:], in1=selc[:, 1, qt, :], op=ALU.add)
            C.dve.op(nc.vector.max, reads=[Bsel], writes=[Bsel], out=sm[:, 16:24], in_=sc[:])
            C.dve.op(nc.vector.tensor_scalar, reads=[Bsel], writes=[Bmb[g]], out=mb[g][:], in0=sc[:], scalar1=sm[:, 23:24], scalar2=NEG,
                     op0=ALU.is_lt, op1=ALU.mult)
            combine(bi, 97, qt, g, 0, True)

        def sel_part2(qt, g):
            C.pe.op(nc.tensor.matmul, reads=[Bmb[g]], writes=[BpsM], out=psM[64:96, 0:128], lhsT=mb[g][:, :], rhs=identb[:, :], start=True, stop=True)
            C.act.op(nc.scalar.copy, reads=[BpsM], writes=[BMB[g]], out=q3[g][64:96, :, qt * 128:(qt + 1) * 128],
                     in_=bass.AP(psM[64:96, 0:128].tensor, psM[64:96, 0:128].offset, [list(psM[64:96, 0:128].ap[0]), [0, 4], [1, 128]]))

        def fin1(qt):
            C.act.op(nc.scalar.copy, reads=[Boacc], writes=[Boab], out=oab[:], in_=oacc[:])

        def fin2(qt):
            for j in range(4):
                C.pe.op(nc.tensor.transpose, reads=[Boab], writes=[BpsX], mark=(j == 3), out=psX[:, j, :], in_=oab[:, j * 128:(j + 1) * 128],
                        identity=identb[:, :])
            C.dve.op(nc.vector.tensor_copy, reads=[BpsX], writes=[BoAT], out=oAT[:, :, qt * 128:(qt + 1) * 128], in_=psX[:, :, :])

        for qt in range(16):
            nwin = min(qt + 1, 5)
            for g in range(2):
                ncv = min(8 * (qt + 1), 127)
                c0 = 128 - 8 * qt
                tiles.append(dict(qt=qt, g=g, lhsT=Kcaug[g][:, 0:ncv], M=ncv, extra=(Iw[0:16, c0:c0 + ncv], Mcmp[g][:, :, :]), reads=[Bc, BMB[g]],
                                  bank=g, W=97, rhs=Vc[0:ncv, g, :], first=True, last=True,
                                  hooks=[(0, lambda qt=qt, g=g: sel_part1(qt, g, g)), (min(3, nwin), lambda qt=qt, g=g: sel_part2(qt, g))]))
            for g in range(2):
                kts = list(range(max(0, qt - 4), qt + 1))
                for i, kt in enumerate(kts):
                    d = qt - kt
                    extra = None
                    if d <= 1:
                        extra = (identb[:, :], BT[g][d][:, :, :])
                    elif d == 4:
                        extra = (identb[:, :], Wm4[:, :, :])
                    tiles.append(dict(qt=qt, g=g, lhsT=Kwaug[g][:, kt * 128:(kt + 1) * 128], M=128, extra=extra, reads=[BMB[g]], bank=2, W=65,
                                      rhs=Vw[:, kt, g, :], first=(i == 0), last=(i == len(kts) - 1),
                                      hooks=([(0, lambda qt=qt, g=g: combine(2, 65, qt, g, 2, False))] if i == len(kts) - 1 else [])))
                for kt in range(qt + 1):
                    d = qt - kt
                    extra = (identb[:, :], BT[g][d][:, :, :]) if d <= 1 else None
                    hooks = []
                    if kt == qt:
                        hooks.append((0, lambda qt=qt, g=g: combine(3, 65, qt, g, 1, False)))
                        if g == 1:
                            hooks.append((0, lambda qt=qt: fin1(qt)))
                            hooks.append((2, lambda qt=qt: fin2(qt)))
                    tiles.append(dict(qt=qt, g=g, lhsT=Ksaug[g][:, kt * 128:(kt + 1) * 128], M=128, extra=extra, reads=[BMB[g]], bank=3, W=65,
                                      rhs=Vs[:, kt, g, :], first=(kt == 0), last=(kt == qt), hooks=hooks))
        pending = {}
        inflight = {}
        nt_ = len(tiles)
        for step in range(nt_ + SKEW + 4):
            j = step - SKEW
            if 0 <= j < nt_:
                T = tiles[j]
                P, BP = inflight.pop(j)
                bi = T["bank"]; W = T["W"]; M = T["M"]
                for r in range(4):
                    C.pe.op(nc.tensor.matmul, reads=[BP] + T["reads"], writes=[BpsO[bi]], mark=(r == 3), out=psO[bi][:, r * W:(r + 1) * W],
                            lhsT=P[0:M, r * 128:(r + 1) * 128], rhs=T["rhs"], start=(T["first"] and r == 0), stop=(T["last"] and r == 3),
                            skip_group_check=True)
                for (dl, fn) in T["hooks"]:
                    pending.setdefault(step + dl, []).append(fn)
            for fn in pending.pop(step, []):
                fn()
            if step < nt_:
                T = tiles[step]
                inflight[step] = qk_exp(T["lhsT"], 97, T["M"], T["qt"], T["g"], T["extra"], T["reads"])
        S.close()
    Bsmp = Buf()
    for g in range(2):
        C.dve.op(nc.vector.tensor_copy, writes=[Bsmp], out=qS[g][:], in_=qaug[g][:, :].rearrange("p (r t) -> p r t", r=4)[:, :, 2048:NTOK])
        C.dve.op(nc.vector.tensor_copy, writes=[Bsmp], out=KsS[g][:], in_=Ksaug[g][:, 2048:NTOK])
        C.dve.op(nc.vector.tensor_copy, writes=[Bsmp], out=KwS[g][:], in_=Kwaug[g][:, 2048:NTOK])
    SAB1.close()


    if "D" in phases:
        S = C.scope()
        oS_scr = dscr("oS_scr", [64, 512])
        pgb = [S.sbuf("pg%d" % c, [128, 16, 128], F32) for c in range(4)]; Bpg = [Buf() for _ in range(4)]
        wkvb = [S.sbuf("wkvb%d" % j, [128, 4, 128], F32) for j in range(2)]; Bwkv_ = [Buf(), Buf()]
        pgbf = [S.sbuf("pgbf%d" % c, [128, 16, 128], BF16) for c in range(2)]; Bpgf = [Buf() for _ in range(3)]
        pgbf.append(wst[1][:, :, :].rearrange("p a b -> p (a b)")[:, 1024:2048].bitcast(BF16).rearrange("p (a b) -> p a b", b=128))
        cT = [S.sbuf("cT%d" % j, [128, 2048], BF16) for j in range(2)]; BcT = Buf()
        KsT = [S.sbuf("KsT%d" % g, [97, 2048], BF16) for g in range(2)]; BKsT = Buf()
        KwT = [S.sbuf("KwT%d" % g, [97, 512], BF16) for g in range(2)]; BKwT = Buf()
        Vsb = S.sbuf("Vsb", [128, 16, 2, 65], BF16); BVsb = Buf()
        Vwb = S.sbuf("Vwb", [128, 4, 2, 65], BF16); BVwb = Buf()
        VnS = S.sbuf("VnS", [4, 16, 2, 2, 65], BF16)
        vnst1 = wst[0][0:4, :, :].rearrange("p a b -> p (a b)").rearrange("p (b c) -> p b c", c=128)
        Ws = S.sbuf("Ws", [128, 4, 4], BF16)
        oS = wst[0][0:16, :, :].rearrange("p a (b c) -> p (a b) c", c=64).rearrange("p (b g) c -> p b g c", g=2); BoS = Buf()
        Rsel = S.sbuf("Rsel", [16, 4], F32)
        selS = S.sbuf("selS", [4, 2, 32], F32)
        hid = [S.sbuf("hidD%d" % j, [128, 128], BF16) for j in range(2)]; Bhid = [Buf(), Buf()]
        Pd = [S.sbuf("Pd%d" % i, [128, 288], BF16) for i in range(3)]; BPd = [Buf() for _ in range(3)]
        smd = S.sbuf("smd", [16, 64], F32); impn = S.sbuf("impn", [16, 32], F32)
        sc4 = S.sbuf("sc4", [4, 32], F32); mb4 = S.sbuf("mb4", [4, 32], BF16)
        Ball_s = [S.sbuf("Ball_s%d" % g, [128, 17, 4, 4], BF16) for g in range(2)]
        Ball_w = [S.sbuf("Ball_w%d" % g, [128, 5, 4, 4], BF16) for g in range(2)]
        oSt = wst[1][0:64, :, :].rearrange("p a (b c) -> p (a b) c", c=64).rearrange("p a c -> p (a c)")[:, 0:512]; oSb = S.sbuf("oSb", [64, 512], BF16)
        psTr = [S.psum("psTr%d" % i, [128, 512], F32) for i in range(2)]; BpsTr = [Buf(), Buf()]
        psC = [S.psum("psC%d" % i, [128, 512], F32) for i in range(2)]; BpsC = [Buf(), Buf()]
        psTr4 = [psTr[0], psTr[1], psC[0], psC[1]]; BpsTr4 = [BpsTr[0], BpsTr[1], BpsC[0], BpsC[1]]
        psSd = [S.psum("psSd%d" % i, [128, 512], F32) for i in range(2)]; BpsSd = [Buf(), Buf()]
        psOd = S.psum("psOd", [128, 512], F32); BpsOd = Buf(); BpsOw = Buf()
        psMd = S.psum("psMd", [128, 512], F32); BpsMd = Buf()
        sl_pg = [C.slot("pg%d" % c) for c in range(4)]; sl_wk = [C.slot("wk0"), C.slot("wk1")]
        Bk = Buf(); BqS = Buf()
        for j in range(2):
            C.dma(C.sp, sl_misc, o_swin[j][:, 0:508, :], st_win[j][:, 4:512, :])
        idx = S.sbuf("idx", [128, 16], I32); idxf = S.sbuf("idxf", [128, 16], F32)
        Mcs = [S.sbuf("Mcs%d" % g, [128, 4, 4], BF16) for g in range(2)]
        Bsp = [S.sbuf("Bsp%d" % g, [128, 16, 4, 4], BF16) for g in range(2)]
        d_setup_post(idx, idxf, Mcs, Bsp)
        for g in range(2):
            C.pool.op(nc.gpsimd.memset, writes=[Bk], ap=KsT[g][64:96, :], constant=1.0)
            E3 = KsT[g][64:96, :].rearrange("p (a m) -> p a m", m=128)
            C.pool.op(nc.gpsimd.affine_select, writes=[Bk], out=E3, in_=E3, pattern=[[0, 16], [1, 128]], base=0,
                      channel_multiplier=-4, compare_op=ALU.is_ge, fill=0.0)
            C.pool.op(nc.gpsimd.affine_select, writes=[Bk], out=E3, in_=E3, pattern=[[0, 16], [-1, 128]], base=3,
                      channel_multiplier=4, compare_op=ALU.is_ge, fill=0.0)
            C.pool.op(nc.gpsimd.memset, writes=[Bk], ap=KsT[g][96:97, :], constant=1.0)
            C.pool.op(nc.gpsimd.memset, writes=[Bk], ap=KwT[g][64:96, :], constant=0.0)
            C.pool.op(nc.gpsimd.memset, writes=[Bk], ap=KwT[g][96:97, :], constant=1.0)
        C.pool.op(nc.gpsimd.memset, writes=[Bk], ap=Vsb[:].rearrange("p a b c -> p (a b c)"), constant=1.0)
        C.pool.op(nc.gpsimd.memset, writes=[Bk], ap=Vwb[:].rearrange("p a b c -> p (a b c)"), constant=1.0)
        C.pool.op(nc.gpsimd.memset, writes=[Bk], ap=VnS[:].rearrange("p a b c d -> p (a b c d)"), constant=1.0)
        Bvn = Buf()
        for jj, j in enumerate((3, 5)):
            C.dma(C.sp, sl_misc, vnst1, o_skv[j].rearrange("(b i) c -> i b c", i=4), writes=[Bvn])
            C.dve.op(nc.vector.tensor_copy, reads=[Bvn, Bk], writes=[Bk], out=VnS[:, :, jj, :, 0:64],
                     in_=vnst1.rearrange("p b (g d) -> p b g d", g=2))
            Bvn.r.append(Bk.w)
        C.pool.op(nc.gpsimd.memset, writes=[Bk], ap=Ws[:].rearrange("p a b -> p (a b)"), constant=0.0)
        C.pool.op(nc.gpsimd.affine_select, writes=[Bk], out=Ws[:], in_=Ws[:], pattern=[[0, 4], [-1, 4]], base=0, channel_multiplier=1,
                  compare_op=ALU.is_gt, fill=NEG)
        for g in range(2):
            C.pool.op(nc.gpsimd.memset, writes=[Bk], ap=Ball_s[g][:].rearrange("p a b c -> p (a b c)"), constant=0.0)
            C.pool.op(nc.gpsimd.memset, writes=[Bk], ap=Ball_w[g][:].rearrange("p a b c -> p (a b c)"), constant=0.0)
            C.dve.op(nc.vector.tensor_copy, reads=[BDS, Bk], writes=[Bk], out=Ball_s[g][:, 0:16, :, :], in_=Bsp[g][:, :, :, :])
            C.dve.op(nc.vector.tensor_copy, reads=[Bk], writes=[Bk], out=Ball_s[g][0:4, 16, :, :], in_=BT[g][0][0:4, :, 0:4])
            C.dve.op(nc.vector.tensor_copy, reads=[Bk], writes=[Bk], out=Ball_w[g][:, 0, :, :], in_=Ws[:, :, :])
            C.dve.op(nc.vector.tensor_copy, reads=[Bk], writes=[Bk], out=Ball_w[g][:, 3, :, :], in_=BT[g][1][:, :, 0:4])
            C.dve.op(nc.vector.tensor_copy, reads=[Bk], writes=[Bk], out=Ball_w[g][0:4, 4, :, :], in_=BT[g][0][0:4, :, 0:4])
        C.dve.op(nc.vector.tensor_copy, reads=[Bid], writes=[Bk], out=Rsel[:], in_=identf[0:16, 0:4])
        for r in range(1, 4):
            C.dve.op(nc.vector.tensor_tensor, reads=[Bid, Bk], writes=[Bk], out=Rsel[:], in0=Rsel[:], in1=identf[0:16, 4 * r:4 * r + 4], op=ALU.add)
        C.pool.op(nc.gpsimd.memset, writes=[Bk], ap=selS[:, 0, :], constant=1.0)
        C.pool.op(nc.gpsimd.memset, writes=[Bk], ap=selS[:, 1, :], constant=0.0)
        for j in (0, 31):
            C.pool.op(nc.gpsimd.memset, writes=[Bk], ap=selS[:, 0, j:j + 1], constant=0.0)
            C.pool.op(nc.gpsimd.memset, writes=[Bk], ap=selS[:, 1, j:j + 1], constant=1e9)
        tr = [0]
        prd = [0]

        def transposes(src_fn, nparts_out, ntiles, dst_fn, Bsrc, Bdst, bf=False):
            for k0 in range(0, ntiles, 4):
                b = tr[0] % 4; tr[0] += 1
                pt = psTr4[b][:, 0:256].bitcast(BF16) if bf else psTr4[b]
                for k in range(k0, k0 + 4):
                    C.pe.op(nc.tensor.transpose, reads=[Bsrc, Bid], writes=[BpsTr4[b]], mark=(k == k0 + 3),
                            out=pt[0:nparts_out, (k - k0) * 128:(k - k0 + 1) * 128], in_=src_fn(k), identity=(identb[:, :] if bf else identf[:, :]))
                if b % 2 == 0:
                    C.act.op(nc.scalar.copy, reads=[BpsTr4[b]], writes=[Bdst], out=dst_fn(k0), in_=pt[0:nparts_out, :])
                else:
                    C.dve.op(nc.vector.tensor_copy, reads=[BpsTr4[b]], writes=[Bdst], out=dst_fn(k0), in_=pt[0:nparts_out, :])

        for bb in range(16):
            for c in range(4):
                for t_ in Bpg[c].r:
                    C.pool.wait(t_)
                C.pool.wait(Bpg[c].w); C.pool.wait(BDS.w); sl_pg[c].pre_issue(C.pool)
                ins = nc.gpsimd.indirect_dma_start(out=pgb[c][:, :, :].rearrange("p a b -> p (a b)"), out_offset=None, in_=caches[c][:, :],
                                                   in_offset=bass.IndirectOffsetOnAxis(ap=idx[:, bb:bb + 1], axis=0))
                C.ndma += 1
                Bpg[c].w = sl_pg[c].issued(ins); Bpg[c].r = []
            for j in range(2):
                C.dma(C.sp, sl_wk[j], wkvb[j][:], st_win[j][bb].rearrange("(t p) c -> p t c", p=128), writes=[Bwkv_[j]])
            for c in range(3):
                if c == 1:
                    C.dve.op(nc.vector.tensor_copy, reads=[Bpg[c]], writes=[Bpgf[c]], out=pgbf[c][:, :, :], in_=pgb[c][:, :, :])
                else:
                    C.act.op(nc.scalar.copy, reads=[Bpg[c]], writes=[Bpgf[c]], out=pgbf[c][:, :, :], in_=pgb[c][:, :, :])
            for j in range(2):
                transposes(lambda k, j=j: pgbf[j][:, k, :], 128, 16, lambda k0, j=j: cT[j][:, k0 * 128:(k0 + 4) * 128], Bpgf[j], BcT, bf=True)
            compress(psC, BpsC, lambda j, l: cT[j][:, (l % 16) * 128 + l // 16:(l % 16) * 128 + l // 16 + 127], BcT, hid, Bhid, [Kcaug[g][0:64, 0:127] for g in range(2)], Vc[0:127, :, 0:64], Bk)
            for g in range(2):
                transposes(lambda k, g=g: pgbf[2][:, k, g * 64:(g + 1) * 64], 64, 16, lambda k0, g=g: KsT[g][0:64, k0 * 128:(k0 + 4) * 128],
                           Bpgf[2], BKsT, bf=True)
                transposes(lambda k, g=g: wkvb[0][:, k, g * 64:(g + 1) * 64], 64, 4, lambda k0, g=g: KwT[g][0:64, :], Bwkv_[0], BKwT)
            C.act.op(nc.scalar.copy, reads=[Bpg[3]], writes=[BVsb], out=Vsb[:, :, :, 0:64],
                     in_=pgb[3][:, :, :].rearrange("p a (g d) -> p a g d", g=2))
            C.dve.op(nc.vector.tensor_copy, reads=[Bwkv_[1]], writes=[BVwb], out=Vwb[:, :, :, 0:64],
                     in_=wkvb[1][:, :, :].rearrange("p a (g d) -> p a g d", g=2))
            for g in range(2):
                q16 = qS[g][:, :, 4 * bb:4 * bb + 4]

                def branch(tiles, W, bank_col, ball=None):
                    sb = prd[0] % 2; pi = prd[0] % 3; prd[0] += 1
                    nt_ = len(tiles)
                    if ball is not None:
                        Mb = ball.shape[0]
                        C.pe.op(nc.tensor.matmul, reads=[Bk], writes=[BpsSd[sb]], mark=False, out=psSd[sb][0:Mb, 0:nt_ * 16], lhsT=identb[0:Mb, 0:Mb],
                                rhs=ball, start=True, stop=False)
                    for ti, (lh, M, extra, vr, rd) in enumerate(tiles):
                        if ball is not None:
                            C.pe.op(nc.tensor.matmul, reads=rd, writes=[BpsSd[sb]], mark=(ti == nt_ - 1), out=psSd[sb][0:M, ti * 16:(ti + 1) * 16],
                                    lhsT=lh, rhs=q16, start=False, stop=(ti == nt_ - 1), skip_group_check=True)
                            continue
                        C.pe.op(nc.tensor.matmul, reads=rd, writes=[BpsSd[sb]], mark=(extra is None and ti == len(tiles) - 1),
                                out=psSd[sb][0:M, ti * 16:(ti + 1) * 16], lhsT=lh, rhs=q16, start=True, stop=(extra is None))
                        if extra is not None:
                            C.pe.op(nc.tensor.matmul, reads=[Bk], writes=[BpsSd[sb]], mark=(ti == len(tiles) - 1),
                                    out=psSd[sb][0:M, ti * 16:(ti + 1) * 16], lhsT=extra[0], rhs=extra[1], start=False, stop=True)
                    C.act.op(nc.scalar.activation, reads=[BpsSd[sb]], writes=[BPd[pi]], out=Pd[pi][:, 0:nt_ * 16], in_=psSd[sb][:, 0:nt_ * 16], func=AF.Exp)
                    for ti, (lh, M, extra, vr, rd) in enumerate(tiles):
                        C.pe.op(nc.tensor.matmul, reads=[BPd[pi]] + rd, writes=[BpsOd], mark=(ti == nt_ - 1),
                                out=psOd[0:16, bank_col:bank_col + W], lhsT=Pd[pi][0:M, ti * 16:(ti + 1) * 16], rhs=vr,
                                start=(ti == 0), stop=(ti == nt_ - 1), skip_group_check=True)

                def combine_s(bank_col, br, first, Bo=None):
                    Bo = Bo or BpsOd
                    C.dve.op(nc.vector.tensor_scalar, reads=[Bo], writes=[Bk], out=smd[:, 0:1], in0=psOd[0:16, bank_col + 64:bank_col + 65],
                             scalar1=1e-30, scalar2=None, op0=ALU.add)
                    C.dve.op(nc.vector.reciprocal, reads=[Bk], writes=[Bk], out=smd[:, 0:1], in_=smd[:, 0:1])
                    C.dve.op(nc.vector.tensor_tensor, reads=[Bk], writes=[Bk], out=smd[:, 1:2], in0=smd[:, 0:1], in1=gS[:, bb, br, g:g + 1], op=ALU.mult)
                    dst = oS[:, bb, g, :]
                    if first:
                        C.dve.op(nc.vector.tensor_scalar, reads=[Bo, Bk], writes=[BoS], out=dst, in0=psOd[0:16, bank_col:bank_col + 64],
                                 scalar1=smd[:, 1:2], scalar2=None, op0=ALU.mult)
                    else:
                        C.dve.op(nc.vector.scalar_tensor_tensor, reads=[Bo, Bk], writes=[BoS], out=dst, in0=psOd[0:16, bank_col:bank_col + 64],
                                 scalar=smd[:, 1:2], in1=dst, op0=ALU.mult, op1=ALU.add)

                sb = prd[0] % 2; pi = prd[0] % 3; prd[0] += 1
                wt = [(KwT[g][:, kt * 128:(kt + 1) * 128], 128, Vwb[:, kt, g, :]) for kt in range(4)]
                wt.append((KwS[g][:, 4 * bb:4 * bb + 4], 4, VnS[0:4, bb, 1, g, :]))
                C.pe.op(nc.tensor.matmul, reads=[Bk], writes=[BpsSd[sb]], mark=False, out=psSd[sb][:, 16:96], lhsT=identb[:, :],
                        rhs=Ball_w[g][:, :, :, :], start=True, stop=False)
                for ti, (lh, M, vr) in enumerate(wt):
                    C.pe.op(nc.tensor.matmul, reads=[BKwT, BqS, Bk], writes=[BpsSd[sb]], mark=False, out=psSd[sb][0:M, 16 + ti * 16:32 + ti * 16],
                            lhsT=lh, rhs=q16, start=False, stop=False, skip_group_check=True)
                C.pe.op(nc.tensor.matmul, reads=[Bk, BqS], writes=[BpsSd[sb]], mark=False, out=psSd[sb][0:127, 0:16], lhsT=Kcaug[g][:, 0:127],
                        rhs=q16, start=False, stop=False, skip_group_check=True)
                C.pe.op(nc.tensor.matmul, reads=[Bk], writes=[BpsSd[sb]], out=psSd[sb][0:127, 0:16], lhsT=identb[0:127, 0:127],
                        rhs=Mcs[g][0:127, :, :], start=False, stop=True, skip_group_check=True)
                C.act.op(nc.scalar.activation, reads=[BpsSd[sb]], writes=[BPd[pi]], out=Pd[pi][:, 0:96], in_=psSd[sb][:, 0:96], func=AF.Exp)
                C.pe.op(nc.tensor.matmul, reads=[BPd[pi], Bk], writes=[BpsOd], out=psOd[0:16, 0:97], lhsT=Pd[pi][0:127, 0:16], rhs=Vc[0:127, g, :],
                        start=True, stop=True, skip_group_check=True)
                for ti, (lh, M, vr) in enumerate(wt):
                    C.pe.op(nc.tensor.matmul, reads=[BPd[pi], BVwb, Bk], writes=[BpsOw], mark=(ti == 4), out=psOd[0:16, 128:193],
                            lhsT=Pd[pi][0:M, 16 + ti * 16:32 + ti * 16], rhs=vr, start=False, stop=(ti == 4), skip_group_check=True)
                C.dve.op(nc.vector.tensor_scalar, reads=[BpsOd], writes=[Bk], out=smd[:, 8:9], in0=psOd[0:16, 64:65], scalar1=1e-30, scalar2=None,
                         op0=ALU.add)
                C.dve.op(nc.vector.reciprocal, reads=[Bk], writes=[Bk], out=smd[:, 8:9], in_=smd[:, 8:9])
                C.dve.op(nc.vector.tensor_scalar, reads=[BpsOd, Bk], writes=[Bk], out=impn[:], in0=psOd[0:16, 65:97], scalar1=smd[:, 8:9],
                         scalar2=None, op0=ALU.mult)
                C.pe.op(nc.tensor.matmul, reads=[Bk], writes=[BpsMd], out=psMd[0:4, 0:32], lhsT=Rsel[:, :], rhs=impn[:, :], start=True, stop=True)
                C.dve.op(nc.vector.tensor_tensor, reads=[BpsMd, Bk], writes=[Bk], out=sc4[:], in0=psMd[0:4, 0:32], in1=selS[:, 0, :], op=ALU.mult)
                C.dve.op(nc.vector.tensor_tensor, reads=[Bk], writes=[Bk], out=sc4[:], in0=sc4[:], in1=selS[:, 1, :], op=ALU.add)
                C.dve.op(nc.vector.max, reads=[Bk], writes=[Bk], out=smd[0:4, 16:24], in_=sc4[:])
                C.dve.op(nc.vector.tensor_scalar, reads=[Bk], writes=[Bk], out=mb4[:], in0=sc4[:], scalar1=smd[0:4, 22:23], scalar2=NEG,
                         op0=ALU.is_lt, op1=ALU.mult)
                C.pe.op(nc.tensor.matmul, reads=[Bk], writes=[BpsMd], out=psMd[64:96, 64:68], lhsT=mb4[:, :], rhs=identb[0:4, 0:4], start=True, stop=True)
                src = psMd[64:96, 64:68]
                C.act.op(nc.scalar.copy, reads=[BpsMd], writes=[BqS], out=qS[g][64:96, :, 4 * bb:4 * bb + 4],
                         in_=bass.AP(src.tensor, src.offset, [list(src.ap[0]), [0, 4], [1, 4]]))
                combine_s(0, 0, True)
                combine_s(128, 2, False, BpsOw)
                tiles = []
                for kt in range(16):
                    extra = (identb[:, :], Bsp[g][:, kt, :, :])
                    tiles.append((KsT[g][:, kt * 128:(kt + 1) * 128], 128, extra, Vsb[:, kt, g, :], [BKsT, BVsb, BqS]))
                tiles.append((KsS[g][:, 4 * bb:4 * bb + 4], 4, (identb[0:4, 0:4], BT[g][0][0:4, :, 0:4]), VnS[0:4, bb, 0, g, :], [Bk, BqS]))
                branch(tiles, 65, 256, Ball_s[g][:, :, :, :])
                combine_s(256, 1, False)
        Bsh = Buf()
        for r in range(4):
            for g in range(2):
                C.dma(C.sp, sl_misc, bass.AP(oS_scr.tensor, g * 256 + r * 64, [[512, 4], [2048, 16], [1, 64]]), oS[4 * r:4 * r + 4, :, g, :],
                      reads=[BoS], writes=[Bsh])
        C.dma(C.sp, sl_misc, oSt, oS_scr, reads=[Bsh], writes=[Bsh])
        C.dve.op(nc.vector.tensor_copy, reads=[Bsh], writes=[Bsh], out=oSb[:], in_=oSt)
        psX3 = psTr[0][:].bitcast(BF16).rearrange("p (j t) -> p j t", t=128)
        for j in range(4):
            C.pe.op(nc.tensor.transpose, reads=[Bsh, Bid], writes=[BpsTr[0]], mark=(j == 3), out=psX3[:, j, 0:64], in_=oSb[0:64, j * 128:(j + 1) * 128],
                    identity=identb[0:64, 0:64])
        C.dve.op(nc.vector.tensor_copy, reads=[BpsTr[0]], writes=[Bsh], out=oAT[:, :, 2048:NTOK], in_=psX3[:, 0:4, 0:64])
        S.close()
    SAB2.close()


    if "E" in phases:
        S = C.scope()
        qdT = S.sbuf("qdT", [128, 4, NTOK], BF16); kdT = S.sbuf("kdT", [128, 4, NTOK], BF16)
        vtok = S.sbuf("vtok", [128, 16, 512], BF16); vS = [S.sbuf("vS%d" % i, [4, 512], BF16) for i in range(2)]; BvS = [Buf(), Buf()]
        wq = S.sbuf("wq", [128, KC, 512], BF16); wf = S.sbuf("wf", [128, KC, 512], BF16); wib = S.sbuf("wib", [128, KC, 512], BF16)
        lbt = S.sbuf("lbt", [128, 2, 4], F32); oml = S.sbuf("oml", [128, 4], F32); noml = S.sbuf("noml", [128, 4], F32)
        rmask = S.sbuf("rmask", [128, 512], F32); rmask4 = S.sbuf("rmask4", [128, 64], F32)
        eGl = S.sbuf("eGl", [128, 4, 32], F32); eGls = S.sbuf("eGls", [128, 4, 16], F32)
        Sst = S.sbuf("Sst", [128, 4, 128], F32); Sbf = S.sbuf("Sbf", [128, 4, 128], BF16)
        hgbc = S.sbuf("hgbc", [128, 512], F32); hgcol = S.sbuf("hgcol", [128, 4], F32)
        tri = S.sbuf("tri", [128, 64], F32)
        tmpf2 = [[S.sbuf("tmpf%d_%d" % (i, k), [128, 512], F32) for i in range(5)] for k in range(2)]
        Btmp2 = [[Buf() for _ in range(5)] for k in range(2)]
        pbank = [S.psum("psE%d" % i, [128, 512], F32) for i in range(8)]; Bpbank = [Buf() for _ in range(8)]
        ps2 = pbank[0:2]; Bps2 = Bpbank[0:2]
        psA = pbank[2][:, 0:256].rearrange("p (h t) -> p h t", h=4); BpsA = Bpbank[2]
        psK = pbank[3][:, 0:256].bitcast(BF16).rearrange("p (h t) -> p h t", h=4); BpsK = Bpbank[3]
        psOo = pbank[4][:, :].rearrange("p (h t) -> p h t", h=4); BpsOo = Bpbank[4]
        psD = [pbank[5 + i][:, :].rearrange("p (h t) -> p h t", h=4) for i in range(2)]; BpsD = Bpbank[5:7]
        psX2 = pbank[7][:, 0:256].bitcast(BF16).rearrange("p (h t) -> p h t", h=4); BpsX2 = Bpbank[7]
        Bw = Buf(); Bc = Buf(); Bqk = Buf(); Bv = Buf(); BoBT = Buf()
        Bwh = [Buf(), Buf()]
        for half in range(2):
            load_w(wf[:, :, half * 256:(half + 1) * 256], Bwh[half], w_in[:, C_FB + half * 256:C_FB + (half + 1) * 256], 256)
            load_w(wq[:, :, half * 256:(half + 1) * 256], Bwh[half], w_in[:, C_QB + half * 256:C_QB + (half + 1) * 256], 256)
        load_w(wib, Bw, w_in[:, C_IB:C_IB + 512], 512)
        with nc.allow_non_contiguous_dma(reason="tiny strided load of the HGRN lower-bound logits"):
            C.dma(C.sp, sl_misc, lbt[:], hg_lower.rearrange("a (h d) -> d a h", d=128), writes=[Bc])
            C.dma(C.sp, sl_misc, hgcol[:], hg_norm_g.rearrange("a (h d) -> d (a h)", d=128), writes=[Bc])
        C.dma(C.sp, sl_misc, hgbc[:], dram_bcast(hg_norm_g, 128), writes=[Bc])
        C.dma(C.sp, sl_misc, tri[:], c_tri, writes=[Bc])
        C.dve.op(nc.vector.tensor_tensor, reads=[Bc], writes=[Bc], out=oml[:], in0=lbt[:, 0, :], in1=lbt[:, 1, :], op=ALU.subtract)
        C.act.op(nc.scalar.activation, reads=[Bc], writes=[Bc], out=oml[:], in_=oml[:], func=AF.Exp)
        C.dve.op(nc.vector.tensor_scalar, reads=[Bc], writes=[Bc], out=oml[:], in0=oml[:], scalar1=1.0, scalar2=None, op0=ALU.add)
        C.dve.op(nc.vector.reciprocal, reads=[Bc], writes=[Bc], out=oml[:], in_=oml[:])
        C.dve.op(nc.vector.tensor_scalar, reads=[Bc], writes=[Bc], out=noml[:], in0=oml[:], scalar1=-1.0, scalar2=None, op0=ALU.mult)
        C.pool.op(nc.gpsimd.memset, writes=[Bc], ap=rmask[:], constant=1.0)
        C.pool.op(nc.gpsimd.memset, writes=[Bc], ap=rmask[:, 0:512:64], constant=0.0)
        C.pool.op(nc.gpsimd.memset, writes=[Bc], ap=rmask4[:], constant=1.0)
        C.pool.op(nc.gpsimd.memset, writes=[Bc], ap=rmask4[:, 0:64:4], constant=0.0)
        C.pool.op(nc.gpsimd.memset, writes=[Bc], ap=Sst[:].rearrange("p a b -> p (a b)"), constant=0.0)
        C.pool.op(nc.gpsimd.memset, writes=[Bc], ap=Sbf[:].rearrange("p a b -> p (a b)"), constant=0.0)
        for h in range(4):
            for st in range(5):
                sneg, lnf, Gt, eG, eGn = tmpf2[(h * 5 + st) % 2]; Btmp = Btmp2[(h * 5 + st) % 2]
                n = 512 if st < 4 else 64
                cs = slice(st * 512, st * 512 + n)
                pr_ = (h * 5 + st) % 4
                ps2 = pbank[2 * pr_:2 * pr_ + 2]; Bps2 = Bpbank[2 * pr_:2 * pr_ + 2]
                for (wt, b) in ((wf, 0), (wq, 1)):
                    for kc in range(KC):
                        C.pe.op(nc.tensor.matmul, reads=[Bwh[h // 2]], writes=[Bps2[b]], mark=(kc == KC - 1), out=ps2[b][:, 0:n],
                                lhsT=wt[:, kc, h * 128:(h + 1) * 128], rhs=xnT[:, kc, cs], start=(kc == 0), stop=(kc == KC - 1))
                C.act.op(nc.scalar.activation, reads=[Bps2[0]], writes=[Btmp[0]], out=sneg[:, 0:n], in_=ps2[0][:, 0:n], func=AF.Exp)
                C.dve.op(nc.vector.tensor_scalar, reads=[Btmp[0]], writes=[Btmp[0]], out=sneg[:, 0:n], in0=sneg[:, 0:n], scalar1=1.0,
                         scalar2=None, op0=ALU.add)
                C.dve.op(nc.vector.reciprocal, reads=[Btmp[0]], writes=[Btmp[0]], out=sneg[:, 0:n], in_=sneg[:, 0:n])
                C.act.op(nc.scalar.activation, reads=[Btmp[0], Bc], writes=[Btmp[1]], out=lnf[:, 0:n], in_=sneg[:, 0:n], func=AF.Ln,
                         scale=noml[:, h:h + 1], bias=1.0)
                C.dve.op(nc.vector.tensor_tensor_scan, reads=[Btmp[1], Bc], writes=[Btmp[2]], out=Gt[:, 0:n],
                         data0=(rmask[:, 0:n] if st < 4 else rmask4[:, 0:n]), data1=lnf[:, 0:n], initial=0.0, op0=ALU.mult, op1=ALU.add)
                C.act.op(nc.scalar.activation, reads=[Btmp[2]], writes=[Btmp[3]], out=eG[:, 0:n], in_=Gt[:, 0:n], func=AF.Exp)
                C.act.op(nc.scalar.activation, reads=[Btmp[2]], writes=[Btmp[4]], out=eGn[:, 0:n], in_=Gt[:, 0:n], func=AF.Exp, scale=-1.0)
                C.dve.op(nc.vector.tensor_tensor, reads=[Bps2[1], Btmp[3]], writes=[Bqk], out=qdT[:, h, cs], in0=ps2[1][:, 0:n], in1=eG[:, 0:n],
                         op=ALU.mult)
                C.dve.op(nc.vector.scalar_tensor_tensor, reads=[Btmp[0], Btmp[4], Bc], writes=[Bqk], out=kdT[:, h, cs], in0=sneg[:, 0:n],
                         scalar=oml[:, h:h + 1], in1=eGn[:, 0:n], op0=ALU.mult, op1=ALU.mult)
                if st < 4:
                    C.dve.op(nc.vector.tensor_copy, reads=[Btmp[3]], writes=[Bqk], out=eGl[:, h, st * 8:(st + 1) * 8], in_=eG[:, 63:512:64])
                else:
                    C.dve.op(nc.vector.tensor_copy, reads=[Btmp[3]], writes=[Bqk], out=eGls[:, h, :], in_=eG[:, 3:64:4])
        ps2 = pbank[0:2]; Bps2 = Bpbank[0:2]
        for t in range(16):
            b = t % 2
            for kc in range(KC):
                C.pe.op(nc.tensor.matmul, reads=[Bw], writes=[Bps2[b]], mark=(kc == KC - 1), out=ps2[b][:, :],
                        lhsT=xnT[:, kc, t * 128:(t + 1) * 128], rhs=wib[:, kc, :], start=(kc == 0), stop=(kc == KC - 1))
            if b == 0:
                C.act.op(nc.scalar.copy, reads=[Bps2[b]], writes=[Bv], out=vtok[:, t, :], in_=ps2[b][:, :])
            else:
                C.dve.op(nc.vector.tensor_copy, reads=[Bps2[b]], writes=[Bv], out=vtok[:, t, :], in_=ps2[b][:, :])
        Am = S.sbuf("Am", [128, 4, 64], BF16); BAm = Buf()
        kdtok = S.sbuf("kdtok", [128, 4, 128], BF16); Bkdtok = Buf()
        ob = S.sbuf("ob", [128, 512], BF16); Bob = Buf()
        sq = S.sbuf("sq", [128, 128], BF16); BS = Buf(); BSbf = Buf()
        nrm = S.sbuf("nrm", [128, 16], F32); Bn = Buf()
        stmp = S.sbuf("stmp", [128, 4, 128], F32)
        psOo_r = [psOo, pbank[0][:, :].rearrange("p (h t) -> p h t", h=4)]; BpsOo_r = [BpsOo, Bpbank[0]]
        psX2_r = [psX2, pbank[1][:, 0:256].bitcast(BF16).rearrange("p (h t) -> p h t", h=4)]; BpsX2_r = [BpsX2, Bpbank[1]]
        for t in range(16):
            psOo = psOo_r[t % 2]; BpsOo = BpsOo_r[t % 2]; psX2 = psX2_r[t % 2]; BpsX2 = BpsX2_r[t % 2]
            for h in range(4):
                for c in range(2):
                    cs = slice(t * 128 + c * 64, t * 128 + c * 64 + 64)
                    C.pe.op(nc.tensor.matmul, reads=[Bqk], writes=[BpsA], mark=(h == 3 and c == 1), out=psA[c * 64:(c + 1) * 64, h, :],
                            lhsT=kdT[:, h, cs], rhs=qdT[:, h, cs], start=True, stop=True)
                C.pe.op(nc.tensor.transpose, reads=[Bqk, Bid], writes=[BpsK], mark=(h == 3), out=psK[:, h, :],
                        in_=kdT[:, h, t * 128:(t + 1) * 128], identity=identb[:, :])
            C.dve.op(nc.vector.tensor_tensor, reads=[BpsA, Bc], writes=[BAm], out=Am[:], in0=psA[:],
                     in1=bass.AP(tri[:].tensor, tri[:].offset, [list(tri[:].ap[0]), [0, 4], [1, 64]]), op=ALU.mult)
            C.act.op(nc.scalar.copy, reads=[BpsK], writes=[Bkdtok], out=kdtok[:], in_=psK[:])
            for c in range(2):
                rs_ = slice(c * 64, (c + 1) * 64)
                d = psD[c]
                for h in range(4):
                    cs = slice(t * 128 + c * 64, t * 128 + c * 64 + 64)
                    C.pe.op(nc.tensor.matmul, reads=[Bqk, BSbf], writes=[BpsOo], mark=False, out=psOo[rs_, h, :], lhsT=qdT[:, h, cs],
                            rhs=Sbf[:, h, :], start=True, stop=False)
                    C.pe.op(nc.tensor.matmul, reads=[BAm, Bv], writes=[BpsOo], mark=(c == 1 and h == 3), out=psOo[rs_, h, :],
                            lhsT=Am[rs_, h, :], rhs=vtok[rs_, t, h * 128:(h + 1) * 128], start=False, stop=True)
                    C.pe.op(nc.tensor.matmul, reads=[Bkdtok, Bv], writes=[BpsD[c]], mark=(h == 3), out=d[:, h, :], lhsT=kdtok[rs_, h, :],
                            rhs=vtok[rs_, t, h * 128:(h + 1) * 128], start=True, stop=True)
                C.dve.op(nc.vector.tensor_tensor, reads=[BpsD[c], BS], writes=[BS], out=stmp[:], in0=d[:], in1=Sst[:], op=ALU.add)
                eg = eGl[:, :, 2 * t + c]
                C.dve.op(nc.vector.tensor_tensor, reads=[BS, Bqk], writes=[BS], out=Sst[:], in0=stmp[:],
                         in1=bass.AP(eg.tensor, eg.offset, [list(eg.ap[0]), list(eg.ap[1]), [0, 128]]), op=ALU.mult)
                C.act.op(nc.scalar.copy, reads=[BS], writes=[BSbf], out=Sbf[:], in_=Sst[:])
            for h in range(4):
                C.act.op(nc.scalar.activation, reads=[BpsOo], writes=[Bn], out=sq[:], in_=psOo[:, h, :], func=AF.Square,
                         accum_out=nrm[:, h:h + 1])
            C.act.op(nc.scalar.activation, reads=[Bn], writes=[Bn], out=nrm[:, 4:8], in_=nrm[:, 0:4], func=AF.Sqrt, scale=1.0 / 128, bias=EPS)
            C.dve.op(nc.vector.reciprocal, reads=[Bn], writes=[Bn], out=nrm[:, 8:12], in_=nrm[:, 4:8])
            for h in range(4):
                C.dve.op(nc.vector.scalar_tensor_tensor, reads=[BpsOo, Bn, Bc], writes=[Bob], out=ob[:, h * 128:(h + 1) * 128],
                         in0=psOo[:, h, :], scalar=nrm[:, 8 + h:9 + h], in1=hgbc[:, h * 128:(h + 1) * 128], op0=ALU.mult, op1=ALU.mult)
            for j in range(4):
                C.pe.op(nc.tensor.transpose, reads=[Bob, Bid], writes=[BpsX2], mark=(j == 3), out=psX2[:, j, :],
                        in_=ob[:, j * 128:(j + 1) * 128], identity=identb[:, :])
            C.act.op(nc.scalar.copy, reads=[BpsX2], writes=[BoBT], out=oBT[:, :, t * 128:(t + 1) * 128], in_=psX2[:, :, :])
        C.dma(C.sp, sl_misc, o_phg.rearrange("h d e -> d h e"), Sst[:], reads=[BS])
        psOo = psOo_r[0]; BpsOo = BpsOo_r[0]; psX2 = psX2_r[0]; BpsX2 = BpsX2_r[0]
        S0 = [S.sbuf("S0_%d" % i, [128, 4, 128], F32) for i in range(2)]; BS0 = [Buf(), Buf()]
        S0b2 = [S.sbuf("S0b%d" % i, [128, 4, 128], BF16) for i in range(2)]; BS0b2 = [Buf(), Buf()]
        Sn = [S.sbuf("Sn%d" % i, [128, 4, 128], F32) for i in range(2)]; BSn = [Buf(), Buf()]
        Am4 = S.sbuf("Am4", [4, 4, 4], BF16); kd4 = S.sbuf("kd4", [4, 4, 128], BF16); BA4 = Buf(); Bk4 = Buf()
        oTs = S.sbuf("oTs", [128, 4, 64], F32); BoTs = Buf()
        onesb = S.sbuf("onesb", [128, 128], BF16)
        sqs = S.sbuf("sqs", [128, 256], BF16); rss = S.sbuf("rss", [128, 256], F32)
        C.pool.op(nc.gpsimd.memset, writes=[Bc], ap=onesb[:], constant=1.0)
        sl_s0 = [C.slot("s0a"), C.slot("s0b")]; sl_sn = [C.slot("sna"), C.slot("snb")]
        for bb in range(16):
            s = bb % 2
            cs = slice(2048 + 4 * bb, 2048 + 4 * bb + 4)
            S0b = S0b2[s]; BS0b = BS0b2[s]
            if bb == 0:
                C.dma(C.sp, sl_s0[0], S0[0][:], st_hg[0].rearrange("h d e -> d h e"), writes=[BS0[0]])
            if bb + 1 < 16:
                C.dma(C.sp, sl_s0[1 - s], S0[1 - s][:], st_hg[bb + 1].rearrange("h d e -> d h e"), writes=[BS0[1 - s]])
            for kc in range(KC):
                C.pe.op(nc.tensor.matmul, reads=[Bw], writes=[Bps2[s]], mark=(kc == KC - 1), out=ps2[s][0:4, :],
                        lhsT=xnT[:, kc, 2048 + 4 * bb:2048 + 4 * bb + 4], rhs=wib[:, kc, :], start=(kc == 0), stop=(kc == KC - 1))
            C.act.op(nc.scalar.copy, reads=[Bps2[s]], writes=[BvS[s]], out=vS[s][0:4, :], in_=ps2[s][0:4, :])
            C.act.op(nc.scalar.copy, reads=[BS0[s]], writes=[BS0b], out=S0b[:], in_=S0[s][:])
            for h in range(4):
                C.pe.op(nc.tensor.matmul, reads=[Bqk], writes=[BpsA], mark=(h == 3), out=psA[0:4, h, 0:4], lhsT=kdT[:, h, cs],
                        rhs=qdT[:, h, cs], start=True, stop=True)
                C.pe.op(nc.tensor.transpose, reads=[Bqk, Bid], writes=[BpsK], mark=(h == 3), out=psK[0:4, h, :], in_=kdT[:, h, cs],
                        identity=identb[:, :])
            C.dve.op(nc.vector.tensor_tensor, reads=[BpsA, Bc], writes=[BA4], out=Am4[:], in0=psA[0:4, :, 0:4],
                     in1=bass.AP(tri[0:4, 0:4].tensor, tri[0:4, 0:4].offset, [list(tri[0:4, 0:4].ap[0]), [0, 4], [1, 4]]), op=ALU.mult)
            C.act.op(nc.scalar.copy, reads=[BpsK], writes=[Bk4], out=kd4[:], in_=psK[0:4, :, :])
            for h in range(4):
                C.pe.op(nc.tensor.matmul, reads=[Bqk, BS0b], writes=[BpsOo], mark=False, out=psOo[:, h, 0:4], lhsT=S0b[:, h, :],
                        rhs=qdT[:, h, cs], start=True, stop=False)
                C.pe.op(nc.tensor.matmul, reads=[BA4, BvS[s], BS0b], writes=[BpsOo], mark=(h == 3), out=psOo[:, h, 0:4],
                        lhsT=vS[s][0:4, h * 128:(h + 1) * 128], rhs=Am4[0:4, h, :], start=False, stop=True)
                C.pe.op(nc.tensor.matmul, reads=[Bk4, BvS[s]], writes=[BpsD[s]], mark=(h == 3), out=psD[s][:, h, :], lhsT=kd4[0:4, h, :],
                        rhs=vS[s][0:4, h * 128:(h + 1) * 128], start=True, stop=True)
            C.dve.op(nc.vector.tensor_tensor, reads=[BpsD[s], BS0[s]], writes=[BS], out=stmp[:], in0=psD[s][:], in1=S0[s][:], op=ALU.add)
            eg = eGls[:, :, bb]
            C.dve.op(nc.vector.tensor_tensor, reads=[BS, Bqk], writes=[BSn[s]], out=Sn[s][:], in0=stmp[:],
                     in1=bass.AP(eg.tensor, eg.offset, [list(eg.ap[0]), list(eg.ap[1]), [0, 128]]), op=ALU.mult)
            C.dma(C.sp, sl_sn[s], o_shg[bb].rearrange("h d e -> d h e"), Sn[s][:], reads=[BSn[s]])
            C.act.op(nc.scalar.copy, reads=[BpsOo], writes=[BoTs], out=oTs[:, :, 4 * bb:4 * bb + 4], in_=psOo[:, :, 0:4])
        C.act.op(nc.scalar.activation, reads=[BoTs], writes=[Bn], out=sqs[:], in_=oTs[:].rearrange("p a b -> p (a b)"), func=AF.Square)
        C.pe.op(nc.tensor.matmul, reads=[Bn, Bc], writes=[Bps2[0]], out=ps2[0][:, 0:256], lhsT=onesb[:, :], rhs=sqs[:, :], start=True, stop=True)
        C.act.op(nc.scalar.activation, reads=[Bps2[0]], writes=[Bn], out=rss[:], in_=ps2[0][:, 0:256], func=AF.Sqrt, scale=1.0 / 128, bias=EPS)
        C.dve.op(nc.vector.reciprocal, reads=[Bn], writes=[Bn], out=rss[:], in_=rss[:])
        C.dve.op(nc.vector.tensor_tensor, reads=[Bn, BoTs], writes=[BoTs], out=oTs[:].rearrange("p a b -> p (a b)"),
                 in0=oTs[:].rearrange("p a b -> p (a b)"), in1=rss[:], op=ALU.mult)
        for h in range(4):
            C.dve.op(nc.vector.tensor_scalar, reads=[BoTs, Bc], writes=[BoBT], out=oBT[:, h, 2048:NTOK], in0=oTs[:, h, :],
                     scalar1=hgcol[:, h:h + 1], scalar2=None, op0=ALU.mult)
        S.close()

    if "F" in phases:
        S = C.scope()
        y2T = S.sbuf("y2T", [128, KC, NTOK], BF16)
        wba = S.sbuf("wba", [128, 4, D], BF16); wbb = S.sbuf("wbb", [128, 4, D], BF16)
        wo = S.sbuf("wo", [128, KC, D], BF16)
        wch = [S.sbuf("wchF%d" % i, [128, KC, 256], BF16) for i in range(3)]; Bwch = [Buf() for _ in range(3)]
        fgb = S.sbuf("fgb", [128, D], F32)
        sg = [S.sbuf("sg%d" % i, [128, 512], F32) for i in range(2)]; Bsg = [Buf(), Buf()]
        t1 = S.sbuf("t1", [128, 512], F32); Bt1 = Buf()
        xr = [S.sbuf("xr%d" % i, [128, D], F32) for i in range(3)]; Bxr = [Buf() for _ in range(3)]
        hh = [S.sbuf("hh%d" % i, [128, D], F32) for i in range(2)]; Bhh = [Buf(), Buf()]
        junkf = S.sbuf("junkf", [128, D], BF16); Bj = Buf()
        nf = S.sbuf("nf", [128, NT, 4], F32)
        ps = [S.psum("psF%d" % i, [128, 512], F32) for i in range(6)]; Bps = [Buf() for _ in range(6)]
        Bw = Buf(); By = Buf(); Bo = Buf()
        sl_xr = [C.slot("xr%d" % i) for i in range(3)]; sl_y = [C.slot("y%d" % i) for i in range(3)]
        C.dma(C.sp, sl_misc, fgb[:], dram_bcast(final_g, 128), writes=[Bw])
        wi = [0]

        def next_w(c0, n):
            s = wi[0] % 3; wi[0] += 1
            load_w(wch[s], Bwch[s], w_in[:, c0:c0 + n], n)
            return wch[s], Bwch[s]
        pr = [0]
        for (c0, oT) in ((C_ZA, oAT), (C_ZB, oBT)):
            for piece in range(2):
                wt, Bwt = next_w(c0 + piece * 256, 256)
                for jj in range(2):
                    j = piece * 2 + jj
                    for st in range(5):
                        n = 512 if st < 4 else 64
                        cs = slice(st * 512, st * 512 + n)
                        b = pr[0] % 6; pr[0] += 1
                        for kc in range(KC):
                            C.pe.op(nc.tensor.matmul, reads=[Bwt], writes=[Bps[b]], mark=(kc == KC - 1), out=ps[b][:, 0:n],
                                    lhsT=wt[:, kc, jj * 128:(jj + 1) * 128], rhs=xnT[:, kc, cs], start=(kc == 0), stop=(kc == KC - 1))
                        s = b % 2
                        C.act.op(nc.scalar.activation, reads=[Bps[b]], writes=[Bsg[s]], out=sg[s][:, 0:n], in_=ps[b][:, 0:n], func=AF.Silu)
                        C.dve.op(nc.vector.tensor_tensor, reads=[Bsg[s], Bo], writes=[Bo], out=oT[:, j, cs], in0=oT[:, j, cs], in1=sg[s][:, 0:n],
                                 op=ALU.mult)
        load_w(wba, Bw, w_ba, D, kcn=4)
        load_w(wbb, Bw, w_bb, D, kcn=4)
        load_w(wo, Bw, w_out, D)
        for piece in range(4):
            wtA, BwA = next_w(C_MA + piece * 256, 256)
            wtB, BwB = next_w(C_MB + piece * 256, 256)
            for jj in range(2):
                cc = piece * 2 + jj
                for st in range(5):
                    n = 512 if st < 4 else 64
                    cs = slice(st * 512, st * 512 + n)
                    bA, bB, bMA, bMB = [(pr[0] + i) % 6 for i in range(4)]; pr[0] += 4
                    for (b, wsrc, oT) in ((bA, wba, oAT), (bB, wbb, oBT)):
                        for k in range(4):
                            C.pe.op(nc.tensor.matmul, reads=[Bw, Bo], writes=[Bps[b]], mark=(k == 3), out=ps[b][:, 0:n],
                                    lhsT=wsrc[:, k, cc * 128:(cc + 1) * 128], rhs=oT[:, k, cs], start=(k == 0), stop=(k == 3))
                    for (b, wt, Bwt) in ((bMA, wtA, BwA), (bMB, wtB, BwB)):
                        for kc in range(KC):
                            C.pe.op(nc.tensor.matmul, reads=[Bwt], writes=[Bps[b]], mark=(kc == KC - 1), out=ps[b][:, 0:n],
                                    lhsT=wt[:, kc, jj * 128:(jj + 1) * 128], rhs=xnT[:, kc, cs], start=(kc == 0), stop=(kc == KC - 1))
                    for (i, bm) in enumerate((bMA, bMB)):
                        C.act.op(nc.scalar.activation, reads=[Bps[bm]], writes=[Bsg[i]], out=sg[i][:, 0:n], in_=ps[bm][:, 0:n], func=AF.Sigmoid)
                    C.dve.op(nc.vector.tensor_tensor, reads=[Bsg[0], Bps[bA]], writes=[Bt1], out=t1[:, 0:n], in0=sg[0][:, 0:n], in1=ps[bA][:, 0:n],
                             op=ALU.mult)
                    C.dve.op(nc.vector.tensor_tensor, reads=[Bsg[1], Bps[bB]], writes=[Bsg[1]], out=sg[1][:, 0:n], in0=sg[1][:, 0:n],
                             in1=ps[bB][:, 0:n], op=ALU.mult)
                    C.dve.op(nc.vector.tensor_tensor, reads=[Bsg[1], Bt1], writes=[By], out=y2T[:, cc, cs], in0=sg[1][:, 0:n], in1=t1[:, 0:n],
                             op=ALU.add)
        def xload(t):
            C.dma(C.sp, sl_xr[t % 3], xr[t % 3][0:rows(t), :], x_all[t * 128:t * 128 + rows(t), :], writes=[Bxr[t % 3]])
        xload(0); xload(1)
        for t in range(NT):
            r = rows(t); s = t % 2
            if t + 2 < NT:
                xload(t + 2)
            b0 = pr[0] % 6; b1 = (pr[0] + 1) % 6; pr[0] += 2
            for (b, half) in ((b0, 0), (b1, 1)):
                for c2 in range(KC):
                    C.pe.op(nc.tensor.matmul, reads=[By, Bw], writes=[Bps[b]], mark=(c2 == KC - 1), out=ps[b][0:r, :],
                            lhsT=y2T[:, c2, t * 128:t * 128 + r], rhs=wo[:, c2, half * 512:(half + 1) * 512], start=(c2 == 0), stop=(c2 == KC - 1))
                C.dve.op(nc.vector.tensor_tensor, reads=[Bps[b], Bxr[t % 3]], writes=[Bhh[s]], out=hh[s][0:r, half * 512:(half + 1) * 512],
                         in0=ps[b][0:r, :], in1=xr[t % 3][0:r, half * 512:(half + 1) * 512], op=ALU.add)
            C.act.op(nc.scalar.activation, reads=[Bhh[s]], writes=[Bj], out=junkf[0:r, :], in_=hh[s][0:r, :], func=AF.Square,
                     accum_out=nf[0:r, t, 0:1])
            C.act.op(nc.scalar.activation, reads=[Bj], writes=[Bj], out=nf[0:r, t, 1:2], in_=nf[0:r, t, 0:1], func=AF.Sqrt, scale=1.0 / D, bias=EPS)
            C.dve.op(nc.vector.reciprocal, reads=[Bj], writes=[Bj], out=nf[0:r, t, 2:3], in_=nf[0:r, t, 1:2])
            C.dve.op(nc.vector.scalar_tensor_tensor, reads=[Bhh[s], Bj, Bw], writes=[Bxr[t % 3]], out=xr[t % 3][0:r, :], in0=hh[s][0:r, :],
                     scalar=nf[0:r, t, 2:3], in1=fgb[0:r, :], op0=ALU.mult, op1=ALU.mult)
            C.dma(C.sp, sl_y[t % 3], o_y[t * 128:t * 128 + r, :], xr[t % 3][0:r, :], reads=[Bxr[t % 3]])
        S.close()

    C.finish()
    return nc


def _t5_bucket(rel):
    n = np.maximum(rel, 0)
    nf = np.maximum(n, 1).astype(np.float32)
    large = 16 + (np.log(nf / np.float32(16)) / np.float32(np.log(np.float32(8.0))) * np.float32(16)).astype(np.int32)
    large = np.minimum(large, 31)
    return np.where(n < 16, n, large)


def make_consts():
    c = {}
    ohx = np.zeros((33, TW), np.float32)
    for m in range(TW):
        rel = m - TOFF
        if rel >= 0:
            ohx[int(_t5_bucket(np.array(rel))), m] += 1.0
            ohx[31, m] -= 1.0
        else:
            ohx[32, m] = NEG
    c["c_ohx"] = ohx
    cs = np.arange(127) * 16; ce = cs + 32
    bs = np.arange(32) * 64; be = bs + 64
    ov = np.clip(np.minimum(ce[:, None], be[None, :]) - np.maximum(cs[:, None], bs[None, :]), 0, None) / 16
    c["c_cover"] = ov.astype(np.float32)
    selc = np.zeros((128, 2, 16, 32), np.float32)
    for qt in range(16):
        t = qt * 128 + np.arange(128)[:, None]
        j = np.arange(32)[None, :]
        valid = (j * 64) <= t
        cur = t // 64
        forced = (j == 0) | (j == cur) | (j == cur - 1)
        selc[:, 0, qt, :] = (valid & ~forced).astype(np.float32)
        selc[:, 1, qt, :] = np.where(valid, np.where(forced, 1e9, 0.0), -1.0)
    c["c_selc"] = selc
    p = np.arange(128)[:, None] % 64
    c["c_tri"] = (p <= np.arange(64)[None, :]).astype(np.float32)
    c["c_pm8"] = (np.arange(128) % 8).astype(np.float32).reshape(128, 1)
    return c


_NC_CACHE = {}
PHASES = ("A", "B", "T", "C", "D", "E", "F")


def kernel(**inp):
    f = lambda k: np.ascontiguousarray(np.asarray(inp[k]))
    xp = f("x_prompt"); xsm = f("x_sample")
    consts = make_consts()
    if "nc" not in _NC_CACHE:
        _NC_CACHE["nc"] = build(PHASES)
    nc = _NC_CACHE["nc"]
    shared = {
        "w_in": f("w_in")[0], "norm_g": f("norm_g"), "final_g": f("final_g").reshape(1, D), "rel_bias": f("rel_bias"),
        "pos_k": f("cmp_pos_k")[0], "pos_v": f("cmp_pos_v")[0], "w1_k": f("cmp_w1_k")[0], "w1_v": f("cmp_w1_v")[0],
        "w2_k": f("cmp_w2_k")[0], "w2_v": f("cmp_w2_v")[0], "hg_lower": f("hg_lower"), "hg_norm_g": f("hg_norm_g"),
        "w_ba": f("w_branch_a")[0], "w_bb": f("w_branch_b")[0], "w_out": f("w_out")[0],
        "c_ck": f("cache_cmp_k").reshape(2560 * 8, 2048), "c_cv": f("cache_cmp_v").reshape(2560 * 8, 2048),
        "c_sk": f("cache_sel_k").reshape(2560 * 8, 2048), "c_sv": f("cache_sel_v").reshape(2560 * 8, 2048),
    }
    shared.update(consts)
    swk = f("state_win_k")[0].reshape(128, 512, 128); swv = f("state_win_v")[0].reshape(128, 512, 128)
    shg = f("state_hgrn")[0]
    pt = f("page_table").astype(np.int32)
    in_maps = []
    for c in range(8):
        m = dict(shared)
        m["x_all"] = np.concatenate([xp[c], xsm[16 * c:16 * c + 16].reshape(64, D)], axis=0)
        m["st_wk"] = swk[16 * c:16 * c + 16]; m["st_wv"] = swv[16 * c:16 * c + 16]
        m["st_hg"] = shg[16 * c:16 * c + 16]
        m["ptab"] = pt[16 * c:16 * c + 16].reshape(1, 256)
        in_maps.append(m)
    res = run_bass_kernel_spmd(nc, in_maps, core_ids=list(range(8)))
    R = res.results
    y_p = np.stack([R[c]["o_y"][0:2048] for c in range(8)], 0)
    y_s = np.concatenate([R[c]["o_y"][2048:].reshape(16, 4, D) for c in range(8)], 0)
    outs = [y_p, y_s]
    for j in range(4):
        outs.append(np.stack([R[c]["o_pkv"][j].reshape(2048, 2, 64) for c in range(8)], 0)[None])
    for j in range(2):
        outs.append(np.stack([R[c]["o_pwin"][j].reshape(512, 2, 64) for c in range(8)], 0)[None])
    outs.append(np.stack([R[c]["o_phg"] for c in range(8)], 0)[None])
    for j in range(4):
        outs.append(np.concatenate([R[c]["o_skv"][j].reshape(16, 4, 2, 64) for c in range(8)], 0)[None])
    for j in range(2):
        outs.append(np.concatenate([R[c]["o_swin"][j].reshape(16, 512, 2, 64) for c in range(8)], 0)[None])
    outs.append(np.concatenate([R[c]["o_shg"] for c in range(8)], 0)[None])
    return tuple(np.ascontiguousarray(o.astype(np.float32)) for o in outs)
```

```python
import numpy as np
import contextlib
import concourse.bass as bass
import concourse.mybir as mybir
from concourse.bass_utils import run_bass_kernel_spmd

F32 = mybir.dt.float32; BF16 = mybir.dt.bfloat16; I32 = mybir.dt.int32
AF = mybir.ActivationFunctionType
ALU = mybir.AluOpType
AX = mybir.AxisListType

SAME_ENGINE_SYNC = True
NEG = -30000.0
CAST_SPLIT = True
NTOK = 2112
NT = 17
D = 1024
KC = 8
EPS = 1e-6
C_Q = 0; C_KV = 512; C_G = 1280; C_ZA = 1304; C_QB = 1816; C_FB = 2328; C_IB = 2840; C_ZB = 3352; C_MA = 3864; C_MB = 4888
NCOL = 5912
TW = 768
TOFF = 160


class Tok:
    __slots__ = ("sem", "val", "eng", "slot")

    def __init__(self, sem, val, eng=None, slot=None):
        self.sem = sem; self.val = val; self.eng = eng; self.slot = slot


class Buf:
    __slots__ = ("w", "r", "name")

    def __init__(self, name=""):
        self.w = None; self.r = []; self.name = name


class Eng:
    def __init__(self, ctx, name, eng, is_pe=False):
        self.ctx = ctx; self.name = name; self.e = eng
        self.sem = ctx.new_sem("e_" + name)
        self.cnt = 0
        self.seen = {}
        self.is_pe = is_pe
        self.ninstr = 0

    def wait(self, tok):
        if tok is None:
            return
        if tok.eng is self:
            if self.is_pe or not SAME_ENGINE_SYNC:
                return
        key = id(tok.sem)
        val = tok.val
        if tok.slot is not None:
            val = tok.slot.cnt
            tok.slot.waited_max = max(tok.slot.waited_max, val)
        if self.seen.get(key, 0) >= val:
            return
        self.e.wait_ge(tok.sem, val)
        self.seen[key] = val

    def wait_many(self, toks):
        best = {}
        for t in toks:
            if t is None:
                continue
            k = id(t.sem)
            if k not in best or best[k].val < t.val:
                best[k] = t
        for t in best.values():
            self.wait(t)

    def deps(self, reads, writes):
        toks = [b.w for b in reads]
        for b in writes:
            toks.append(b.w)
            toks.extend(b.r)
        self.wait_many(toks)

    def op(self, fn, reads=(), writes=(), mark=True, **kw):
        self.deps(reads, writes)
        ins = fn(**kw)
        self.ninstr += 1
        if not mark:
            return None
        self.cnt += 1
        ins.then_inc(self.sem, 1)
        tok = Tok(self.sem, self.cnt, self)
        for b in reads:
            b.r.append(tok)
            if len(b.r) > 16:
                b.r = b.r[-16:]
        for b in writes:
            b.w = tok; b.r = []
        return tok


class DmaSlot:
    def __init__(self, ctx, name):
        self.sem = ctx.new_sem("d_" + name); self.cnt = 0; self.waited_max = 0

    def pre_issue(self, q):
        if self.waited_max:
            q.wait(Tok(self.sem, self.waited_max, None, self))

    def issued(self, ins):
        self.cnt += 16
        ins.then_inc(self.sem, 16)
        return Tok(self.sem, self.cnt, None, self)


class Scope:
    def __init__(self, C):
        self.C = C; self.es = contextlib.ExitStack()

    def sbuf(self, name, shape, dt):
        return self.es.enter_context(self.C.nc.sbuf_tensor(name, list(shape), dt))

    def psum(self, name, shape, dt=F32):
        return self.es.enter_context(self.C.nc.psum_tensor(name, list(shape), dt))

    def close(self):
        self.C.barrier()
        self.es.close()


class Ctx:
    def __init__(self, nc):
        self.nc = nc
        self.es = contextlib.ExitStack()
        self.pe = Eng(self, "pe", nc.tensor, is_pe=True)
        self.act = Eng(self, "act", nc.scalar)
        self.dve = Eng(self, "dve", nc.vector)
        self.pool = Eng(self, "pool", nc.gpsimd)
        self.sp = Eng(self, "sp", nc.sync)
        self.engs = (self.pe, self.act, self.dve, self.pool, self.sp)
        self.slots = []
        self.ndma = 0

    def new_sem(self, name):
        return self.es.enter_context(self.nc.semaphore(name))

    def sbuf(self, name, shape, dt):
        return self.es.enter_context(self.nc.sbuf_tensor(name, list(shape), dt))

    def scope(self):
        return Scope(self)

    def slot(self, name):
        s = DmaSlot(self, name); self.slots.append(s); return s

    def dma(self, q, slot, out, in_, reads=(), writes=(), **kw):
        q.deps(reads, writes)
        slot.pre_issue(q)
        ins = q.e.dma_start(out=out, in_=in_, **kw)
        self.ndma += 1
        tok = slot.issued(ins)
        for b in reads:
            b.r.append(tok)
            if len(b.r) > 16:
                b.r = b.r[-16:]
        for b in writes:
            b.w = tok; b.r = []
        return tok

    def barrier(self):
        for e in self.engs:
            for f in self.engs:
                if f is not e and f.cnt and f is not self.sp:
                    e.wait(Tok(f.sem, f.cnt, f))
            for s in self.slots:
                if s.cnt:
                    e.wait(Tok(s.sem, s.cnt, None, s))

    def finish(self):
        self.barrier()
        self.es.close()


def sub(ap, p0, p1):
    return ap[p0:p1]


def rows(t):
    return 128 if t < 16 else 64


def dram_bcast(ap2d, nparts):
    n = ap2d.shape[-1]
    return bass.AP(ap2d.tensor, ap2d.offset, [[0, nparts], [1, n]])


def build(phases=("A", "B", "T", "C", "D", "E", "F")):
    nc = bass.Bass("TRN2", target_bir_lowering=False)
    C = Ctx(nc)

    def din(name, shape, dt=F32):
        return nc.dram_tensor(name, list(shape), dt, kind="ExternalInput").ap()

    def dout(name, shape):
        return nc.dram_tensor(name, list(shape), F32, kind="ExternalOutput").ap()

    def dscr(name, shape, dt=F32):
        return nc.dram_tensor(name, list(shape), dt, kind="Internal").ap()

    x_all = din("x_all", [NTOK, D])
    w_in = din("w_in", [D, NCOL])
    norm_g = din("norm_g", [1, D])
    final_g = din("final_g", [1, D])
    rel_bias = din("rel_bias", [32, 8])
    pos_kv = [din("pos_k", [32, 64]), din("pos_v", [32, 64])]
    w1_kv = [din("w1_k", [2048, 64]), din("w1_v", [2048, 64])]
    w2_kv = [din("w2_k", [64, 64]), din("w2_v", [64, 64])]
    hg_lower = din("hg_lower", [2, 512])
    hg_norm_g = din("hg_norm_g", [1, 512])
    w_ba = din("w_ba", [512, D])
    w_bb = din("w_bb", [512, D])
    w_out = din("w_out", [D, D])
    caches = [din(n, [2560 * 8, 2048]) for n in ("c_ck", "c_cv", "c_sk", "c_sv")]
    st_win = [din("st_wk", [16, 512, 128]), din("st_wv", [16, 512, 128])]
    st_hg = din("st_hg", [16, 4, 128, 128])
    ptab = din("ptab", [1, 256], I32)
    c_ohx = din("c_ohx", [33, TW])
    c_cover = din("c_cover", [127, 32])
    c_selc = din("c_selc", [128, 2, 16, 32])
    c_tri = din("c_tri", [128, 64])
    c_pm8 = din("c_pm8", [128, 1])

    o_y = dout("o_y", [NTOK, D])
    o_pkv = dout("o_pkv", [6, 2048, 128])
    o_pwin = dout("o_pwin", [2, 512, 128])
    o_phg = dout("o_phg", [4, 128, 128])
    o_skv = dout("o_skv", [6, 64, 128])
    o_swin = dout("o_swin", [2, 16, 512, 128])
    o_shg = dout("o_shg", [16, 4, 128, 128])

    xnT = C.sbuf("xnT", [128, KC, NTOK], BF16)
    identb = C.sbuf("identb", [128, 128], BF16)
    identf = C.sbuf("identf", [128, 128], F32)
    wst = [C.sbuf("wst%d" % i, [128, KC, 256], F32) for i in range(2)]
    Bwst = [Buf() for _ in range(2)]
    sl_w = [C.slot("w%d" % i) for i in range(2)]
    sl_misc = C.slot("misc")
    sl_st = [C.slot("st%d" % i) for i in range(2)]
    wring = [0]

    Bid = Buf()
    C.pool.op(nc.gpsimd.memset, writes=[Bid], ap=identf[:], constant=1.0)
    C.pool.op(nc.gpsimd.affine_select, reads=[Bid], writes=[Bid], out=identf[:], in_=identf[:],
              pattern=[[-1, 128]], base=0, channel_multiplier=1, compare_op=ALU.is_equal, fill=0.0)
    C.dve.op(nc.vector.tensor_copy, reads=[Bid], writes=[Bid], out=identb[:], in_=identf[:])

    def load_w(dst, Bdst, src2d, ncols, kcn=KC, cast_eng=None):
        for c0 in range(0, ncols, 256):
            n = min(256, ncols - c0)
            s = wring[0] % 2; wring[0] += 1
            C.dma(C.sp, sl_w[s], wst[s][:, 0:kcn, 0:n],
                  src2d[:, c0:c0 + n].rearrange("(k p) c -> p k c", p=128), writes=[Bwst[s]])
            if cast_eng is None and s == 1 and CAST_SPLIT:
                C.act.op(nc.scalar.copy, reads=[Bwst[s]], writes=[Bdst], out=dst[:, 0:kcn, c0:c0 + n], in_=wst[s][:, 0:kcn, 0:n])
            else:
                e = cast_eng or C.pool
                e.op(e.e.tensor_copy, reads=[Bwst[s]], writes=[Bdst], out=dst[:, 0:kcn, c0:c0 + n], in_=wst[s][:, 0:kcn, 0:n])

    oAT = C.sbuf("oAT", [128, 4, NTOK], BF16)
    oBT = C.sbuf("oBT", [128, 4, NTOK], BF16)
    SAB2 = C.scope()
    BT = [[SAB2.sbuf("BT%d%d" % (g, d), [128, 4, 128], BF16) for d in range(2)] for g in range(2)]
    Mcmp = [SAB2.sbuf("Mcmp%d" % g, [16, 4, 128], BF16) for g in range(2)]
    Wm4 = SAB2.sbuf("Wm4", [128, 4, 128], BF16)
    Iw = SAB2.sbuf("Iw", [16, 144], BF16)
    W1bd = [SAB2.sbuf("W1bd%d" % j, [128, 32, 128], BF16) for j in range(2)]
    cvec = SAB2.sbuf("cvec", [128, 2], F32)
    W2k = SAB2.sbuf("W2k", [128, 2, 64], BF16)
    W2v = SAB2.sbuf("W2v", [128, 128], BF16)
    Kcaug = [SAB2.sbuf("Kcaug%d" % g, [97, 128], BF16) for g in range(2)]
    Vc = SAB2.sbuf("Vc", [128, 2, 97], BF16)
    gat = SAB2.sbuf("gat", [128, NT, 24], F32)
    rbcol = SAB2.sbuf("rbcol", [128, 8], F32)
    qS = [SAB2.sbuf("qS%d" % g, [97, 4, 64], BF16) for g in range(2)]
    KsS = [SAB2.sbuf("KsS%d" % g, [97, 64], BF16) for g in range(2)]
    KwS = [SAB2.sbuf("KwS%d" % g, [97, 64], BF16) for g in range(2)]
    gS = SAB2.sbuf("gS", [16, 16, 3, 2], F32)
    _w1flat = wst[1][:, :, :].rearrange("p a b -> p (a b)")
    bspst = _w1flat[:, 0:512].rearrange("p (a b c) -> p a b c", a=16, b=8)
    mcst = _w1flat[:, 512:544].rearrange("p (a b) -> p a b", a=8)
    idxraw = SAB2.sbuf("idxraw", [128, 16], I32); pm8f = SAB2.sbuf("pm8f", [128, 1], F32)
    BDS = Buf()
    sl_ds = C.slot("dset")
    SAB1 = C.scope()
    qaug = [SAB1.sbuf("qaug%d" % g, [97, 4 * NTOK], BF16) for g in range(2)]
    Ksaug = [SAB1.sbuf("Ksaug%d" % g, [97, NTOK], BF16) for g in range(2)]
    Kwaug = [SAB1.sbuf("Kwaug%d" % g, [97, NTOK], BF16) for g in range(2)]
    kcvT = [SAB1.sbuf("kcT", [128, 2048], BF16), SAB1.sbuf("vcT", [128, 2048], BF16)]
    Vs = SAB1.sbuf("Vs", [128, NT, 2, 65], BF16)
    Vw = SAB1.sbuf("Vw", [128, NT, 2, 65], BF16)
    selc = SAB1.sbuf("selc", [128, 2, 16, 32], F32)
    Baug = Buf()

    if "A" in phases:
        S = C.scope()
        gbc = S.sbuf("gbc", [128, D], F32); Bgbc = Buf()
        xs = [S.sbuf("xs%d" % i, [128, D], F32) for i in range(2)]; Bxs = [Buf(), Buf()]
        xnb = [S.sbuf("xnb%d" % i, [128, D], BF16) for i in range(2)]; Bxnb = [Buf(), Buf()]
        junk = S.sbuf("junk", [128, D], BF16); Bjunk = Buf()
        ss = S.sbuf("ss", [128, NT], F32); sd = S.sbuf("sd", [128, NT], F32); rstd = S.sbuf("rstd", [128, NT], F32)
        pT = [S.psum("pT%d" % i, [128, KC, 128], BF16) for i in range(2)]; BpT = [Buf(), Buf()]
        sl_x = [C.slot("x0"), C.slot("x1")]
        C.dma(C.sp, sl_misc, gbc[:], dram_bcast(norm_g, 128), writes=[Bgbc])
        for g in range(2):
            C.pool.op(nc.gpsimd.memset, writes=[Baug], ap=qaug[g][64:96, :], constant=0.0)
            C.pool.op(nc.gpsimd.memset, writes=[Baug], ap=qaug[g][96:97, :], constant=1.0)
            C.pool.op(nc.gpsimd.memset, writes=[Baug], ap=Kwaug[g][64:96, :], constant=0.0)
            C.pool.op(nc.gpsimd.memset, writes=[Baug], ap=Kwaug[g][96:97, :], constant=1.0)
            C.pool.op(nc.gpsimd.memset, writes=[Baug], ap=Ksaug[g][96:97, :], constant=1.0)
            C.pool.op(nc.gpsimd.memset, writes=[Baug], ap=Ksaug[g][64:96, 0:2048], constant=1.0)
            C.pool.op(nc.gpsimd.memset, writes=[Baug], ap=Ksaug[g][64:96, 2048:NTOK], constant=0.0)
            C.pool.op(nc.gpsimd.affine_select, writes=[Baug], out=Ksaug[g][64:96, 0:2048], in_=Ksaug[g][64:96, 0:2048],
                      pattern=[[1, 2048]], base=0, channel_multiplier=-64, compare_op=ALU.is_ge, fill=0.0)
            C.pool.op(nc.gpsimd.affine_select, writes=[Baug], out=Ksaug[g][64:96, 0:2048], in_=Ksaug[g][64:96, 0:2048],
                      pattern=[[-1, 2048]], base=63, channel_multiplier=64, compare_op=ALU.is_ge, fill=0.0)
        C.pool.op(nc.gpsimd.memset, writes=[Baug], ap=Vs[:].rearrange("p a b c -> p (a b c)"), constant=1.0)
        C.pool.op(nc.gpsimd.memset, writes=[Baug], ap=Vw[:].rearrange("p a b c -> p (a b c)"), constant=1.0)
        C.dma(C.sp, sl_misc, rbcol[96:97, :], rel_bias[31:32, :], writes=[Baug])
        for h in range(8):
            g, r = divmod(h, 4)
            C.dve.op(nc.vector.tensor_scalar, reads=[Baug], writes=[Baug], out=qaug[g][96:97, r * NTOK:(r + 1) * NTOK],
                     in0=qaug[g][96:97, r * NTOK:(r + 1) * NTOK], scalar1=rbcol[96:97, h:h + 1], scalar2=None, op0=ALU.mult)
        Bt = [Buf() for _ in range(NT)]
        for t in range(NT):
            r = rows(t); s = t % 2
            C.dma(C.sp, sl_x[s], xs[s][0:r, :], x_all[t * 128:t * 128 + r, :], writes=[Bxs[s]])
            C.act.op(nc.scalar.activation, reads=[Bxs[s]], writes=[Bjunk, Bt[t]], out=junk[0:r, :], in_=xs[s][0:r, :],
                     func=AF.Square, accum_out=ss[0:r, t:t + 1])
            C.act.op(nc.scalar.activation, reads=[Bt[t]], writes=[Bt[t]], out=sd[0:r, t:t + 1], in_=ss[0:r, t:t + 1],
                     func=AF.Sqrt, scale=1.0 / D, bias=EPS)
            C.dve.op(nc.vector.reciprocal, reads=[Bt[t]], writes=[Bt[t]], out=rstd[0:r, t:t + 1], in_=sd[0:r, t:t + 1])
            C.dve.op(nc.vector.scalar_tensor_tensor, reads=[Bxs[s], Bt[t], Bgbc], writes=[Bxnb[s]], out=xnb[s][0:r, :],
                     in0=xs[s][0:r, :], scalar=rstd[0:r, t:t + 1], in1=gbc[0:r, :], op0=ALU.mult, op1=ALU.mult)
            for kc in range(KC):
                C.pe.op(nc.tensor.transpose, reads=[Bxnb[s], Bid], writes=[BpT[s]], mark=(kc == KC - 1),
                        out=pT[s][:, kc, 0:r], in_=xnb[s][0:r, kc * 128:(kc + 1) * 128], identity=identb[0:r, 0:r])
            e = C.act if t % 2 == 0 else C.dve
            fn = nc.scalar.copy if t % 2 == 0 else nc.vector.tensor_copy
            e.op(fn, reads=[BpT[s]], writes=[Bt[t]], out=xnT[:, :, t * 128:t * 128 + r], in_=pT[s][:, :, 0:r])
        S.close()

    if "B" in phases:
        S = C.scope()
        wkv = S.sbuf("wkv", [128, KC, 792], BF16); Bwkv = Buf()
        wch = [S.sbuf("wch%d" % i, [128, KC, 256], BF16) for i in range(2)]; Bwch = [Buf(), Buf()]
        stg = [S.sbuf("stg%d" % i, [128, 768], F32) for i in range(2)]; Bstg = [Buf(), Buf()]
        ps = [S.psum("psB%d" % i, [128, 512], F32) for i in range(4)]; Bps = [Buf() for _ in range(4)]
        pring = [0]
        load_w(wkv, Bwkv, w_in[:, C_KV:C_KV + 792], 792)
        Bg = Buf()
        for t in range(NT):
            r = rows(t); s = t % 2
            for gi, (c0, n) in enumerate(((0, 256), (256, 256), (512, 256), (768, 24))):
                b = pring[0] % 4; pring[0] += 1
                for kc in range(KC):
                    C.pe.op(nc.tensor.matmul, reads=[Bwkv], writes=[Bps[b]], mark=(kc == KC - 1), out=ps[b][0:r, 0:n],
                            lhsT=xnT[:, kc, t * 128:t * 128 + r], rhs=wkv[:, kc, c0:c0 + n], start=(kc == 0), stop=(kc == KC - 1))
                if gi < 3:
                    if gi % 2 == 0:
                        C.act.op(nc.scalar.copy, reads=[Bps[b]], writes=[Bstg[s]], out=stg[s][0:r, c0:c0 + n], in_=ps[b][0:r, 0:n])
                    else:
                        C.dve.op(nc.vector.tensor_copy, reads=[Bps[b]], writes=[Bstg[s]], out=stg[s][0:r, c0:c0 + n], in_=ps[b][0:r, 0:n])
                else:
                    C.act.op(nc.scalar.activation, reads=[Bps[b]], writes=[Bg], out=gat[0:r, t, :], in_=ps[b][0:r, 0:24], func=AF.Sigmoid)
            C.pool.op(nc.gpsimd.tensor_copy, reads=[Bstg[s]], writes=[Baug], out=Vs[0:r, t, :, 0:64],
                      in_=stg[s][0:r, 384:512].rearrange("p (g d) -> p g d", g=2))
            C.pool.op(nc.gpsimd.tensor_copy, reads=[Bstg[s]], writes=[Baug], out=Vw[0:r, t, :, 0:64],
                      in_=stg[s][0:r, 640:768].rearrange("p (g d) -> p g d", g=2))
            if t < 16:
                C.dma(C.sp, sl_st[s], o_pkv[:, t * 128:(t + 1) * 128, :].rearrange("j p c -> p j c"),
                      stg[s][:, :].rearrange("p (j c) -> p j c", j=6), reads=[Bstg[s]])
                if t >= 12:
                    C.dma(C.sp, sl_st[s], o_pwin[:, (t - 12) * 128:(t - 11) * 128, :].rearrange("j p c -> p j c"),
                          stg[s][:, 512:768].rearrange("p (j c) -> p j c", j=2), reads=[Bstg[s]])
            else:
                C.dma(C.sp, sl_st[s], o_skv[:, :, :].rearrange("j p c -> p j c"),
                      stg[s][0:64, :].rearrange("p (j c) -> p j c", j=6), reads=[Bstg[s]])
                for j in range(2):
                    for bb in range(16):
                        C.dma(C.sp, sl_st[s], o_swin[j, bb, 508:512, :], stg[s][4 * bb:4 * bb + 4, 512 + 128 * j:640 + 128 * j], reads=[Bstg[s]])
        def featproj(wt, Bw, M, evac):
            for st in range(5):
                n = 512 if st < 4 else 64
                b = pring[0] % 4; pring[0] += 1
                for kc in range(KC):
                    C.pe.op(nc.tensor.matmul, reads=[Bw], writes=[Bps[b]], mark=(kc == KC - 1), out=ps[b][0:M, 0:n],
                            lhsT=wt[:, kc, :], rhs=xnT[:, kc, st * 512:st * 512 + n], start=(kc == 0), stop=(kc == KC - 1))
                evac(st, n, ps[b], Bps[b])
        ev = [0]

        def evac_to(dst_fn, scale=None, Bd=None):
            Baug = Bd if Bd is not None else Buf()
            def f(st, n, p, Bp):
                dst = dst_fn(st, n)
                if dst is None:
                    return
                M = dst.shape[0]
                ev[0] += 1
                if ev[0] % 2 == 0:
                    C.act.op(nc.scalar.activation, reads=[Bp], writes=[Baug], out=dst, in_=p[0:M, 0:n], func=AF.Copy,
                             scale=(scale if scale is not None else 1.0))
                else:
                    C.dve.op(nc.vector.tensor_scalar, reads=[Bp], writes=[Baug], out=dst, in0=p[0:M, 0:n],
                             scalar1=(scale if scale is not None else 1.0), scalar2=None, op0=ALU.mult)
            return f
        wi = [0]

        def next_w(c0, n):
            s = wi[0] % 2; wi[0] += 1
            load_w(wch[s], Bwch[s], w_in[:, c0:c0 + n], n)
            return wch[s], Bwch[s]
        for piece in range(2):
            wt, Bw = next_w(C_Q + piece * 256, 256)
            for hh in range(4):
                h = piece * 4 + hh; g, r = divmod(h, 4)
                featproj(wt[:, :, hh * 64:(hh + 1) * 64], Bw, 64,
                         evac_to(lambda st, n, g=g, r=r: qaug[g][0:64, r * NTOK + st * 512:r * NTOK + st * 512 + n], scale=0.125))
        wt, Bw = next_w(C_KV, 256)
        for j in range(2):
            featproj(wt[:, :, j * 128:(j + 1) * 128], Bw, 128,
                     evac_to(lambda st, n, j=j: (kcvT[j][:, st * 512:st * 512 + n] if st < 4 else None)))
        wt, Bw = next_w(C_KV + 256, 128)
        for g in range(2):
            featproj(wt[:, :, g * 64:(g + 1) * 64], Bw, 64, evac_to(lambda st, n, g=g: Ksaug[g][0:64, st * 512:st * 512 + n]))
        wt, Bw = next_w(C_KV + 512, 128)
        for g in range(2):
            featproj(wt[:, :, g * 64:(g + 1) * 64], Bw, 64, evac_to(lambda st, n, g=g: Kwaug[g][0:64, st * 512:st * 512 + n]))
        S.close()

    G_scr = dscr("G_scr", [8, 128, TW])
    sl_t = C.slot("tbl")

    def toep(h, off, pstep, nparts, n):
        return bass.AP(G_scr.tensor, h * 128 * TW + off, [[pstep, nparts], [1, n]])

    if "T" in phases:
        S = C.scope()
        rbx = S.sbuf("rbx", [33, 8], F32); ohx = S.sbuf("ohx", [33, TW], F32); lh = S.sbuf("lh", [33, 8, 128], F32)
        gb = [S.sbuf("gb%d" % i, [128, TW], F32) for i in range(2)]; Bgb = [Buf(), Buf()]
        tz = wst[1][:, :, :].rearrange("p a (b c) -> p a b c", b=2); tzc = S.sbuf("tzc", [16, 8, 128], F32)
        w1s = wst[0][:, :, :].rearrange("p a (b c) -> p (a b) c", b=4); posT = S.sbuf("posT", [128, 32], F32); posTb = S.sbuf("posTb", [128, 32], BF16)
        w2s = S.sbuf("w2s", [128, 2, 64], F32); cvs = S.sbuf("cvs", [128, 32], F32)
        psT = [S.psum("psT%d" % i, [128, 512], F32) for i in range(2)]; BpsT = [Buf(), Buf()]
        Bt_ = Buf(); Bw = Buf()
        for g in range(2):
            C.pool.op(nc.gpsimd.memset, writes=[Bw], ap=Kcaug[g][64:96, :], constant=0.0)
            C.pool.op(nc.gpsimd.memset, writes=[Bw], ap=Kcaug[g][96:97, :], constant=1.0)
        C.pool.op(nc.gpsimd.memset, writes=[Bw], ap=Vc[:].rearrange("p a b -> p (a b)"), constant=1.0)
        C.pool.op(nc.gpsimd.memset, writes=[Bw], ap=W2k[:].rearrange("p a b -> p (a b)"), constant=0.0)
        C.pool.op(nc.gpsimd.memset, writes=[Bw], ap=W2v[:], constant=0.0)
        Bcs = Buf()
        cov = S.sbuf("cov", [128, 32], F32)
        C.dma(C.sp, sl_t, cov[0:127, :], c_cover, writes=[Bcs])
        for g in range(2):
            C.dve.op(nc.vector.tensor_copy, reads=[Bcs], writes=[Bw], out=Vc[0:127, g, 65:97], in_=cov[0:127, :])
        Bs = Buf()
        for j in range(2):
            for half in range(2):
                C.dma(C.sp, sl_t, w1s[half * 64:(half + 1) * 64, :, :], w1_kv[j].rearrange("(l d) h -> d l h", d=64), writes=[Bs])
                with nc.allow_non_contiguous_dma(reason="tiny transposed load of the compression position table"):
                    C.dma(C.sp, sl_t, posT[half * 64:(half + 1) * 64, :], pos_kv[j].rearrange("l d -> d l"), writes=[Bs])
                C.dma(C.sp, sl_t, w2s[half * 64:(half + 1) * 64, j, :], w2_kv[j], writes=[Bs])
            C.pool.op(nc.gpsimd.memset, writes=[Bw], ap=W1bd[j][:].rearrange("p a b -> p (a b)"), constant=0.0)
            C.dve.op(nc.vector.tensor_copy, reads=[Bs], writes=[Bw], out=W1bd[j][0:64, :, 0:64], in_=w1s[0:64, :, :])
            C.dve.op(nc.vector.tensor_copy, reads=[Bs], writes=[Bw], out=W1bd[j][64:128, :, 64:128], in_=w1s[64:128, :, :])
            C.dve.op(nc.vector.tensor_copy, reads=[Bs], writes=[Bw], out=posTb[:], in_=posT[:])
            if j == 0:
                for g in range(2):
                    C.dve.op(nc.vector.tensor_copy, reads=[Bs], writes=[Bw], out=W2k[g * 64:(g + 1) * 64, g, :], in_=w2s[g * 64:(g + 1) * 64, 0, :])
            else:
                for g in range(2):
                    C.dve.op(nc.vector.tensor_copy, reads=[Bs], writes=[Bw], out=W2v[g * 64:(g + 1) * 64, g * 64:(g + 1) * 64],
                             in_=w2s[g * 64:(g + 1) * 64, 1, :])
            for l in range(32):
                C.pe.op(nc.tensor.matmul, reads=[Bw], writes=[BpsT[0]], mark=(l == 31), out=psT[0][:, 0:1], lhsT=W1bd[j][:, l, :],
                        rhs=posTb[:, l:l + 1], start=(l == 0), stop=(l == 31))
            C.dve.op(nc.vector.tensor_copy, reads=[BpsT[0]], writes=[Bw], out=cvec[:, j:j + 1], in_=psT[0][:, 0:1])
        C.pool.op(nc.gpsimd.memset, writes=[Bt_], ap=rbx[32:33, :], constant=1.0)
        C.dma(C.sp, sl_t, rbx[0:32, :], rel_bias, writes=[Bt_])
        C.dma(C.sp, sl_t, ohx[:], c_ohx, writes=[Bt_])
        C.dma(C.sp, sl_t, selc[:].rearrange("p a b c -> p (a b c)"), c_selc.rearrange("p a b c -> p (a b c)"), writes=[Bt_])
        C.dve.op(nc.vector.tensor_copy, reads=[Bt_], writes=[Bt_], out=lh[:],
                 in_=bass.AP(rbx[:].tensor, rbx[:].offset, [list(rbx[:].ap[0]), [1, 8], [0, 128]]))
        BGs = []
        sl_gb = [C.slot('gb0'), C.slot('gb1')]
        for h in range(8):
            s = h % 2
            for ci, (c0, n) in enumerate(((0, 512), (512, 256))):
                C.pe.op(nc.tensor.matmul, reads=[Bt_], writes=[BpsT[ci]], out=psT[ci][:, 0:n], lhsT=lh[:, h, :], rhs=ohx[:, c0:c0 + n],
                        start=True, stop=True)
                if ci == 0:
                    C.act.op(nc.scalar.copy, reads=[BpsT[ci]], writes=[Bgb[s]], out=gb[s][:, c0:c0 + n], in_=psT[ci][:, 0:n])
                else:
                    C.dve.op(nc.vector.tensor_copy, reads=[BpsT[ci]], writes=[Bgb[s]], out=gb[s][:, c0:c0 + n], in_=psT[ci][:, 0:n])
            BGh = Buf()
            C.dma(C.sp, sl_gb[s], G_scr[h], gb[s][:], reads=[Bgb[s]], writes=[BGh]); BGs.append(BGh)
        Btz = Buf()
        sl_tz = C.slot("tz")
        for h in range(8):
            for d in range(2):
                tok = C.dma(C.sp, sl_tz, tz[:, h, d, :], toep(h, TOFF + 128 * d, TW - 1, 128, 128), reads=BGs)
            tok = C.dma(C.sp, sl_tz, tzc[:, h, :], toep(h, TOFF + 97, TW - 16, 16, 128), reads=BGs)
        Btz.w = tok
        for g in range(2):
            for d in range(2):
                C.dve.op(nc.vector.tensor_copy, reads=[Btz], writes=[Bw], out=BT[g][d][:], in_=tz[:, 4 * g:4 * g + 4, d, :])
            C.dve.op(nc.vector.tensor_copy, reads=[Btz], writes=[Bw], out=Mcmp[g][:], in_=tzc[:, 4 * g:4 * g + 4, :])
        C.pool.op(nc.gpsimd.memset, writes=[Bw], ap=Wm4[:].rearrange("p a b -> p (a b)"), constant=0.0)
        C.pool.op(nc.gpsimd.affine_select, writes=[Bw], out=Wm4[:], in_=Wm4[:], pattern=[[0, 4], [-1, 128]], base=0,
                  channel_multiplier=1, compare_op=ALU.is_gt, fill=NEG)
        C.pool.op(nc.gpsimd.memset, writes=[Bw], ap=Iw[:], constant=1.0)
        C.pool.op(nc.gpsimd.affine_select, writes=[Bw], out=Iw[:], in_=Iw[:], pattern=[[1, 144]], base=-120,
                  channel_multiplier=-1, compare_op=ALU.is_equal, fill=0.0)
        S.close()

    gs_scr = dscr("gs_scr", [4, 64, 6])
    if "D" in phases:
        Bm0 = Buf()
        C.pool.op(nc.gpsimd.memset, writes=[Bm0], ap=_w1flat[:, 0:544], constant=0.0)
        C.sp.wait(Bm0.w)
        tok = None
        with nc.allow_non_contiguous_dma(reason="tiny setup shuffles (page table spread, gate rows)"):
            for r8 in range(8):
                tok = C.dma(C.sp, sl_ds, idxraw[r8:128:8, :], bass.AP(ptab.tensor, 0, [[1, 16], [16, 16]]))
            tok = C.dma(C.sp, sl_ds, pm8f[:], c_pm8)
            for h in range(8):
                tok = C.dma(C.sp, sl_ds, mcst[96:127, h, :], toep(h, TOFF + 2017 - 16 * 96, TW - 16, 31, 4))
            for tau in range(16):
                for h in range(8):
                    tok = C.dma(C.sp, sl_ds, bspst[120:128, tau, h, :], toep(h, TOFF + 128 - tau, TW - 16, 8, 4))
            Bgs = Buf()
            for r in range(4):
                C.dma(C.sp, sl_ds, gs_scr[r], gat[0:64, 16, r:24:4], writes=[Bgs])
            for r in range(4):
                tok = C.dma(C.sp, sl_ds, gS[4 * r:4 * r + 4, :, :, :].rearrange("p b a g -> p b (a g)"),
                            bass.AP(gs_scr.tensor, r * 384, [[6, 4], [24, 16], [1, 6]]), reads=[Bgs])
        BDS.w = tok

    def d_setup_post(idx, idxf, Mcs, Bsp):
        C.dve.op(nc.vector.tensor_copy, reads=[BDS], writes=[BDS], out=idxf[:], in_=idxraw[:])
        C.dve.op(nc.vector.tensor_scalar, reads=[BDS], writes=[BDS], out=idxf[:], in0=idxf[:], scalar1=8.0, scalar2=pm8f[:, 0:1], op0=ALU.mult, op1=ALU.add)
        C.dve.op(nc.vector.tensor_copy, reads=[BDS], writes=[BDS], out=idx[:], in_=idxf[:])
        for g in range(2):
            C.dve.op(nc.vector.tensor_copy, reads=[BDS], writes=[BDS], out=Mcs[g][:], in_=mcst[:, 4 * g:4 * g + 4, :])
            C.dve.op(nc.vector.tensor_copy, reads=[BDS], writes=[BDS], out=Bsp[g][:], in_=bspst[:, :, 4 * g:4 * g + 4, :])

    def compress(S_ps, BS_ps, rhs_fn, Bsrc, hid, Bhid, Kdst, Vdst, Bdst):
        for j in range(2):
            p = S_ps[j]; Bp = BS_ps[j]
            for l in range(32):
                C.pe.op(nc.tensor.matmul, reads=[Bsrc], writes=[Bp], mark=(l == 31), out=p[:, 0:127], lhsT=W1bd[j][:, l, :],
                        rhs=rhs_fn(j, l), start=(l == 0), stop=(l == 31))
            C.act.op(nc.scalar.activation, reads=[Bp], writes=[Bhid[j]], out=hid[j][:, 0:127], in_=p[:, 0:127], func=AF.Silu,
                     bias=cvec[:, j:j + 1], scale=1.0)
        for g in range(2):
            p = S_ps[g]; Bp = BS_ps[g]
            C.pe.op(nc.tensor.matmul, reads=[Bhid[0]], writes=[Bp], out=p[0:64, 0:127], lhsT=W2k[:, g, :], rhs=hid[0][:, 0:127],
                    start=True, stop=True)
            C.dve.op(nc.vector.tensor_copy, reads=[Bp], writes=[Bdst], out=Kdst[g], in_=p[0:64, 0:127])
        p = S_ps[0]; Bp = BS_ps[0]
        C.pe.op(nc.tensor.matmul, reads=[Bhid[1]], writes=[Bp], out=p[0:127, 0:128], lhsT=hid[1][:, 0:127], rhs=W2v[:, :],
                start=True, stop=True)
        C.act.op(nc.scalar.copy, reads=[Bp], writes=[Bdst], out=Vdst, in_=p[0:127, 0:128].rearrange("p (g d) -> p g d", g=2))

    if "C" in phases:
        S = C.scope()
        psS = [S.psum("psS%d" % i, [128, 512], F32) for i in range(3)]; BpsS = [Buf() for _ in range(3)]
        psO = [S.psum("psO%d" % i, [128, 512], F32) for i in range(4)]; BpsO = [Buf() for _ in range(4)]
        psMX = S.psum("psMX", [128, 512], F32)
        psM = psMX[:, 0:128]; BpsM = Buf()
        psX = psMX[:, 128:384].bitcast(BF16).rearrange("p (j t) -> p j t", t=128); BpsX = Buf()
        hid = [S.sbuf("hid%d" % j, [128, 128], BF16) for j in range(2)]; Bhid = [Buf(), Buf()]
        Pb = [S.sbuf("Pb%d" % i, [128, 512], BF16) for i in range(4)]; BPb = [Buf() for _ in range(4)]
        oacc = S.sbuf("oacc", [128, 512], F32); Boacc = Buf()
        oab = S.sbuf("oab", [128, 512], BF16); Boab = Buf()
        sm = S.sbuf("sm", [128, 64], F32)
        imp = S.sbuf("imp", [128, 32], F32); sc = S.sbuf("sc", [128, 32], F32); mb = [S.sbuf("mb%d" % g, [128, 32], BF16) for g in range(2)]; Bmb = [Buf(), Buf()]
        Bsel = Buf(); BMB = [Buf(), Buf()]; Bc = Buf(); BoAT = Buf()
        compress(psS, BpsS, lambda j, l: kcvT[j][:, l:l + 16 * 126 + 1:16], Bc, hid, Bhid,
                 [Kcaug[g][0:64, 0:127] for g in range(2)], Vc[0:127, :, 0:64], Bc)
        sring = [0]; pring2 = [0]
        q3 = [qaug[g][:, :].rearrange("p (r t) -> p r t", r=4) for g in range(2)]

        def qk_exp(lhsT, K, M, qt, g, extra, reads):
            b = sring[0] % 3; sring[0] += 1
            C.pe.op(nc.tensor.matmul, reads=reads, writes=[BpsS[b]], mark=(extra is None), out=psS[b][0:M, :], lhsT=lhsT,
                    rhs=q3[g][0:K, :, qt * 128:(qt + 1) * 128], start=True, stop=(extra is None))
            if extra is not None:
                el, er = extra
                C.pe.op(nc.tensor.matmul, reads=reads, writes=[BpsS[b]], out=psS[b][0:M, :], lhsT=el, rhs=er, start=False, stop=True)
            pi = pring2[0] % 4; pring2[0] += 1
            C.act.op(nc.scalar.activation, reads=[BpsS[b]], writes=[BPb[pi]], out=Pb[pi][0:M, :], in_=psS[b][0:M, :], func=AF.Exp)
            return Pb[pi], BPb[pi]

        def combine(bi, W, qt, g, br, first):
            O = psO[bi]
            O3 = O[:, 0:4 * W].rearrange("p (r w) -> p r w", r=4)
            C.dve.op(nc.vector.tensor_scalar, reads=[BpsO[bi]], writes=[Bsel], out=sm[:, 0:4], in0=O3[:, :, 64], scalar1=1e-30,
                     scalar2=None, op0=ALU.add)
            C.dve.op(nc.vector.reciprocal, reads=[Bsel], writes=[Bsel], out=sm[:, 0:4], in_=sm[:, 0:4])
            C.dve.op(nc.vector.tensor_tensor, reads=[Bsel], writes=[Bsel], out=sm[:, 4:8], in0=sm[:, 0:4],
                     in1=gat[:, qt, br * 8 + g * 4:br * 8 + g * 4 + 4], op=ALU.mult)
            for r in range(4):
                dst = oacc[:, (g * 4 + r) * 64:(g * 4 + r + 1) * 64]
                if first:
                    C.dve.op(nc.vector.tensor_scalar, reads=[BpsO[bi], Bsel], writes=[Boacc], out=dst, in0=O[:, r * W:r * W + 64],
                             scalar1=sm[:, 4 + r:5 + r], scalar2=None, op0=ALU.mult)
                else:
                    C.dve.op(nc.vector.scalar_tensor_tensor, reads=[BpsO[bi], Bsel], writes=[Boacc], out=dst, in0=O[:, r * W:r * W + 64],
                             scalar=sm[:, 4 + r:5 + r], in1=dst, op0=ALU.mult, op1=ALU.add)

        SKEW = 2
        tiles = []

        def sel_part1(qt, g, bi):
            O3 = psO[bi][:, 0:388].rearrange("p (r w) -> p r w", r=4)
            C.dve.op(nc.vector.tensor_scalar, reads=[BpsO[bi]], writes=[Bsel], out=sm[:, 8:12], in0=O3[:, :, 64], scalar1=1e-30,
                     scalar2=None, op0=ALU.add)
            C.dve.op(nc.vector.reciprocal, reads=[Bsel], writes=[Bsel], out=sm[:, 8:12], in_=sm[:, 8:12])
            for r in range(4):
                if r == 0:
                    C.dve.op(nc.vector.tensor_scalar, reads=[BpsO[bi], Bsel], writes=[Bsel], out=imp[:], in0=psO[bi][:, 65:97],
                             scalar1=sm[:, 8:9], scalar2=None, op0=ALU.mult)
                else:
                    C.dve.op(nc.vector.scalar_tensor_tensor, reads=[BpsO[bi], Bsel], writes=[Bsel], out=imp[:],
                             in0=psO[bi][:, r * 97 + 65:r * 97 + 97], scalar=sm[:, 8 + r:9 + r], in1=imp[:], op0=ALU.mult, op1=ALU.add)
            C.dve.op(nc.vector.tensor_tensor, reads=[Bsel], writes=[Bsel], out=sc[:], in0=imp[:], in1=selc[:, 0, qt, :], op=ALU.mult)
            C.dve.op(nc.vector.tensor_tensor, reads=[Bsel], writes=[Bsel], out=sc[:], in0=sc[:], in1=selc[:, 1, qt, :], op=ALU.add)
            C.dve.op(nc.vector.max, reads=[Bsel], writes=[Bsel], out=sm[:, 16:24], in_=sc[:])
            C.dve.op(nc.vector.tensor_scalar, reads=[Bsel], writes=[Bmb[g]], out=mb[g][:], in0=sc[:], scalar1=sm[:, 23:24], scalar2=NEG,
                     op0=ALU.is_lt, op1=ALU.mult)
            combine(bi, 97, qt, g, 0, True)

        def sel_part2(qt, g):
            C.pe.op(nc.tensor.matmul, reads=[Bmb[g]], writes=[BpsM], out=psM[64:96, 0:128], lhsT=mb[g][:, :], rhs=identb[:, :], start=True, stop=True)
            C.act.op(nc.scalar.copy, reads=[BpsM], writes=[BMB[g]], out=q3[g][64:96, :, qt * 128:(qt + 1) * 128],
                     in_=bass.AP(psM[64:96, 0:128].tensor, psM[64:96, 0:128].offset, [list(psM[64:96, 0:128].ap[0]), [0, 4], [1, 128]]))

        def fin1(qt):
            C.act.op(nc.scalar.copy, reads=[Boacc], writes=[Boab], out=oab[:], in_=oacc[:])

        def fin2(qt):
            for j in range(4):
                C.pe.op(nc.tensor.transpose, reads=[Boab], writes=[BpsX], mark=(j == 3), out=psX[:, j, :], in_=oab[:, j * 128:(j + 1) * 128],
                        identity=identb[:, :])
            C.dve.op(nc.vector.tensor_copy, reads=[BpsX], writes=[BoAT], out=oAT[:, :, qt * 128:(qt + 1) * 128], in_=psX[:, :, :])

        for qt in range(16):
            nwin = min(qt + 1, 5)
            for g in range(2):
                ncv = min(8 * (qt + 1), 127)
                c0 = 128 - 8 * qt
                tiles.append(dict(qt=qt, g=g, lhsT=Kcaug[g][:, 0:ncv], M=ncv, extra=(Iw[0:16, c0:c0 + ncv], Mcmp[g][:, :, :]), reads=[Bc, BMB[g]],
                                  bank=g, W=97, rhs=Vc[0:ncv, g, :], first=True, last=True,
                                  hooks=[(0, lambda qt=qt, g=g: sel_part1(qt, g, g)), (min(3, nwin), lambda qt=qt, g=g: sel_part2(qt, g))]))
            for g in range(2):
                kts = list(range(max(0, qt - 4), qt + 1))
                for i, kt in enumerate(kts):
                    d = qt - kt
                    extra = None
                    if d <= 1:
                        extra = (identb[:, :], BT[g][d][:, :, :])
                    elif d == 4:
                        extra = (identb[:, :], Wm4[:, :, :])
                    tiles.append(dict(qt=qt, g=g, lhsT=Kwaug[g][:, kt * 128:(kt + 1) * 128], M=128, extra=extra, reads=[BMB[g]], bank=2, W=65,
                                      rhs=Vw[:, kt, g, :], first=(i == 0), last=(i == len(kts) - 1),
                                      hooks=([(0, lambda qt=qt, g=g: combine(2, 65, qt, g, 2, False))] if i == len(kts) - 1 else [])))
                for kt in range(qt + 1):
                    d = qt - kt
                    extra = (identb[:, :], BT[g][d][:, :, :]) if d <= 1 else None
                    hooks = []
                    if kt == qt:
                        hooks.append((0, lambda qt=qt, g=g: combine(3, 65, qt, g, 1, False)))
                        if g == 1:
                            hooks.append((0, lambda qt=qt: fin1(qt)))
                            hooks.append((2, lambda qt=qt: fin2(qt)))
                    tiles.append(dict(qt=qt, g=g, lhsT=Ksaug[g][:, kt * 128:(kt + 1) * 128], M=128, extra=extra, reads=[BMB[g]], bank=3, W=65,
                                      rhs=Vs[:, kt, g, :], first=(kt == 0), last=(kt == qt), hooks=hooks))
        pending = {}
        inflight = {}
        nt_ = len(tiles)
        for step in range(nt_ + SKEW + 4):
            j = step - SKEW
            if 0 <= j < nt_:
                T = tiles[j]
                P, BP = inflight.pop(j)
                bi = T["bank"]; W = T["W"]; M = T["M"]
                for r in range(4):
                    C.pe.op(nc.tensor.matmul, reads=[BP] + T["reads"], writes=[BpsO[bi]], mark=(r == 3), out=psO[bi][:, r * W:(r + 1) * W],
                            lhsT=P[0:M, r * 128:(r + 1) * 128], rhs=T["rhs"], start=(T["first"] and r == 0), stop=(T["last"] and r == 3),
                            skip_group_check=True)
                for (dl, fn) in T["hooks"]:
                    pending.setdefault(step + dl, []).append(fn)
            for fn in pending.pop(step, []):
                fn()
            if step < nt_:
                T = tiles[step]
                inflight[step] = qk_exp(T["lhsT"], 97, T["M"], T["qt"], T["g"], T["extra"], T["reads"])
        S.close()
    Bsmp = Buf()
    for g in range(2):
        C.dve.op(nc.vector.tensor_copy, writes=[Bsmp], out=qS[g][:], in_=qaug[g][:, :].rearrange("p (r t) -> p r t", r=4)[:, :, 2048:NTOK])
        C.dve.op(nc.vector.tensor_copy, writes=[Bsmp], out=KsS[g][:], in_=Ksaug[g][:, 2048:NTOK])
        C.dve.op(nc.vector.tensor_copy, writes=[Bsmp], out=KwS[g][:], in_=Kwaug[g][:, 2048:NTOK])
    SAB1.close()


    if "D" in phases:
        S = C.scope()
        oS_scr = dscr("oS_scr", [64, 512])
        pgb = [S.sbuf("pg%d" % c, [128, 16, 128], F32) for c in range(4)]; Bpg = [Buf() for _ in range(4)]
        wkvb = [S.sbuf("wkvb%d" % j, [128, 4, 128], F32) for j in range(2)]; Bwkv_ = [Buf(), Buf()]
        pgbf = [S.sbuf("pgbf%d" % c, [128, 16, 128], BF16) for c in range(2)]; Bpgf = [Buf() for _ in range(3)]
        pgbf.append(wst[1][:, :, :].rearrange("p a b -> p (a b)")[:, 1024:2048].bitcast(BF16).rearrange("p (a b) -> p a b", b=128))
        cT = [S.sbuf("cT%d" % j, [128, 2048], BF16) for j in range(2)]; BcT = Buf()
        KsT = [S.sbuf("KsT%d" % g, [97, 2048], BF16) for g in range(2)]; BKsT = Buf()
        KwT = [S.sbuf("KwT%d" % g, [97, 512], BF16) for g in range(2)]; BKwT = Buf()
        Vsb = S.sbuf("Vsb", [128, 16, 2, 65], BF16); BVsb = Buf()
        Vwb = S.sbuf("Vwb", [128, 4, 2, 65], BF16); BVwb = Buf()
        VnS = S.sbuf("VnS", [4, 16, 2, 2, 65], BF16)
        vnst1 = wst[0][0:4, :, :].rearrange("p a b -> p (a b)").rearrange("p (b c) -> p b c", c=128)
        Ws = S.sbuf("Ws", [128, 4, 4], BF16)
        oS = wst[0][0:16, :, :].rearrange("p a (b c) -> p (a b) c", c=64).rearrange("p (b g) c -> p b g c", g=2); BoS = Buf()
        Rsel = S.sbuf("Rsel", [16, 4], F32)
        selS = S.sbuf("selS", [4, 2, 32], F32)
        hid = [S.sbuf("hidD%d" % j, [128, 128], BF16) for j in range(2)]; Bhid = [Buf(), Buf()]
        Pd = [S.sbuf("Pd%d" % i, [128, 288], BF16) for i in range(3)]; BPd = [Buf() for _ in range(3)]
        smd = S.sbuf("smd", [16, 64], F32); impn = S.sbuf("impn", [16, 32], F32)
        sc4 = S.sbuf("sc4", [4, 32], F32); mb4 = S.sbuf("mb4", [4, 32], BF16)
        Ball_s = [S.sbuf("Ball_s%d" % g, [128, 17, 4, 4], BF16) for g in range(2)]
        Ball_w = [S.sbuf("Ball_w%d" % g, [128, 5, 4, 4], BF16) for g in range(2)]
        oSt = wst[1][0:64, :, :].rearrange("p a (b c) -> p (a b) c", c=64).rearrange("p a c -> p (a c)")[:, 0:512]; oSb = S.sbuf("oSb", [64, 512], BF16)
        psTr = [S.psum("psTr%d" % i, [128, 512], F32) for i in range(2)]; BpsTr = [Buf(), Buf()]
        psC = [S.psum("psC%d" % i, [128, 512], F32) for i in range(2)]; BpsC = [Buf(), Buf()]
        psTr4 = [psTr[0], psTr[1], psC[0], psC[1]]; BpsTr4 = [BpsTr[0], BpsTr[1], BpsC[0], BpsC[1]]
        psSd = [S.psum("psSd%d" % i, [128, 512], F32) for i in range(2)]; BpsSd = [Buf(), Buf()]
        psOd = S.psum("psOd", [128, 512], F32); BpsOd = Buf(); BpsOw = Buf()
        psMd = S.psum("psMd", [128, 512], F32); BpsMd = Buf()
        sl_pg = [C.slot("pg%d" % c) for c in range(4)]; sl_wk = [C.slot("wk0"), C.slot("wk1")]
        Bk = Buf(); BqS = Buf()
        for j in range(2):
            C.dma(C.sp, sl_misc, o_swin[j][:, 0:508, :], st_win[j][:, 4:512, :])
        idx = S.sbuf("idx", [128, 16], I32); idxf = S.sbuf("idxf", [128, 16], F32)
        Mcs = [S.sbuf("Mcs%d" % g, [128, 4, 4], BF16) for g in range(2)]
        Bsp = [S.sbuf("Bsp%d" % g, [128, 16, 4, 4], BF16) for g in range(2)]
        d_setup_post(idx, idxf, Mcs, Bsp)
        for g in range(2):
            C.pool.op(nc.gpsimd.memset, writes=[Bk], ap=KsT[g][64:96, :], constant=1.0)
            E3 = KsT[g][64:96, :].rearrange("p (a m) -> p a m", m=128)
            C.pool.op(nc.gpsimd.affine_select, writes=[Bk], out=E3, in_=E3, pattern=[[0, 16], [1, 128]], base=0,
                      channel_multiplier=-4, compare_op=ALU.is_ge, fill=0.0)
            C.pool.op(nc.gpsimd.affine_select, writes=[Bk], out=E3, in_=E3, pattern=[[0, 16], [-1, 128]], base=3,
                      channel_multiplier=4, compare_op=ALU.is_ge, fill=0.0)
            C.pool.op(nc.gpsimd.memset, writes=[Bk], ap=KsT[g][96:97, :], constant=1.0)
            C.pool.op(nc.gpsimd.memset, writes=[Bk], ap=KwT[g][64:96, :], constant=0.0)
            C.pool.op(nc.gpsimd.memset, writes=[Bk], ap=KwT[g][96:97, :], constant=1.0)
        C.pool.op(nc.gpsimd.memset, writes=[Bk], ap=Vsb[:].rearrange("p a b c -> p (a b c)"), constant=1.0)
        C.pool.op(nc.gpsimd.memset, writes=[Bk], ap=Vwb[:].rearrange("p a b c -> p (a b c)"), constant=1.0)
        C.pool.op(nc.gpsimd.memset, writes=[Bk], ap=VnS[:].rearrange("p a b c d -> p (a b c d)"), constant=1.0)
        Bvn = Buf()
        for jj, j in enumerate((3, 5)):
            C.dma(C.sp, sl_misc, vnst1, o_skv[j].rearrange("(b i) c -> i b c", i=4), writes=[Bvn])
            C.dve.op(nc.vector.tensor_copy, reads=[Bvn, Bk], writes=[Bk], out=VnS[:, :, jj, :, 0:64],
                     in_=vnst1.rearrange("p b (g d) -> p b g d", g=2))
            Bvn.r.append(Bk.w)
        C.pool.op(nc.gpsimd.memset, writes=[Bk], ap=Ws[:].rearrange("p a b -> p (a b)"), constant=0.0)
        C.pool.op(nc.gpsimd.affine_select, writes=[Bk], out=Ws[:], in_=Ws[:], pattern=[[0, 4], [-1, 4]], base=0, channel_multiplier=1,
                  compare_op=ALU.is_gt, fill=NEG)
        for g in range(2):
            C.pool.op(nc.gpsimd.memset, writes=[Bk], ap=Ball_s[g][:].rearrange("p a b c -> p (a b c)"), constant=0.0)
            C.pool.op(nc.gpsimd.memset, writes=[Bk], ap=Ball_w[g][:].rearrange("p a b c -> p (a b c)"), constant=0.0)
            C.dve.op(nc.vector.tensor_copy, reads=[BDS, Bk], writes=[Bk], out=Ball_s[g][:, 0:16, :, :], in_=Bsp[g][:, :, :, :])
            C.dve.op(nc.vector.tensor_copy, reads=[Bk], writes=[Bk], out=Ball_s[g][0:4, 16, :, :], in_=BT[g][0][0:4, :, 0:4])
            C.dve.op(nc.vector.tensor_copy, reads=[Bk], writes=[Bk], out=Ball_w[g][:, 0, :, :], in_=Ws[:, :, :])
            C.dve.op(nc.vector.tensor_copy, reads=[Bk], writes=[Bk], out=Ball_w[g][:, 3, :, :], in_=BT[g][1][:, :, 0:4])
            C.dve.op(nc.vector.tensor_copy, reads=[Bk], writes=[Bk], out=Ball_w[g][0:4, 4, :, :], in_=BT[g][0][0:4, :, 0:4])
        C.dve.op(nc.vector.tensor_copy, reads=[Bid], writes=[Bk], out=Rsel[:], in_=identf[0:16, 0:4])
        for r in range(1, 4):
            C.dve.op(nc.vector.tensor_tensor, reads=[Bid, Bk], writes=[Bk], out=Rsel[:], in0=Rsel[:], in1=identf[0:16, 4 * r:4 * r + 4], op=ALU.add)
        C.pool.op(nc.gpsimd.memset, writes=[Bk], ap=selS[:, 0, :], constant=1.0)
        C.pool.op(nc.gpsimd.memset, writes=[Bk], ap=selS[:, 1, :], constant=0.0)
        for j in (0, 31):
            C.pool.op(nc.gpsimd.memset, writes=[Bk], ap=selS[:, 0, j:j + 1], constant=0.0)
            C.pool.op(nc.gpsimd.memset, writes=[Bk], ap=selS[:, 1, j:j + 1], constant=1e9)
        tr = [0]
        prd = [0]

        def transposes(src_fn, nparts_out, ntiles, dst_fn, Bsrc, Bdst, bf=False):
            for k0 in range(0, ntiles, 4):
                b = tr[0] % 4; tr[0] += 1
                pt = psTr4[b][:, 0:256].bitcast(BF16) if bf else psTr4[b]
                for k in range(k0, k0 + 4):
                    C.pe.op(nc.tensor.transpose, reads=[Bsrc, Bid], writes=[BpsTr4[b]], mark=(k == k0 + 3),
                            out=pt[0:nparts_out, (k - k0) * 128:(k - k0 + 1) * 128], in_=src_fn(k), identity=(identb[:, :] if bf else identf[:, :]))
                if b % 2 == 0:
                    C.act.op(nc.scalar.copy, reads=[BpsTr4[b]], writes=[Bdst], out=dst_fn(k0), in_=pt[0:nparts_out, :])
                else:
                    C.dve.op(nc.vector.tensor_copy, reads=[BpsTr4[b]], writes=[Bdst], out=dst_fn(k0), in_=pt[0:nparts_out, :])

        for bb in range(16):
            for c in range(4):
                for t_ in Bpg[c].r:
                    C.pool.wait(t_)
                C.pool.wait(Bpg[c].w); C.pool.wait(BDS.w); sl_pg[c].pre_issue(C.pool)
                ins = nc.gpsimd.indirect_dma_start(out=pgb[c][:, :, :].rearrange("p a b -> p (a b)"), out_offset=None, in_=caches[c][:, :],
                                                   in_offset=bass.IndirectOffsetOnAxis(ap=idx[:, bb:bb + 1], axis=0))
                C.ndma += 1
                Bpg[c].w = sl_pg[c].issued(ins); Bpg[c].r = []
            for j in range(2):
                C.dma(C.sp, sl_wk[j], wkvb[j][:], st_win[j][bb].rearrange("(t p) c -> p t c", p=128), writes=[Bwkv_[j]])
            for c in range(3):
                if c == 1:
                    C.dve.op(nc.vector.tensor_copy, reads=[Bpg[c]], writes=[Bpgf[c]], out=pgbf[c][:, :, :], in_=pgb[c][:, :, :])
                else:
                    C.act.op(nc.scalar.copy, reads=[Bpg[c]], writes=[Bpgf[c]], out=pgbf[c][:, :, :], in_=pgb[c][:, :, :])
            for j in range(2):
                transposes(lambda k, j=j: pgbf[j][:, k, :], 128, 16, lambda k0, j=j: cT[j][:, k0 * 128:(k0 + 4) * 128], Bpgf[j], BcT, bf=True)
            compress(psC, BpsC, lambda j, l: cT[j][:, (l % 16) * 128 + l // 16:(l % 16) * 128 + l // 16 + 127], BcT, hid, Bhid, [Kcaug[g][0:64, 0:127] for g in range(2)], Vc[0:127, :, 0:64], Bk)
            for g in range(2):
                transposes(lambda k, g=g: pgbf[2][:, k, g * 64:(g + 1) * 64], 64, 16, lambda k0, g=g: KsT[g][0:64, k0 * 128:(k0 + 4) * 128],
                           Bpgf[2], BKsT, bf=True)
                transposes(lambda k, g=g: wkvb[0][:, k, g * 64:(g + 1) * 64], 64, 4, lambda k0, g=g: KwT[g][0:64, :], Bwkv_[0], BKwT)
            C.act.op(nc.scalar.copy, reads=[Bpg[3]], writes=[BVsb], out=Vsb[:, :, :, 0:64],
                     in_=pgb[3][:, :, :].rearrange("p a (g d) -> p a g d", g=2))
            C.dve.op(nc.vector.tensor_copy, reads=[Bwkv_[1]], writes=[BVwb], out=Vwb[:, :, :, 0:64],
                     in_=wkvb[1][:, :, :].rearrange("p a (g d) -> p a g d", g=2))
            for g in range(2):
                q16 = qS[g][:, :, 4 * bb:4 * bb + 4]

                def branch(tiles, W, bank_col, ball=None):
                    sb = prd[0] % 2; pi = prd[0] % 3; prd[0] += 1
                    nt_ = len(tiles)
                    if ball is not None:
                        Mb = ball.shape[0]
                        C.pe.op(nc.tensor.matmul, reads=[Bk], writes=[BpsSd[sb]], mark=False, out=psSd[sb][0:Mb, 0:nt_ * 16], lhsT=identb[0:Mb, 0:Mb],
                                rhs=ball, start=True, stop=False)
                    for ti, (lh, M, extra, vr, rd) in enumerate(tiles):
                        if ball is not None:
                            C.pe.op(nc.tensor.matmul, reads=rd, writes=[BpsSd[sb]], mark=(ti == nt_ - 1), out=psSd[sb][0:M, ti * 16:(ti + 1) * 16],
                                    lhsT=lh, rhs=q16, start=False, stop=(ti == nt_ - 1), skip_group_check=True)
                            continue
                        C.pe.op(nc.tensor.matmul, reads=rd, writes=[BpsSd[sb]], mark=(extra is None and ti == len(tiles) - 1),
                                out=psSd[sb][0:M, ti * 16:(ti + 1) * 16], lhsT=lh, rhs=q16, start=True, stop=(extra is None))
                        if extra is not None:
                            C.pe.op(nc.tensor.matmul, reads=[Bk], writes=[BpsSd[sb]], mark=(ti == len(tiles) - 1),
                                    out=psSd[sb][0:M, ti * 16:(ti + 1) * 16], lhsT=extra[0], rhs=extra[1], start=False, stop=True)
                    C.act.op(nc.scalar.activation, reads=[BpsSd[sb]], writes=[BPd[pi]], out=Pd[pi][:, 0:nt_ * 16], in_=psSd[sb][:, 0:nt_ * 16], func=AF.Exp)
                    for ti, (lh, M, extra, vr, rd) in enumerate(tiles):
                        C.pe.op(nc.tensor.matmul, reads=[BPd[pi]] + rd, writes=[BpsOd], mark=(ti == nt_ - 1),
                                out=psOd[0:16, bank_col:bank_col + W], lhsT=Pd[pi][0:M, ti * 16:(ti + 1) * 16], rhs=vr,
                                start=(ti == 0), stop=(ti == nt_ - 1), skip_group_check=True)

                def combine_s(bank_col, br, first, Bo=None):
                    Bo = Bo or BpsOd
                    C.dve.op(nc.vector.tensor_scalar, reads=[Bo], writes=[Bk], out=smd[:, 0:1], in0=psOd[0:16, bank_col + 64:bank_col + 65],
                             scalar1=1e-30, scalar2=None, op0=ALU.add)
                    C.dve.op(nc.vector.reciprocal, reads=[Bk], writes=[Bk], out=smd[:, 0:1], in_=smd[:, 0:1])
                    C.dve.op(nc.vector.tensor_tensor, reads=[Bk], writes=[Bk], out=smd[:, 1:2], in0=smd[:, 0:1], in1=gS[:, bb, br, g:g + 1], op=ALU.mult)
                    dst = oS[:, bb, g, :]
                    if first:
                        C.dve.op(nc.vector.tensor_scalar, reads=[Bo, Bk], writes=[BoS], out=dst, in0=psOd[0:16, bank_col:bank_col + 64],
                                 scalar1=smd[:, 1:2], scalar2=None, op0=ALU.mult)
                    else:
                        C.dve.op(nc.vector.scalar_tensor_tensor, reads=[Bo, Bk], writes=[BoS], out=dst, in0=psOd[0:16, bank_col:bank_col + 64],
                                 scalar=smd[:, 1:2], in1=dst, op0=ALU.mult, op1=ALU.add)

                sb = prd[0] % 2; pi = prd[0] % 3; prd[0] += 1
                wt = [(KwT[g][:, kt * 128:(kt + 1) * 128], 128, Vwb[:, kt, g, :]) for kt in range(4)]
                wt.append((KwS[g][:, 4 * bb:4 * bb + 4], 4, VnS[0:4, bb, 1, g, :]))
                C.pe.op(nc.tensor.matmul, reads=[Bk], writes=[BpsSd[sb]], mark=False, out=psSd[sb][:, 16:96], lhsT=identb[:, :],
                        rhs=Ball_w[g][:, :, :, :], start=True, stop=False)
                for ti, (lh, M, vr) in enumerate(wt):
                    C.pe.op(nc.tensor.matmul, reads=[BKwT, BqS, Bk], writes=[BpsSd[sb]], mark=False, out=psSd[sb][0:M, 16 + ti * 16:32 + ti * 16],
                            lhsT=lh, rhs=q16, start=False, stop=False, skip_group_check=True)
                C.pe.op(nc.tensor.matmul, reads=[Bk, BqS], writes=[BpsSd[sb]], mark=False, out=psSd[sb][0:127, 0:16], lhsT=Kcaug[g][:, 0:127],
                        rhs=q16, start=False, stop=False, skip_group_check=True)
                C.pe.op(nc.tensor.matmul, reads=[Bk], writes=[BpsSd[sb]], out=psSd[sb][0:127, 0:16], lhsT=identb[0:127, 0:127],
                        rhs=Mcs[g][0:127, :, :], start=False, stop=True, skip_group_check=True)
                C.act.op(nc.scalar.activation, reads=[BpsSd[sb]], writes=[BPd[pi]], out=Pd[pi][:, 0:96], in_=psSd[sb][:, 0:96], func=AF.Exp)
                C.pe.op(nc.tensor.matmul, reads=[BPd[pi], Bk], writes=[BpsOd], out=psOd[0:16, 0:97], lhsT=Pd[pi][0:127, 0:16], rhs=Vc[0:127, g, :],
                        start=True, stop=True, skip_group_check=True)
                for ti, (lh, M, vr) in enumerate(wt):
                    C.pe.op(nc.tensor.matmul, reads=[BPd[pi], BVwb, Bk], writes=[BpsOw], mark=(ti == 4), out=psOd[0:16, 128:193],
                            lhsT=Pd[pi][0:M, 16 + ti * 16:32 + ti * 16], rhs=vr, start=False, stop=(ti == 4), skip_group_check=True)
                C.dve.op(nc.vector.tensor_scalar, reads=[BpsOd], writes=[Bk], out=smd[:, 8:9], in0=psOd[0:16, 64:65], scalar1=1e-30, scalar2=None,
                         op0=ALU.add)
                C.dve.op(nc.vector.reciprocal, reads=[Bk], writes=[Bk], out=smd[:, 8:9], in_=smd[:, 8:9])
                C.dve.op(nc.vector.tensor_scalar, reads=[BpsOd, Bk], writes=[Bk], out=impn[:], in0=psOd[0:16, 65:97], scalar1=smd[:, 8:9],
                         scalar2=None, op0=ALU.mult)
                C.pe.op(nc.tensor.matmul, reads=[Bk], writes=[BpsMd], out=psMd[0:4, 0:32], lhsT=Rsel[:, :], rhs=impn[:, :], start=True, stop=True)
                C.dve.op(nc.vector.tensor_tensor, reads=[BpsMd, Bk], writes=[Bk], out=sc4[:], in0=psMd[0:4, 0:32], in1=selS[:, 0, :], op=ALU.mult)
                C.dve.op(nc.vector.tensor_tensor, reads=[Bk], writes=[Bk], out=sc4[:], in0=sc4[:], in1=selS[:, 1, :], op=ALU.add)
                C.dve.op(nc.vector.max, reads=[Bk], writes=[Bk], out=smd[0:4, 16:24], in_=sc4[:])
                C.dve.op(nc.vector.tensor_scalar, reads=[Bk], writes=[Bk], out=mb4[:], in0=sc4[:], scalar1=smd[0:4, 22:23], scalar2=NEG,
                         op0=ALU.is_lt, op1=ALU.mult)
                C.pe.op(nc.tensor.matmul, reads=[Bk], writes=[BpsMd], out=psMd[64:96, 64:68], lhsT=mb4[:, :], rhs=identb[0:4, 0:4], start=True, stop=True)
                src = psMd[64:96, 64:68]
                C.act.op(nc.scalar.copy, reads=[BpsMd], writes=[BqS], out=qS[g][64:96, :, 4 * bb:4 * bb + 4],
                         in_=bass.AP(src.tensor, src.offset, [list(src.ap[0]), [0, 4], [1, 4]]))
                combine_s(0, 0, True)
                combine_s(128, 2, False, BpsOw)
                tiles = []
                for kt in range(16):
                    extra = (identb[:, :], Bsp[g][:, kt, :, :])
                    tiles.append((KsT[g][:, kt * 128:(kt + 1) * 128], 128, extra, Vsb[:, kt, g, :], [BKsT, BVsb, BqS]))
                tiles.append((KsS[g][:, 4 * bb:4 * bb + 4], 4, (identb[0:4, 0:4], BT[g][0][0:4, :, 0:4]), VnS[0:4, bb, 0, g, :], [Bk, BqS]))
                branch(tiles, 65, 256, Ball_s[g][:, :, :, :])
                combine_s(256, 1, False)
        Bsh = Buf()
        for r in range(4):
            for g in range(2):
                C.dma(C.sp, sl_misc, bass.AP(oS_scr.tensor, g * 256 + r * 64, [[512, 4], [2048, 16], [1, 64]]), oS[4 * r:4 * r + 4, :, g, :],
                      reads=[BoS], writes=[Bsh])
        C.dma(C.sp, sl_misc, oSt, oS_scr, reads=[Bsh], writes=[Bsh])
        C.dve.op(nc.vector.tensor_copy, reads=[Bsh], writes=[Bsh], out=oSb[:], in_=oSt)
        psX3 = psTr[0][:].bitcast(BF16).rearrange("p (j t) -> p j t", t=128)
        for j in range(4):
            C.pe.op(nc.tensor.transpose, reads=[Bsh, Bid], writes=[BpsTr[0]], mark=(j == 3), out=psX3[:, j, 0:64], in_=oSb[0:64, j * 128:(j + 1) * 128],
                    identity=identb[0:64, 0:64])
        C.dve.op(nc.vector.tensor_copy, reads=[BpsTr[0]], writes=[Bsh], out=oAT[:, :, 2048:NTOK], in_=psX3[:, 0:4, 0:64])
        S.close()
    SAB2.close()


    if "E" in phases:
        S = C.scope()
        qdT = S.sbuf("qdT", [128, 4, NTOK], BF16); kdT = S.sbuf("kdT", [128, 4, NTOK], BF16)
        vtok = S.sbuf("vtok", [128, 16, 512], BF16); vS = [S.sbuf("vS%d" % i, [4, 512], BF16) for i in range(2)]; BvS = [Buf(), Buf()]
        wq = S.sbuf("wq", [128, KC, 512], BF16); wf = S.sbuf("wf", [128, KC, 512], BF16); wib = S.sbuf("wib", [128, KC, 512], BF16)
        lbt = S.sbuf("lbt", [128, 2, 4], F32); oml = S.sbuf("oml", [128, 4], F32); noml = S.sbuf("noml", [128, 4], F32)
        rmask = S.sbuf("rmask", [128, 512], F32); rmask4 = S.sbuf("rmask4", [128, 64], F32)
        eGl = S.sbuf("eGl", [128, 4, 32], F32); eGls = S.sbuf("eGls", [128, 4, 16], F32)
        Sst = S.sbuf("Sst", [128, 4, 128], F32); Sbf = S.sbuf("Sbf", [128, 4, 128], BF16)
        hgbc = S.sbuf("hgbc", [128, 512], F32); hgcol = S.sbuf("hgcol", [128, 4], F32)
        tri = S.sbuf("tri", [128, 64], F32)
        tmpf2 = [[S.sbuf("tmpf%d_%d" % (i, k), [128, 512], F32) for i in range(5)] for k in range(2)]
        Btmp2 = [[Buf() for _ in range(5)] for k in range(2)]
        pbank = [S.psum("psE%d" % i, [128, 512], F32) for i in range(8)]; Bpbank = [Buf() for _ in range(8)]
        ps2 = pbank[0:2]; Bps2 = Bpbank[0:2]
        psA = pbank[2][:, 0:256].rearrange("p (h t) -> p h t", h=4); BpsA = Bpbank[2]
        psK = pbank[3][:, 0:256].bitcast(BF16).rearrange("p (h t) -> p h t", h=4); BpsK = Bpbank[3]
        psOo = pbank[4][:, :].rearrange("p (h t) -> p h t", h=4); BpsOo = Bpbank[4]
        psD = [pbank[5 + i][:, :].rearrange("p (h t) -> p h t", h=4) for i in range(2)]; BpsD = Bpbank[5:7]
        psX2 = pbank[7][:, 0:256].bitcast(BF16).rearrange("p (h t) -> p h t", h=4); BpsX2 = Bpbank[7]
        Bw = Buf(); Bc = Buf(); Bqk = Buf(); Bv = Buf(); BoBT = Buf()
        Bwh = [Buf(), Buf()]
        for half in range(2):
            load_w(wf[:, :, half * 256:(half + 1) * 256], Bwh[half], w_in[:, C_FB + half * 256:C_FB + (half + 1) * 256], 256)
            load_w(wq[:, :, half * 256:(half + 1) * 256], Bwh[half], w_in[:, C_QB + half * 256:C_QB + (half + 1) * 256], 256)
        load_w(wib, Bw, w_in[:, C_IB:C_IB + 512], 512)
        with nc.allow_non_contiguous_dma(reason="tiny strided load of the HGRN lower-bound logits"):
            C.dma(C.sp, sl_misc, lbt[:], hg_lower.rearrange("a (h d) -> d a h", d=128), writes=[Bc])
            C.dma(C.sp, sl_misc, hgcol[:], hg_norm_g.rearrange("a (h d) -> d (a h)", d=128), writes=[Bc])
        C.dma(C.sp, sl_misc, hgbc[:], dram_bcast(hg_norm_g, 128), writes=[Bc])
        C.dma(C.sp, sl_misc, tri[:], c_tri, writes=[Bc])
        C.dve.op(nc.vector.tensor_tensor, reads=[Bc], writes=[Bc], out=oml[:], in0=lbt[:, 0, :], in1=lbt[:, 1, :], op=ALU.subtract)
        C.act.op(nc.scalar.activation, reads=[Bc], writes=[Bc], out=oml[:], in_=oml[:], func=AF.Exp)
        C.dve.op(nc.vector.tensor_scalar, reads=[Bc], writes=[Bc], out=oml[:], in0=oml[:], scalar1=1.0, scalar2=None, op0=ALU.add)
        C.dve.op(nc.vector.reciprocal, reads=[Bc], writes=[Bc], out=oml[:], in_=oml[:])
        C.dve.op(nc.vector.tensor_scalar, reads=[Bc], writes=[Bc], out=noml[:], in0=oml[:], scalar1=-1.0, scalar2=None, op0=ALU.mult)
        C.pool.op(nc.gpsimd.memset, writes=[Bc], ap=rmask[:], constant=1.0)
        C.pool.op(nc.gpsimd.memset, writes=[Bc], ap=rmask[:, 0:512:64], constant=0.0)
        C.pool.op(nc.gpsimd.memset, writes=[Bc], ap=rmask4[:], constant=1.0)
        C.pool.op(nc.gpsimd.memset, writes=[Bc], ap=rmask4[:, 0:64:4], constant=0.0)
        C.pool.op(nc.gpsimd.memset, writes=[Bc], ap=Sst[:].rearrange("p a b -> p (a b)"), constant=0.0)
        C.pool.op(nc.gpsimd.memset, writes=[Bc], ap=Sbf[:].rearrange("p a b -> p (a b)"), constant=0.0)
        for h in range(4):
            for st in range(5):
                sneg, lnf, Gt, eG, eGn = tmpf2[(h * 5 + st) % 2]; Btmp = Btmp2[(h * 5 + st) % 2]
                n = 512 if st < 4 else 64
                cs = slice(st * 512, st * 512 + n)
                pr_ = (h * 5 + st) % 4
                ps2 = pbank[2 * pr_:2 * pr_ + 2]; Bps2 = Bpbank[2 * pr_:2 * pr_ + 2]
                for (wt, b) in ((wf, 0), (wq, 1)):
                    for kc in range(KC):
                        C.pe.op(nc.tensor.matmul, reads=[Bwh[h // 2]], writes=[Bps2[b]], mark=(kc == KC - 1), out=ps2[b][:, 0:n],
                                lhsT=wt[:, kc, h * 128:(h + 1) * 128], rhs=xnT[:, kc, cs], start=(kc == 0), stop=(kc == KC - 1))
                C.act.op(nc.scalar.activation, reads=[Bps2[0]], writes=[Btmp[0]], out=sneg[:, 0:n], in_=ps2[0][:, 0:n], func=AF.Exp)
                C.dve.op(nc.vector.tensor_scalar, reads=[Btmp[0]], writes=[Btmp[0]], out=sneg[:, 0:n], in0=sneg[:, 0:n], scalar1=1.0,
                         scalar2=None, op0=ALU.add)
                C.dve.op(nc.vector.reciprocal, reads=[Btmp[0]], writes=[Btmp[0]], out=sneg[:, 0:n], in_=sneg[:, 0:n])
                C.act.op(nc.scalar.activation, reads=[Btmp[0], Bc], writes=[Btmp[1]], out=lnf[:, 0:n], in_=sneg[:, 0:n], func=AF.Ln,
                         scale=noml[:, h:h + 1], bias=1.0)
                C.dve.op(nc.vector.tensor_tensor_scan, reads=[Btmp[1], Bc], writes=[Btmp[2]], out=Gt[:, 0:n],
                         data0=(rmask[:, 0:n] if st < 4 else rmask4[:, 0:n]), data1=lnf[:, 0:n], initial=0.0, op0=ALU.mult, op1=ALU.add)
                C.act.op(nc.scalar.activation, reads=[Btmp[2]], writes=[Btmp[3]], out=eG[:, 0:n], in_=Gt[:, 0:n], func=AF.Exp)
                C.act.op(nc.scalar.activation, reads=[Btmp[2]], writes=[Btmp[4]], out=eGn[:, 0:n], in_=Gt[:, 0:n], func=AF.Exp, scale=-1.0)
                C.dve.op(nc.vector.tensor_tensor, reads=[Bps2[1], Btmp[3]], writes=[Bqk], out=qdT[:, h, cs], in0=ps2[1][:, 0:n], in1=eG[:, 0:n],
                         op=ALU.mult)
                C.dve.op(nc.vector.scalar_tensor_tensor, reads=[Btmp[0], Btmp[4], Bc], writes=[Bqk], out=kdT[:, h, cs], in0=sneg[:, 0:n],
                         scalar=oml[:, h:h + 1], in1=eGn[:, 0:n], op0=ALU.mult, op1=ALU.mult)
                if st < 4:
                    C.dve.op(nc.vector.tensor_copy, reads=[Btmp[3]], writes=[Bqk], out=eGl[:, h, st * 8:(st + 1) * 8], in_=eG[:, 63:512:64])
                else:
                    C.dve.op(nc.vector.tensor_copy, reads=[Btmp[3]], writes=[Bqk], out=eGls[:, h, :], in_=eG[:, 3:64:4])
        ps2 = pbank[0:2]; Bps2 = Bpbank[0:2]
        for t in range(16):
            b = t % 2
            for kc in range(KC):
                C.pe.op(nc.tensor.matmul, reads=[Bw], writes=[Bps2[b]], mark=(kc == KC - 1), out=ps2[b][:, :],
                        lhsT=xnT[:, kc, t * 128:(t + 1) * 128], rhs=wib[:, kc, :], start=(kc == 0), stop=(kc == KC - 1))
            if b == 0:
                C.act.op(nc.scalar.copy, reads=[Bps2[b]], writes=[Bv], out=vtok[:, t, :], in_=ps2[b][:, :])
            else:
                C.dve.op(nc.vector.tensor_copy, reads=[Bps2[b]], writes=[Bv], out=vtok[:, t, :], in_=ps2[b][:, :])
        Am = S.sbuf("Am", [128, 4, 64], BF16); BAm = Buf()
        kdtok = S.sbuf("kdtok", [128, 4, 128], BF16); Bkdtok = Buf()
        ob = S.sbuf("ob", [128, 512], BF16); Bob = Buf()
        sq = S.sbuf("sq", [128, 128], BF16); BS = Buf(); BSbf = Buf()
        nrm = S.sbuf("nrm", [128, 16], F32); Bn = Buf()
        stmp = S.sbuf("stmp", [128, 4, 128], F32)
        psOo_r = [psOo, pbank[0][:, :].rearrange("p (h t) -> p h t", h=4)]; BpsOo_r = [BpsOo, Bpbank[0]]
        psX2_r = [psX2, pbank[1][:, 0:256].bitcast(BF16).rearrange("p (h t) -> p h t", h=4)]; BpsX2_r = [BpsX2, Bpbank[1]]
        for t in range(16):
            psOo = psOo_r[t % 2]; BpsOo = BpsOo_r[t % 2]; psX2 = psX2_r[t % 2]; BpsX2 = BpsX2_r[t % 2]
            for h in range(4):
                for c in range(2):
                    cs = slice(t * 128 + c * 64, t * 128 + c * 64 + 64)
                    C.pe.op(nc.tensor.matmul, reads=[Bqk], writes=[BpsA], mark=(h == 3 and c == 1), out=psA[c * 64:(c + 1) * 64, h, :],
                            lhsT=kdT[:, h, cs], rhs=qdT[:, h, cs], start=True, stop=True)
                C.pe.op(nc.tensor.transpose, reads=[Bqk, Bid], writes=[BpsK], mark=(h == 3), out=psK[:, h, :],
                        in_=kdT[:, h, t * 128:(t + 1) * 128], identity=identb[:, :])
            C.dve.op(nc.vector.tensor_tensor, reads=[BpsA, Bc], writes=[BAm], out=Am[:], in0=psA[:],
                     in1=bass.AP(tri[:].tensor, tri[:].offset, [list(tri[:].ap[0]), [0, 4], [1, 64]]), op=ALU.mult)
            C.act.op(nc.scalar.copy, reads=[BpsK], writes=[Bkdtok], out=kdtok[:], in_=psK[:])
            for c in range(2):
                rs_ = slice(c * 64, (c + 1) * 64)
                d = psD[c]
                for h in range(4):
                    cs = slice(t * 128 + c * 64, t * 128 + c * 64 + 64)
                    C.pe.op(nc.tensor.matmul, reads=[Bqk, BSbf], writes=[BpsOo], mark=False, out=psOo[rs_, h, :], lhsT=qdT[:, h, cs],
                            rhs=Sbf[:, h, :], start=True, stop=False)
                    C.pe.op(nc.tensor.matmul, reads=[BAm, Bv], writes=[BpsOo], mark=(c == 1 and h == 3), out=psOo[rs_, h, :],
                            lhsT=Am[rs_, h, :], rhs=vtok[rs_, t, h * 128:(h + 1) * 128], start=False, stop=True)
                    C.pe.op(nc.tensor.matmul, reads=[Bkdtok, Bv], writes=[BpsD[c]], mark=(h == 3), out=d[:, h, :], lhsT=kdtok[rs_, h, :],
                            rhs=vtok[rs_, t, h * 128:(h + 1) * 128], start=True, stop=True)
                C.dve.op(nc.vector.tensor_tensor, reads=[BpsD[c], BS], writes=[BS], out=stmp[:], in0=d[:], in1=Sst[:], op=ALU.add)
                eg = eGl[:, :, 2 * t + c]
                C.dve.op(nc.vector.tensor_tensor, reads=[BS, Bqk], writes=[BS], out=Sst[:], in0=stmp[:],
                         in1=bass.AP(eg.tensor, eg.offset, [list(eg.ap[0]), list(eg.ap[1]), [0, 128]]), op=ALU.mult)
                C.act.op(nc.scalar.copy, reads=[BS], writes=[BSbf], out=Sbf[:], in_=Sst[:])
            for h in range(4):
                C.act.op(nc.scalar.activation, reads=[BpsOo], writes=[Bn], out=sq[:], in_=psOo[:, h, :], func=AF.Square,
                         accum_out=nrm[:, h:h + 1])
            C.act.op(nc.scalar.activation, reads=[Bn], writes=[Bn], out=nrm[:, 4:8], in_=nrm[:, 0:4], func=AF.Sqrt, scale=1.0 / 128, bias=EPS)
            C.dve.op(nc.vector.reciprocal, reads=[Bn], writes=[Bn], out=nrm[:, 8:12], in_=nrm[:, 4:8])
            for h in range(4):
                C.dve.op(nc.vector.scalar_tensor_tensor, reads=[BpsOo, Bn, Bc], writes=[Bob], out=ob[:, h * 128:(h + 1) * 128],
                         in0=psOo[:, h, :], scalar=nrm[:, 8 + h:9 + h], in1=hgbc[:, h * 128:(h + 1) * 128], op0=ALU.mult, op1=ALU.mult)
            for j in range(4):
                C.pe.op(nc.tensor.transpose, reads=[Bob, Bid], writes=[BpsX2], mark=(j == 3), out=psX2[:, j, :],
                        in_=ob[:, j * 128:(j + 1) * 128], identity=identb[:, :])
            C.act.op(nc.scalar.copy, reads=[BpsX2], writes=[BoBT], out=oBT[:, :, t * 128:(t + 1) * 128], in_=psX2[:, :, :])
        C.dma(C.sp, sl_misc, o_phg.rearrange("h d e -> d h e"), Sst[:], reads=[BS])
        psOo = psOo_r[0]; BpsOo = BpsOo_r[0]; psX2 = psX2_r[0]; BpsX2 = BpsX2_r[0]
        S0 = [S.sbuf("S0_%d" % i, [128, 4, 128], F32) for i in range(2)]; BS0 = [Buf(), Buf()]
        S0b2 = [S.sbuf("S0b%d" % i, [128, 4, 128], BF16) for i in range(2)]; BS0b2 = [Buf(), Buf()]
        Sn = [S.sbuf("Sn%d" % i, [128, 4, 128], F32) for i in range(2)]; BSn = [Buf(), Buf()]
        Am4 = S.sbuf("Am4", [4, 4, 4], BF16); kd4 = S.sbuf("kd4", [4, 4, 128], BF16); BA4 = Buf(); Bk4 = Buf()
        oTs = S.sbuf("oTs", [128, 4, 64], F32); BoTs = Buf()
        onesb = S.sbuf("onesb", [128, 128], BF16)
        sqs = S.sbuf("sqs", [128, 256], BF16); rss = S.sbuf("rss", [128, 256], F32)
        C.pool.op(nc.gpsimd.memset, writes=[Bc], ap=onesb[:], constant=1.0)
        sl_s0 = [C.slot("s0a"), C.slot("s0b")]; sl_sn = [C.slot("sna"), C.slot("snb")]
        for bb in range(16):
            s = bb % 2
            cs = slice(2048 + 4 * bb, 2048 + 4 * bb + 4)
            S0b = S0b2[s]; BS0b = BS0b2[s]
            if bb == 0:
                C.dma(C.sp, sl_s0[0], S0[0][:], st_hg[0].rearrange("h d e -> d h e"), writes=[BS0[0]])
            if bb + 1 < 16:
                C.dma(C.sp, sl_s0[1 - s], S0[1 - s][:], st_hg[bb + 1].rearrange("h d e -> d h e"), writes=[BS0[1 - s]])
            for kc in range(KC):
                C.pe.op(nc.tensor.matmul, reads=[Bw], writes=[Bps2[s]], mark=(kc == KC - 1), out=ps2[s][0:4, :],
                        lhsT=xnT[:, kc, 2048 + 4 * bb:2048 + 4 * bb + 4], rhs=wib[:, kc, :], start=(kc == 0), stop=(kc == KC - 1))
            C.act.op(nc.scalar.copy, reads=[Bps2[s]], writes=[BvS[s]], out=vS[s][0:4, :], in_=ps2[s][0:4, :])
            C.act.op(nc.scalar.copy, reads=[BS0[s]], writes=[BS0b], out=S0b[:], in_=S0[s][:])
            for h in range(4):
                C.pe.op(nc.tensor.matmul, reads=[Bqk], writes=[BpsA], mark=(h == 3), out=psA[0:4, h, 0:4], lhsT=kdT[:, h, cs],
                        rhs=qdT[:, h, cs], start=True, stop=True)
                C.pe.op(nc.tensor.transpose, reads=[Bqk, Bid], writes=[BpsK], mark=(h == 3), out=psK[0:4, h, :], in_=kdT[:, h, cs],
                        identity=identb[:, :])
            C.dve.op(nc.vector.tensor_tensor, reads=[BpsA, Bc], writes=[BA4], out=Am4[:], in0=psA[0:4, :, 0:4],
                     in1=bass.AP(tri[0:4, 0:4].tensor, tri[0:4, 0:4].offset, [list(tri[0:4, 0:4].ap[0]), [0, 4], [1, 4]]), op=ALU.mult)
            C.act.op(nc.scalar.copy, reads=[BpsK], writes=[Bk4], out=kd4[:], in_=psK[0:4, :, :])
            for h in range(4):
                C.pe.op(nc.tensor.matmul, reads=[Bqk, BS0b], writes=[BpsOo], mark=False, out=psOo[:, h, 0:4], lhsT=S0b[:, h, :],
                        rhs=qdT[:, h, cs], start=True, stop=False)
                C.pe.op(nc.tensor.matmul, reads=[BA4, BvS[s], BS0b], writes=[BpsOo], mark=(h == 3), out=psOo[:, h, 0:4],
                        lhsT=vS[s][0:4, h * 128:(h + 1) * 128], rhs=Am4[0:4, h, :], start=False, stop=True)
                C.pe.op(nc.tensor.matmul, reads=[Bk4, BvS[s]], writes=[BpsD[s]], mark=(h == 3), out=psD[s][:, h, :], lhsT=kd4[0:4, h, :],
                        rhs=vS[s][0:4, h * 128:(h + 1) * 128], start=True, stop=True)
            C.dve.op(nc.vector.tensor_tensor, reads=[BpsD[s], BS0[s]], writes=[BS], out=stmp[:], in0=psD[s][:], in1=S0[s][:], op=ALU.add)
            eg = eGls[:, :, bb]
            C.dve.op(nc.vector.tensor_tensor, reads=[BS, Bqk], writes=[BSn[s]], out=Sn[s][:], in0=stmp[:],
                     in1=bass.AP(eg.tensor, eg.offset, [list(eg.ap[0]), list(eg.ap[1]), [0, 128]]), op=ALU.mult)
            C.dma(C.sp, sl_sn[s], o_shg[bb].rearrange("h d e -> d h e"), Sn[s][:], reads=[BSn[s]])
            C.act.op(nc.scalar.copy, reads=[BpsOo], writes=[BoTs], out=oTs[:, :, 4 * bb:4 * bb + 4], in_=psOo[:, :, 0:4])
        C.act.op(nc.scalar.activation, reads=[BoTs], writes=[Bn], out=sqs[:], in_=oTs[:].rearrange("p a b -> p (a b)"), func=AF.Square)
        C.pe.op(nc.tensor.matmul, reads=[Bn, Bc], writes=[Bps2[0]], out=ps2[0][:, 0:256], lhsT=onesb[:, :], rhs=sqs[:, :], start=True, stop=True)
        C.act.op(nc.scalar.activation, reads=[Bps2[0]], writes=[Bn], out=rss[:], in_=ps2[0][:, 0:256], func=AF.Sqrt, scale=1.0 / 128, bias=EPS)
        C.dve.op(nc.vector.reciprocal, reads=[Bn], writes=[Bn], out=rss[:], in_=rss[:])
        C.dve.op(nc.vector.tensor_tensor, reads=[Bn, BoTs], writes=[BoTs], out=oTs[:].rearrange("p a b -> p (a b)"),
                 in0=oTs[:].rearrange("p a b -> p (a b)"), in1=rss[:], op=ALU.mult)
        for h in range(4):
            C.dve.op(nc.vector.tensor_scalar, reads=[BoTs, Bc], writes=[BoBT], out=oBT[:, h, 2048:NTOK], in0=oTs[:, h, :],
                     scalar1=hgcol[:, h:h + 1], scalar2=None, op0=ALU.mult)
        S.close()

    if "F" in phases:
        S = C.scope()
        y2T = S.sbuf("y2T", [128, KC, NTOK], BF16)
        wba = S.sbuf("wba", [128, 4, D], BF16); wbb = S.sbuf("wbb", [128, 4, D], BF16)
        wo = S.sbuf("wo", [128, KC, D], BF16)
        wch = [S.sbuf("wchF%d" % i, [128, KC, 256], BF16) for i in range(3)]; Bwch = [Buf() for _ in range(3)]
        fgb = S.sbuf("fgb", [128, D], F32)
        sg = [S.sbuf("sg%d" % i, [128, 512], F32) for i in range(2)]; Bsg = [Buf(), Buf()]
        t1 = S.sbuf("t1", [128, 512], F32); Bt1 = Buf()
        xr = [S.sbuf("xr%d" % i, [128, D], F32) for i in range(3)]; Bxr = [Buf() for _ in range(3)]
        hh = [S.sbuf("hh%d" % i, [128, D], F32) for i in range(2)]; Bhh = [Buf(), Buf()]
        junkf = S.sbuf("junkf", [128, D], BF16); Bj = Buf()
        nf = S.sbuf("nf", [128, NT, 4], F32)
        ps = [S.psum("psF%d" % i, [128, 512], F32) for i in range(6)]; Bps = [Buf() for _ in range(6)]
        Bw = Buf(); By = Buf(); Bo = Buf()
        sl_xr = [C.slot("xr%d" % i) for i in range(3)]; sl_y = [C.slot("y%d" % i) for i in range(3)]
        C.dma(C.sp, sl_misc, fgb[:], dram_bcast(final_g, 128), writes=[Bw])
        wi = [0]

        def next_w(c0, n):
            s = wi[0] % 3; wi[0] += 1
            load_w(wch[s], Bwch[s], w_in[:, c0:c0 + n], n)
            return wch[s], Bwch[s]
        pr = [0]
        for (c0, oT) in ((C_ZA, oAT), (C_ZB, oBT)):
            for piece in range(2):
                wt, Bwt = next_w(c0 + piece * 256, 256)
                for jj in range(2):
                    j = piece * 2 + jj
                    for st in range(5):
                        n = 512 if st < 4 else 64
                        cs = slice(st * 512, st * 512 + n)
                        b = pr[0] % 6; pr[0] += 1
                        for kc in range(KC):
                            C.pe.op(nc.tensor.matmul, reads=[Bwt], writes=[Bps[b]], mark=(kc == KC - 1), out=ps[b][:, 0:n],
                                    lhsT=wt[:, kc, jj * 128:(jj + 1) * 128], rhs=xnT[:, kc, cs], start=(kc == 0), stop=(kc == KC - 1))
                        s = b % 2
                        C.act.op(nc.scalar.activation, reads=[Bps[b]], writes=[Bsg[s]], out=sg[s][:, 0:n], in_=ps[b][:, 0:n], func=AF.Silu)
                        C.dve.op(nc.vector.tensor_tensor, reads=[Bsg[s], Bo], writes=[Bo], out=oT[:, j, cs], in0=oT[:, j, cs], in1=sg[s][:, 0:n],
                                 op=ALU.mult)
        load_w(wba, Bw, w_ba, D, kcn=4)
        load_w(wbb, Bw, w_bb, D, kcn=4)
        load_w(wo, Bw, w_out, D)
        for piece in range(4):
            wtA, BwA = next_w(C_MA + piece * 256, 256)
            wtB, BwB = next_w(C_MB + piece * 256, 256)
            for jj in range(2):
                cc = piece * 2 + jj
                for st in range(5):
                    n = 512 if st < 4 else 64
                    cs = slice(st * 512, st * 512 + n)
                    bA, bB, bMA, bMB = [(pr[0] + i) % 6 for i in range(4)]; pr[0] += 4
                    for (b, wsrc, oT) in ((bA, wba, oAT), (bB, wbb, oBT)):
                        for k in range(4):
                            C.pe.op(nc.tensor.matmul, reads=[Bw, Bo], writes=[Bps[b]], mark=(k == 3), out=ps[b][:, 0:n],
                                    lhsT=wsrc[:, k, cc * 128:(cc + 1) * 128], rhs=oT[:, k, cs], start=(k == 0), stop=(k == 3))
                    for (b, wt, Bwt) in ((bMA, wtA, BwA), (bMB, wtB, BwB)):
                        for kc in range(KC):
                            C.pe.op(nc.tensor.matmul, reads=[Bwt], writes=[Bps[b]], mark=(kc == KC - 1), out=ps[b][:, 0:n],
                                    lhsT=wt[:, kc, jj * 128:(jj + 1) * 128], rhs=xnT[:, kc, cs], start=(kc == 0), stop=(kc == KC - 1))
                    for (i, bm) in enumerate((bMA, bMB)):
                        C.act.op(nc.scalar.activation, reads=[Bps[bm]], writes=[Bsg[i]], out=sg[i][:, 0:n], in_=ps[bm][:, 0:n], func=AF.Sigmoid)
                    C.dve.op(nc.vector.tensor_tensor, reads=[Bsg[0], Bps[bA]], writes=[Bt1], out=t1[:, 0:n], in0=sg[0][:, 0:n], in1=ps[bA][:, 0:n],
                             op=ALU.mult)
                    C.dve.op(nc.vector.tensor_tensor, reads=[Bsg[1], Bps[bB]], writes=[Bsg[1]], out=sg[1][:, 0:n], in0=sg[1][:, 0:n],
                             in1=ps[bB][:, 0:n], op=ALU.mult)
                    C.dve.op(nc.vector.tensor_tensor, reads=[Bsg[1], Bt1], writes=[By], out=y2T[:, cc, cs], in0=sg[1][:, 0:n], in1=t1[:, 0:n],
                             op=ALU.add)
        def xload(t):
            C.dma(C.sp, sl_xr[t % 3], xr[t % 3][0:rows(t), :], x_all[t * 128:t * 128 + rows(t), :], writes=[Bxr[t % 3]])
        xload(0); xload(1)
        for t in range(NT):
            r = rows(t); s = t % 2
            if t + 2 < NT:
                xload(t + 2)
            b0 = pr[0] % 6; b1 = (pr[0] + 1) % 6; pr[0] += 2
            for (b, half) in ((b0, 0), (b1, 1)):
                for c2 in range(KC):
                    C.pe.op(nc.tensor.matmul, reads=[By, Bw], writes=[Bps[b]], mark=(c2 == KC - 1), out=ps[b][0:r, :],
                            lhsT=y2T[:, c2, t * 128:t * 128 + r], rhs=wo[:, c2, half * 512:(half + 1) * 512], start=(c2 == 0), stop=(c2 == KC - 1))
                C.dve.op(nc.vector.tensor_tensor, reads=[Bps[b], Bxr[t % 3]], writes=[Bhh[s]], out=hh[s][0:r, half * 512:(half + 1) * 512],
                         in0=ps[b][0:r, :], in1=xr[t % 3][0:r, half * 512:(half + 1) * 512], op=ALU.add)
            C.act.op(nc.scalar.activation, reads=[Bhh[s]], writes=[Bj], out=junkf[0:r, :], in_=hh[s][0:r, :], func=AF.Square,
                     accum_out=nf[0:r, t, 0:1])
            C.act.op(nc.scalar.activation, reads=[Bj], writes=[Bj], out=nf[0:r, t, 1:2], in_=nf[0:r, t, 0:1], func=AF.Sqrt, scale=1.0 / D, bias=EPS)
            C.dve.op(nc.vector.reciprocal, reads=[Bj], writes=[Bj], out=nf[0:r, t, 2:3], in_=nf[0:r, t, 1:2])
            C.dve.op(nc.vector.scalar_tensor_tensor, reads=[Bhh[s], Bj, Bw], writes=[Bxr[t % 3]], out=xr[t % 3][0:r, :], in0=hh[s][0:r, :],
                     scalar=nf[0:r, t, 2:3], in1=fgb[0:r, :], op0=ALU.mult, op1=ALU.mult)
            C.dma(C.sp, sl_y[t % 3], o_y[t * 128:t * 128 + r, :], xr[t % 3][0:r, :], reads=[Bxr[t % 3]])
        S.close()

    C.finish()
    return nc


def _t5_bucket(rel):
    n = np.maximum(rel, 0)
    nf = np.maximum(n, 1).astype(np.float32)
    large = 16 + (np.log(nf / np.float32(16)) / np.float32(np.log(np.float32(8.0))) * np.float32(16)).astype(np.int32)
    large = np.minimum(large, 31)
    return np.where(n < 16, n, large)


def make_consts():
    c = {}
    ohx = np.zeros((33, TW), np.float32)
    for m in range(TW):
        rel = m - TOFF
        if rel >= 0:
            ohx[int(_t5_bucket(np.array(rel))), m] += 1.0
            ohx[31, m] -= 1.0
        else:
            ohx[32, m] = NEG
    c["c_ohx"] = ohx
    cs = np.arange(127) * 16; ce = cs + 32
    bs = np.arange(32) * 64; be = bs + 64
    ov = np.clip(np.minimum(ce[:, None], be[None, :]) - np.maximum(cs[:, None], bs[None, :]), 0, None) / 16
    c["c_cover"] = ov.astype(np.float32)
    selc = np.zeros((128, 2, 16, 32), np.float32)
    for qt in range(16):
        t = qt * 128 + np.arange(128)[:, None]
        j = np.arange(32)[None, :]
        valid = (j * 64) <= t
        cur = t // 64
        forced = (j == 0) | (j == cur) | (j == cur - 1)
        selc[:, 0, qt, :] = (valid & ~forced).astype(np.float32)
        selc[:, 1, qt, :] = np.where(valid, np.where(forced, 1e9, 0.0), -1.0)
    c["c_selc"] = selc
    p = np.arange(128)[:, None] % 64
    c["c_tri"] = (p <= np.arange(64)[None, :]).astype(np.float32)
    c["c_pm8"] = (np.arange(128) % 8).astype(np.float32).reshape(128, 1)
    return c


_NC_CACHE = {}
PHASES = ("A", "B", "T", "C", "D", "E", "F")


def kernel(**inp):
    f = lambda k: np.ascontiguousarray(np.asarray(inp[k]))
    xp = f("x_prompt"); xsm = f("x_sample")
    consts = make_consts()
    if "nc" not in _NC_CACHE:
        _NC_CACHE["nc"] = build(PHASES)
    nc = _NC_CACHE["nc"]
    shared = {
        "w_in": f("w_in")[0], "norm_g": f("norm_g"), "final_g": f("final_g").reshape(1, D), "rel_bias": f("rel_bias"),
        "pos_k": f("cmp_pos_k")[0], "pos_v": f("cmp_pos_v")[0], "w1_k": f("cmp_w1_k")[0], "w1_v": f("cmp_w1_v")[0],
        "w2_k": f("cmp_w2_k")[0], "w2_v": f("cmp_w2_v")[0], "hg_lower": f("hg_lower"), "hg_norm_g": f("hg_norm_g"),
        "w_ba": f("w_branch_a")[0], "w_bb": f("w_branch_b")[0], "w_out": f("w_out")[0],
        "c_ck": f("cache_cmp_k").reshape(2560 * 8, 2048), "c_cv": f("cache_cmp_v").reshape(2560 * 8, 2048),
        "c_sk": f("cache_sel_k").reshape(2560 * 8, 2048), "c_sv": f("cache_sel_v").reshape(2560 * 8, 2048),
    }
    shared.update(consts)
    swk = f("state_win_k")[0].reshape(128, 512, 128); swv = f("state_win_v")[0].reshape(128, 512, 128)
    shg = f("state_hgrn")[0]
    pt = f("page_table").astype(np.int32)
    in_maps = []
    for c in range(8):
        m = dict(shared)
        m["x_all"] = np.concatenate([xp[c], xsm[16 * c:16 * c + 16].reshape(64, D)], axis=0)
        m["st_wk"] = swk[16 * c:16 * c + 16]; m["st_wv"] = swv[16 * c:16 * c + 16]
        m["st_hg"] = shg[16 * c:16 * c + 16]
        m["ptab"] = pt[16 * c:16 * c + 16].reshape(1, 256)
        in_maps.append(m)
    res = run_bass_kernel_spmd(nc, in_maps, core_ids=list(range(8)))
    R = res.results
    y_p = np.stack([R[c]["o_y"][0:2048] for c in range(8)], 0)
    y_s = np.concatenate([R[c]["o_y"][2048:].reshape(16, 4, D) for c in range(8)], 0)
    outs = [y_p, y_s]
    for j in range(4):
        outs.append(np.stack([R[c]["o_pkv"][j].reshape(2048, 2, 64) for c in range(8)], 0)[None])
    for j in range(2):
        outs.append(np.stack([R[c]["o_pwin"][j].reshape(512, 2, 64) for c in range(8)], 0)[None])
    outs.append(np.stack([R[c]["o_phg"] for c in range(8)], 0)[None])
    for j in range(4):
        outs.append(np.concatenate([R[c]["o_skv"][j].reshape(16, 4, 2, 64) for c in range(8)], 0)[None])
    for j in range(2):
        outs.append(np.concatenate([R[c]["o_swin"][j].reshape(16, 512, 2, 64) for c in range(8)], 0)[None])
    outs.append(np.concatenate([R[c]["o_shg"] for c in range(8)], 0)[None])
    return tuple(np.ascontiguousarray(o.astype(np.float32)) for o in outs)
```

```python
import numpy as np
import contextlib
import concourse.bass as bass
import concourse.mybir as mybir
from concourse.bass_utils import run_bass_kernel_spmd

F32 = mybir.dt.float32; BF16 = mybir.dt.bfloat16; I32 = mybir.dt.int32
AF = mybir.ActivationFunctionType
ALU = mybir.AluOpType
AX = mybir.AxisListType

SAME_ENGINE_SYNC = True
NEG = -30000.0
CAST_SPLIT = True
NTOK = 2112
NT = 17
D = 1024
KC = 8
EPS = 1e-6
C_Q = 0; C_KV = 512; C_G = 1280; C_ZA = 1304; C_QB = 1816; C_FB = 2328; C_IB = 2840; C_ZB = 3352; C_MA = 3864; C_MB = 4888
NCOL = 5912
TW = 768
TOFF = 160


class Tok:
    __slots__ = ("sem", "val", "eng", "slot")

    def __init__(self, sem, val, eng=None, slot=None):
        self.sem = sem; self.val = val; self.eng = eng; self.slot = slot


class Buf:
    __slots__ = ("w", "r", "name")

    def __init__(self, name=""):
        self.w = None; self.r = []; self.name = name


class Eng:
    def __init__(self, ctx, name, eng, is_pe=False):
        self.ctx = ctx; self.name = name; self.e = eng
        self.sem = ctx.new_sem("e_" + name)
        self.cnt = 0
        self.seen = {}
        self.is_pe = is_pe
        self.ninstr = 0

    def wait(self, tok):
        if tok is None:
            return
        if tok.eng is self:
            if self.is_pe or not SAME_ENGINE_SYNC:
                return
        key = id(tok.sem)
        val = tok.val
        if tok.slot is not None:
            val = tok.slot.cnt
            tok.slot.waited_max = max(tok.slot.waited_max, val)
        if self.seen.get(key, 0) >= val:
            return
        self.e.wait_ge(tok.sem, val)
        self.seen[key] = val

    def wait_many(self, toks):
        best = {}
        for t in toks:
            if t is None:
                continue
            k = id(t.sem)
            if k not in best or best[k].val < t.val:
                best[k] = t
        for t in best.values():
            self.wait(t)

    def deps(self, reads, writes):
        toks = [b.w for b in reads]
        for b in writes:
            toks.append(b.w)
            toks.extend(b.r)
        self.wait_many(toks)

    def op(self, fn, reads=(), writes=(), mark=True, **kw):
        self.deps(reads, writes)
        ins = fn(**kw)
        self.ninstr += 1
        if not mark:
            return None
        self.cnt += 1
        ins.then_inc(self.sem, 1)
        tok = Tok(self.sem, self.cnt, self)
        for b in reads:
            b.r.append(tok)
            if len(b.r) > 16:
                b.r = b.r[-16:]
        for b in writes:
            b.w = tok; b.r = []
        return tok


class DmaSlot:
    def __init__(self, ctx, name):
        self.sem = ctx.new_sem("d_" + name); self.cnt = 0; self.waited_max = 0

    def pre_issue(self, q):
        if self.waited_max:
            q.wait(Tok(self.sem, self.waited_max, None, self))

    def issued(self, ins):
        self.cnt += 16
        ins.then_inc(self.sem, 16)
        return Tok(self.sem, self.cnt, None, self)


class Scope:
    def __init__(self, C):
        self.C = C; self.es = contextlib.ExitStack()

    def sbuf(self, name, shape, dt):
        return self.es.enter_context(self.C.nc.sbuf_tensor(name, list(shape), dt))

    def psum(self, name, shape, dt=F32):
        return self.es.enter_context(self.C.nc.psum_tensor(name, list(shape), dt))

    def close(self):
        self.C.barrier()
        self.es.close()


class Ctx:
    def __init__(self, nc):
        self.nc = nc
        self.es = contextlib.ExitStack()
        self.pe = Eng(self, "pe", nc.tensor, is_pe=True)
        self.act = Eng(self, "act", nc.scalar)
        self.dve = Eng(self, "dve", nc.vector)
        self.pool = Eng(self, "pool", nc.gpsimd)
        self.sp = Eng(self, "sp", nc.sync)
        self.engs = (self.pe, self.act, self.dve, self.pool, self.sp)
        self.slots = []
        self.ndma = 0

    def new_sem(self, name):
        return self.es.enter_context(self.nc.semaphore(name))

    def sbuf(self, name, shape, dt):
        return self.es.enter_context(self.nc.sbuf_tensor(name, list(shape), dt))

    def scope(self):
        return Scope(self)

    def slot(self, name):
        s = DmaSlot(self, name); self.slots.append(s); return s

    def dma(self, q, slot, out, in_, reads=(), writes=(), **kw):
        q.deps(reads, writes)
        slot.pre_issue(q)
        ins = q.e.dma_start(out=out, in_=in_, **kw)
        self.ndma += 1
        tok = slot.issued(ins)
        for b in reads:
            b.r.append(tok)
            if len(b.r) > 16:
                b.r = b.r[-16:]
        for b in writes:
            b.w = tok; b.r = []
        return tok

    def barrier(self):
        for e in self.engs:
            for f in self.engs:
                if f is not e and f.cnt and f is not self.sp:
                    e.wait(Tok(f.sem, f.cnt, f))
            for s in self.slots:
                if s.cnt:
                    e.wait(Tok(s.sem, s.cnt, None, s))

    def finish(self):
        self.barrier()
        self.es.close()


def sub(ap, p0, p1):
    return ap[p0:p1]


def rows(t):
    return 128 if t < 16 else 64


def dram_bcast(ap2d, nparts):
    n = ap2d.shape[-1]
    return bass.AP(ap2d.tensor, ap2d.offset, [[0, nparts], [1, n]])


def build(phases=("A", "B", "T", "C", "D", "E", "F")):
    nc = bass.Bass("TRN2", target_bir_lowering=False)
    C = Ctx(nc)

    def din(name, shape, dt=F32):
        return nc.dram_tensor(name, list(shape), dt, kind="ExternalInput").ap()

    def dout(name, shape):
        return nc.dram_tensor(name, list(shape), F32, kind="ExternalOutput").ap()

    def dscr(name, shape, dt=F32):
        return nc.dram_tensor(name, list(shape), dt, kind="Internal").ap()

    x_all = din("x_all", [NTOK, D])
    w_in = din("w_in", [D, NCOL])
    norm_g = din("norm_g", [1, D])
    final_g = din("final_g", [1, D])
    rel_bias = din("rel_bias", [32, 8])
    pos_kv = [din("pos_k", [32, 64]), din("pos_v", [32, 64])]
    w1_kv = [din("w1_k", [2048, 64]), din("w1_v", [2048, 64])]
    w2_kv = [din("w2_k", [64, 64]), din("w2_v", [64, 64])]
    hg_lower = din("hg_lower", [2, 512])
    hg_norm_g = din("hg_norm_g", [1, 512])
    w_ba = din("w_ba", [512, D])
    w_bb = din("w_bb", [512, D])
    w_out = din("w_out", [D, D])
    caches = [din(n, [2560 * 8, 2048]) for n in ("c_ck", "c_cv", "c_sk", "c_sv")]
    st_win = [din("st_wk", [16, 512, 128]), din("st_wv", [16, 512, 128])]
    st_hg = din("st_hg", [16, 4, 128, 128])
    ptab = din("ptab", [1, 256], I32)
    c_ohx = din("c_ohx", [33, TW])
    c_cover = din("c_cover", [127, 32])
    c_selc = din("c_selc", [128, 2, 16, 32])
    c_tri = din("c_tri", [128, 64])
    c_pm8 = din("c_pm8", [128, 1])

    o_y = dout("o_y", [NTOK, D])
    o_pkv = dout("o_pkv", [6, 2048, 128])
    o_pwin = dout("o_pwin", [2, 512, 128])
    o_phg = dout("o_phg", [4, 128, 128])
    o_skv = dout("o_skv", [6, 64, 128])
    o_swin = dout("o_swin", [2, 16, 512, 128])
    o_shg = dout("o_shg", [16, 4, 128, 128])

    xnT = C.sbuf("xnT", [128, KC, NTOK], BF16)
    identb = C.sbuf("identb", [128, 128], BF16)
    identf = C.sbuf("identf", [128, 128], F32)
    wst = [C.sbuf("wst%d" % i, [128, KC, 256], F32) for i in range(2)]
    Bwst = [Buf() for _ in range(2)]
    sl_w = [C.slot("w%d" % i) for i in range(2)]
    sl_misc = C.slot("misc")
    sl_st = [C.slot("st%d" % i) for i in range(2)]
    wring = [0]

    Bid = Buf()
    C.pool.op(nc.gpsimd.memset, writes=[Bid], ap=identf[:], constant=1.0)
    C.pool.op(nc.gpsimd.affine_select, reads=[Bid], writes=[Bid], out=identf[:], in_=identf[:],
              pattern=[[-1, 128]], base=0, channel_multiplier=1, compare_op=ALU.is_equal, fill=0.0)
    C.dve.op(nc.vector.tensor_copy, reads=[Bid], writes=[Bid], out=identb[:], in_=identf[:])

    def load_w(dst, Bdst, src2d, ncols, kcn=KC, cast_eng=None):
        for c0 in range(0, ncols, 256):
            n = min(256, ncols - c0)
            s = wring[0] % 2; wring[0] += 1
            C.dma(C.sp, sl_w[s], wst[s][:, 0:kcn, 0:n],
                  src2d[:, c0:c0 + n].rearrange("(k p) c -> p k c", p=128), writes=[Bwst[s]])
            if cast_eng is None and s == 1 and CAST_SPLIT:
                C.act.op(nc.scalar.copy, reads=[Bwst[s]], writes=[Bdst], out=dst[:, 0:kcn, c0:c0 + n], in_=wst[s][:, 0:kcn, 0:n])
            else:
                e = cast_eng or C.pool
                e.op(e.e.tensor_copy, reads=[Bwst[s]], writes=[Bdst], out=dst[:, 0:kcn, c0:c0 + n], in_=wst[s][:, 0:kcn, 0:n])

    oAT = C.sbuf("oAT", [128, 4, NTOK], BF16)
    oBT = C.sbuf("oBT", [128, 4, NTOK], BF16)
    SAB2 = C.scope()
    BT = [[SAB2.sbuf("BT%d%d" % (g, d), [128, 4, 128], BF16) for d in range(2)] for g in range(2)]
    Mcmp = [SAB2.sbuf("Mcmp%d" % g, [16, 4, 128], BF16) for g in range(2)]
    Wm4 = SAB2.sbuf("Wm4", [128, 4, 128], BF16)
    Iw = SAB2.sbuf("Iw", [16, 144], BF16)
    W1bd = [SAB2.sbuf("W1bd%d" % j, [128, 32, 128], BF16) for j in range(2)]
    cvec = SAB2.sbuf("cvec", [128, 2], F32)
    W2k = SAB2.sbuf("W2k", [128, 2, 64], BF16)
    W2v = SAB2.sbuf("W2v", [128, 128], BF16)
    Kcaug = [SAB2.sbuf("Kcaug%d" % g, [97, 128], BF16) for g in range(2)]
    Vc = SAB2.sbuf("Vc", [128, 2, 97], BF16)
    gat = SAB2.sbuf("gat", [128, NT, 24], F32)
    rbcol = SAB2.sbuf("rbcol", [128, 8], F32)
    qS = [SAB2.sbuf("qS%d" % g, [97, 4, 64], BF16) for g in range(2)]
    KsS = [SAB2.sbuf("KsS%d" % g, [97, 64], BF16) for g in range(2)]
    KwS = [SAB2.sbuf("KwS%d" % g, [97, 64], BF16) for g in range(2)]
    gS = SAB2.sbuf("gS", [16, 16, 3, 2], F32)
    _w1flat = wst[1][:, :, :].rearrange("p a b -> p (a b)")
    bspst = _w1flat[:, 0:512].rearrange("p (a b c) -> p a b c", a=16, b=8)
    mcst = _w1flat[:, 512:544].rearrange("p (a b) -> p a b", a=8)
    idxraw = SAB2.sbuf("idxraw", [128, 16], I32); pm8f = SAB2.sbuf("pm8f", [128, 1], F32)
    BDS = Buf()
    sl_ds = C.slot("dset")
    SAB1 = C.scope()
    qaug = [SAB1.sbuf("qaug%d" % g, [97, 4 * NTOK], BF16) for g in range(2)]
    Ksaug = [SAB1.sbuf("Ksaug%d" % g, [97, NTOK], BF16) for g in range(2)]
    Kwaug = [SAB1.sbuf("Kwaug%d" % g, [97, NTOK], BF16) for g in range(2)]
    kcvT = [SAB1.sbuf("kcT", [128, 2048], BF16), SAB1.sbuf("vcT", [128, 2048], BF16)]
    Vs = SAB1.sbuf("Vs", [128, NT, 2, 65], BF16)
    Vw = SAB1.sbuf("Vw", [128, NT, 2, 65], BF16)
    selc = SAB1.sbuf("selc", [128, 2, 16, 32], F32)
    Baug = Buf()

    if "A" in phases:
        S = C.scope()
        gbc = S.sbuf("gbc", [128, D], F32); Bgbc = Buf()
        xs = [S.sbuf("xs%d" % i, [128, D], F32) for i in range(3)]; Bxs = [Buf() for _ in range(3)]
        xnb = [S.sbuf("xnb%d" % i, [128, D], BF16) for i in range(3)]; Bxnb = [Buf() for _ in range(3)]
        junk = S.sbuf("junk", [128, D], BF16); Bjunk = Buf()
        ss = S.sbuf("ss", [128, NT], F32); sd = S.sbuf("sd", [128, NT], F32); rstd = S.sbuf("rstd", [128, NT], F32)
        pT = [S.psum("pT%d" % i, [128, KC, 128], BF16) for i in range(2)]; BpT = [Buf(), Buf()]
        sl_x = [C.slot("x0"), C.slot("x1"), C.slot("x2")]
        C.dma(C.sp, sl_misc, gbc[:], dram_bcast(norm_g, 128), writes=[Bgbc])
        for g in range(2):
            C.pool.op(nc.gpsimd.memset, writes=[Baug], ap=qaug[g][64:96, :], constant=0.0)
            C.pool.op(nc.gpsimd.memset, writes=[Baug], ap=qaug[g][96:97, :], constant=1.0)
            C.pool.op(nc.gpsimd.memset, writes=[Baug], ap=Kwaug[g][64:96, :], constant=0.0)
            C.pool.op(nc.gpsimd.memset, writes=[Baug], ap=Kwaug[g][96:97, :], constant=1.0)
            C.pool.op(nc.gpsimd.memset, writes=[Baug], ap=Ksaug[g][96:97, :], constant=1.0)
            C.pool.op(nc.gpsimd.memset, writes=[Baug], ap=Ksaug[g][64:96, 0:2048], constant=1.0)
            C.pool.op(nc.gpsimd.memset, writes=[Baug], ap=Ksaug[g][64:96, 2048:NTOK], constant=0.0)
            C.pool.op(nc.gpsimd.affine_select, writes=[Baug], out=Ksaug[g][64:96, 0:2048], in_=Ksaug[g][64:96, 0:2048],
                      pattern=[[1, 2048]], base=0, channel_multiplier=-64, compare_op=ALU.is_ge, fill=0.0)
            C.pool.op(nc.gpsimd.affine_select, writes=[Baug], out=Ksaug[g][64:96, 0:2048], in_=Ksaug[g][64:96, 0:2048],
                      pattern=[[-1, 2048]], base=63, channel_multiplier=64, compare_op=ALU.is_ge, fill=0.0)
        C.pool.op(nc.gpsimd.memset, writes=[Baug], ap=Vs[:].rearrange("p a b c -> p (a b c)"), constant=1.0)
        C.pool.op(nc.gpsimd.memset, writes=[Baug], ap=Vw[:].rearrange("p a b c -> p (a b c)"), constant=1.0)
        C.dma(C.sp, sl_misc, rbcol[96:97, :], rel_bias[31:32, :], writes=[Baug])
        for h in range(8):
            g, r = divmod(h, 4)
            C.dve.op(nc.vector.tensor_scalar, reads=[Baug], writes=[Baug], out=qaug[g][96:97, r * NTOK:(r + 1) * NTOK],
                     in0=qaug[g][96:97, r * NTOK:(r + 1) * NTOK], scalar1=rbcol[96:97, h:h + 1], scalar2=None, op0=ALU.mult)
        Bt = [Buf() for _ in range(NT)]
        for t in range(NT):
            r = rows(t); s = t % 2; s3 = t % 3
            C.dma(C.sp, sl_x[s3], xs[s3][0:r, :], x_all[t * 128:t * 128 + r, :], writes=[Bxs[s3]])
            C.act.op(nc.scalar.activation, reads=[Bxs[s3]], writes=[Bjunk, Bt[t]], out=junk[0:r, :], in_=xs[s3][0:r, :],
                     func=AF.Square, accum_out=ss[0:r, t:t + 1])
            C.act.op(nc.scalar.activation, reads=[Bt[t]], writes=[Bt[t]], out=sd[0:r, t:t + 1], in_=ss[0:r, t:t + 1],
                     func=AF.Sqrt, scale=1.0 / D, bias=EPS)
            C.dve.op(nc.vector.reciprocal, reads=[Bt[t]], writes=[Bt[t]], out=rstd[0:r, t:t + 1], in_=sd[0:r, t:t + 1])
            C.dve.op(nc.vector.scalar_tensor_tensor, reads=[Bxs[s3], Bt[t], Bgbc], writes=[Bxnb[s3]], out=xnb[s3][0:r, :],
                     in0=xs[s3][0:r, :], scalar=rstd[0:r, t:t + 1], in1=gbc[0:r, :], op0=ALU.mult, op1=ALU.mult)
            for kc in range(KC):
                C.pe.op(nc.tensor.transpose, reads=[Bxnb[s3], Bid], writes=[BpT[s]], mark=(kc == KC - 1),
                        out=pT[s][:, kc, 0:r], in_=xnb[s3][0:r, kc * 128:(kc + 1) * 128], identity=identb[0:r, 0:r])
            e = C.act if t % 2 == 0 else C.dve
            fn = nc.scalar.copy if t % 2 == 0 else nc.vector.tensor_copy
            e.op(fn, reads=[BpT[s]], writes=[Bt[t]], out=xnT[:, :, t * 128:t * 128 + r], in_=pT[s][:, :, 0:r])
        S.close()

    if "B" in phases:
        S = C.scope()
        wkv = S.sbuf("wkv", [128, KC, 792], BF16); Bwkv = Buf()
        wch = [S.sbuf("wch%d" % i, [128, KC, 256], BF16) for i in range(2)]; Bwch = [Buf(), Buf()]
        stg = [S.sbuf("stg%d" % i, [128, 768], F32) for i in range(2)]; Bstg = [Buf(), Buf()]
        ps = [S.psum("psB%d" % i, [128, 512], F32) for i in range(4)]; Bps = [Buf() for _ in range(4)]
        pring = [0]
        load_w(wkv, Bwkv, w_in[:, C_KV:C_KV + 792], 792)
        Bg = Buf()
        for t in range(NT):
            r = rows(t); s = t % 2
            for gi, (c0, n) in enumerate(((0, 256), (256, 256), (512, 256), (768, 24))):
                b = pring[0] % 4; pring[0] += 1
                for kc in range(KC):
                    C.pe.op(nc.tensor.matmul, reads=[Bwkv], writes=[Bps[b]], mark=(kc == KC - 1), out=ps[b][0:r, 0:n],
                            lhsT=xnT[:, kc, t * 128:t * 128 + r], rhs=wkv[:, kc, c0:c0 + n], start=(kc == 0), stop=(kc == KC - 1))
                if gi < 3:
                    if gi % 2 == 0:
                        C.act.op(nc.scalar.copy, reads=[Bps[b]], writes=[Bstg[s]], out=stg[s][0:r, c0:c0 + n], in_=ps[b][0:r, 0:n])
                    else:
                        C.dve.op(nc.vector.tensor_copy, reads=[Bps[b]], writes=[Bstg[s]], out=stg[s][0:r, c0:c0 + n], in_=ps[b][0:r, 0:n])
                else:
                    C.act.op(nc.scalar.activation, reads=[Bps[b]], writes=[Bg], out=gat[0:r, t, :], in_=ps[b][0:r, 0:24], func=AF.Sigmoid)
            C.pool.op(nc.gpsimd.tensor_copy, reads=[Bstg[s]], writes=[Baug], out=Vs[0:r, t, :, 0:64],
                      in_=stg[s][0:r, 384:512].rearrange("p (g d) -> p g d", g=2))
            C.pool.op(nc.gpsimd.tensor_copy, reads=[Bstg[s]], writes=[Baug], out=Vw[0:r, t, :, 0:64],
                      in_=stg[s][0:r, 640:768].rearrange("p (g d) -> p g d", g=2))
            if t < 16:
                C.dma(C.sp, sl_st[s], o_pkv[:, t * 128:(t + 1) * 128, :].rearrange("j p c -> p j c"),
                      stg[s][:, :].rearrange("p (j c) -> p j c", j=6), reads=[Bstg[s]])
                if t >= 12:
                    C.dma(C.sp, sl_st[s], o_pwin[:, (t - 12) * 128:(t - 11) * 128, :].rearrange("j p c -> p j c"),
                          stg[s][:, 512:768].rearrange("p (j c) -> p j c", j=2), reads=[Bstg[s]])
            else:
                C.dma(C.sp, sl_st[s], o_skv[:, :, :].rearrange("j p c -> p j c"),
                      stg[s][0:64, :].rearrange("p (j c) -> p j c", j=6), reads=[Bstg[s]])
                for j in range(2):
                    for bb in range(16):
                        C.dma(C.sp, sl_st[s], o_swin[j, bb, 508:512, :], stg[s][4 * bb:4 * bb + 4, 512 + 128 * j:640 + 128 * j], reads=[Bstg[s]])
        def featproj(wt, Bw, M, evac):
            for st in range(5):
                n = 512 if st < 4 else 64
                b = pring[0] % 4; pring[0] += 1
                for kc in range(KC):
                    C.pe.op(nc.tensor.matmul, reads=[Bw], writes=[Bps[b]], mark=(kc == KC - 1), out=ps[b][0:M, 0:n],
                            lhsT=wt[:, kc, :], rhs=xnT[:, kc, st * 512:st * 512 + n], start=(kc == 0), stop=(kc == KC - 1))
                evac(st, n, ps[b], Bps[b])
        ev = [0]

        def evac_to(dst_fn, scale=None):
            def f(st, n, p, Bp):
                dst = dst_fn(st, n)
                if dst is None:
                    return
                M = dst.shape[0]
                ev[0] += 1
                if ev[0] % 2 == 0:
                    C.act.op(nc.scalar.activation, reads=[Bp], writes=[Baug], out=dst, in_=p[0:M, 0:n], func=AF.Copy,
                             scale=(scale if scale is not None else 1.0))
                else:
                    C.dve.op(nc.vector.tensor_scalar, reads=[Bp], writes=[Baug], out=dst, in0=p[0:M, 0:n],
                             scalar1=(scale if scale is not None else 1.0), scalar2=None, op0=ALU.mult)
            return f
        wi = [0]

        def next_w(c0, n):
            s = wi[0] % 2; wi[0] += 1
            load_w(wch[s], Bwch[s], w_in[:, c0:c0 + n], n)
            return wch[s], Bwch[s]
        for piece in range(2):
            wt, Bw = next_w(C_Q + piece * 256, 256)
            for hh in range(4):
                h = piece * 4 + hh; g, r = divmod(h, 4)
                featproj(wt[:, :, hh * 64:(hh + 1) * 64], Bw, 64,
                         evac_to(lambda st, n, g=g, r=r: qaug[g][0:64, r * NTOK + st * 512:r * NTOK + st * 512 + n], scale=0.125))
        wt, Bw = next_w(C_KV, 256)
        for j in range(2):
            featproj(wt[:, :, j * 128:(j + 1) * 128], Bw, 128,
                     evac_to(lambda st, n, j=j: (kcvT[j][:, st * 512:st * 512 + n] if st < 4 else None)))
        wt, Bw = next_w(C_KV + 256, 128)
        for g in range(2):
            featproj(wt[:, :, g * 64:(g + 1) * 64], Bw, 64, evac_to(lambda st, n, g=g: Ksaug[g][0:64, st * 512:st * 512 + n]))
        wt, Bw = next_w(C_KV + 512, 128)
        for g in range(2):
            featproj(wt[:, :, g * 64:(g + 1) * 64], Bw, 64, evac_to(lambda st, n, g=g: Kwaug[g][0:64, st * 512:st * 512 + n]))
        S.close()

    G_scr = dscr("G_scr", [8, 128, TW])
    sl_t = C.slot("tbl")

    def toep(h, off, pstep, nparts, n):
        return bass.AP(G_scr.tensor, h * 128 * TW + off, [[pstep, nparts], [1, n]])

    if "T" in phases:
        S = C.scope()
        rbx = S.sbuf("rbx", [33, 8], F32); ohx = S.sbuf("ohx", [33, TW], F32); lh = S.sbuf("lh", [33, 8, 128], F32)
        gb = [S.sbuf("gb%d" % i, [128, TW], F32) for i in range(2)]; Bgb = [Buf(), Buf()]
        tz = wst[1][:, :, :].rearrange("p a (b c) -> p a b c", b=2); tzc = S.sbuf("tzc", [16, 8, 128], F32)
        w1s = wst[0][:, :, :].rearrange("p a (b c) -> p (a b) c", b=4); posT = S.sbuf("posT", [128, 32], F32); posTb = S.sbuf("posTb", [128, 32], BF16)
        w2s = S.sbuf("w2s", [128, 2, 64], F32); cvs = S.sbuf("cvs", [128, 32], F32)
        psT = [S.psum("psT%d" % i, [128, 512], F32) for i in range(2)]; BpsT = [Buf(), Buf()]
        Bt_ = Buf(); Bw = Buf()
        for g in range(2):
            C.pool.op(nc.gpsimd.memset, writes=[Bw], ap=Kcaug[g][64:96, :], constant=0.0)
            C.pool.op(nc.gpsimd.memset, writes=[Bw], ap=Kcaug[g][96:97, :], constant=1.0)
        C.pool.op(nc.gpsimd.memset, writes=[Bw], ap=Vc[:].rearrange("p a b -> p (a b)"), constant=1.0)
        C.pool.op(nc.gpsimd.memset, writes=[Bw], ap=W2k[:].rearrange("p a b -> p (a b)"), constant=0.0)
        C.pool.op(nc.gpsimd.memset, writes=[Bw], ap=W2v[:], constant=0.0)
        Bcs = Buf()
        cov = S.sbuf("cov", [128, 32], F32)
        C.dma(C.sp, sl_t, cov[0:127, :], c_cover, writes=[Bcs])
        for g in range(2):
            C.dve.op(nc.vector.tensor_copy, reads=[Bcs], writes=[Bw], out=Vc[0:127, g, 65:97], in_=cov[0:127, :])
        Bs = Buf()
        for j in range(2):
            for half in range(2):
                C.dma(C.sp, sl_t, w1s[half * 64:(half + 1) * 64, :, :], w1_kv[j].rearrange("(l d) h -> d l h", d=64), writes=[Bs])
                with nc.allow_non_contiguous_dma(reason="tiny transposed load of the compression position table"):
                    C.dma(C.sp, sl_t, posT[half * 64:(half + 1) * 64, :], pos_kv[j].rearrange("l d -> d l"), writes=[Bs])
                C.dma(C.sp, sl_t, w2s[half * 64:(half + 1) * 64, j, :], w2_kv[j], writes=[Bs])
            C.pool.op(nc.gpsimd.memset, writes=[Bw], ap=W1bd[j][:].rearrange("p a b -> p (a b)"), constant=0.0)
            C.dve.op(nc.vector.tensor_copy, reads=[Bs], writes=[Bw], out=W1bd[j][0:64, :, 0:64], in_=w1s[0:64, :, :])
            C.dve.op(nc.vector.tensor_copy, reads=[Bs], writes=[Bw], out=W1bd[j][64:128, :, 64:128], in_=w1s[64:128, :, :])
            C.dve.op(nc.vector.tensor_copy, reads=[Bs], writes=[Bw], out=posTb[:], in_=posT[:])
            if j == 0:
                for g in range(2):
                    C.dve.op(nc.vector.tensor_copy, reads=[Bs], writes=[Bw], out=W2k[g * 64:(g + 1) * 64, g, :], in_=w2s[g * 64:(g + 1) * 64, 0, :])
            else:
                for g in range(2):
                    C.dve.op(nc.vector.tensor_copy, reads=[Bs], writes=[Bw], out=W2v[g * 64:(g + 1) * 64, g * 64:(g + 1) * 64],
                             in_=w2s[g * 64:(g + 1) * 64, 1, :])
            for l in range(32):
                C.pe.op(nc.tensor.matmul, reads=[Bw], writes=[BpsT[0]], mark=(l == 31), out=psT[0][:, 0:1], lhsT=W1bd[j][:, l, :],
                        rhs=posTb[:, l:l + 1], start=(l == 0), stop=(l == 31))
            C.dve.op(nc.vector.tensor_copy, reads=[BpsT[0]], writes=[Bw], out=cvec[:, j:j + 1], in_=psT[0][:, 0:1])
        C.pool.op(nc.gpsimd.memset, writes=[Bt_], ap=rbx[32:33, :], constant=1.0)
        C.dma(C.sp, sl_t, rbx[0:32, :], rel_bias, writes=[Bt_])
        C.dma(C.sp, sl_t, ohx[:], c_ohx, writes=[Bt_])
        C.dma(C.sp, sl_t, selc[:].rearrange("p a b c -> p (a b c)"), c_selc.rearrange("p a b c -> p (a b c)"), writes=[Bt_])
        C.dve.op(nc.vector.tensor_copy, reads=[Bt_], writes=[Bt_], out=lh[:],
                 in_=bass.AP(rbx[:].tensor, rbx[:].offset, [list(rbx[:].ap[0]), [1, 8], [0, 128]]))
        BGs = []
        sl_gb = [C.slot('gb0'), C.slot('gb1')]
        for h in range(8):
            s = h % 2
            for ci, (c0, n) in enumerate(((0, 512), (512, 256))):
                C.pe.op(nc.tensor.matmul, reads=[Bt_], writes=[BpsT[ci]], out=psT[ci][:, 0:n], lhsT=lh[:, h, :], rhs=ohx[:, c0:c0 + n],
                        start=True, stop=True)
                if ci == 0:
                    C.act.op(nc.scalar.copy, reads=[BpsT[ci]], writes=[Bgb[s]], out=gb[s][:, c0:c0 + n], in_=psT[ci][:, 0:n])
                else:
                    C.dve.op(nc.vector.tensor_copy, reads=[BpsT[ci]], writes=[Bgb[s]], out=gb[s][:, c0:c0 + n], in_=psT[ci][:, 0:n])
            BGh = Buf()
            C.dma(C.sp, sl_gb[s], G_scr[h], gb[s][:], reads=[Bgb[s]], writes=[BGh]); BGs.append(BGh)
        Btz = Buf()
        sl_tz = C.slot("tz")
        for h in range(8):
            for d in range(2):
                tok = C.dma(C.sp, sl_tz, tz[:, h, d, :], toep(h, TOFF + 128 * d, TW - 1, 128, 128), reads=BGs)
            tok = C.dma(C.sp, sl_tz, tzc[:, h, :], toep(h, TOFF + 97, TW - 16, 16, 128), reads=BGs)
        Btz.w = tok
        for g in range(2):
            for d in range(2):
                C.dve.op(nc.vector.tensor_copy, reads=[Btz], writes=[Bw], out=BT[g][d][:], in_=tz[:, 4 * g:4 * g + 4, d, :])
            C.dve.op(nc.vector.tensor_copy, reads=[Btz], writes=[Bw], out=Mcmp[g][:], in_=tzc[:, 4 * g:4 * g + 4, :])
        C.pool.op(nc.gpsimd.memset, writes=[Bw], ap=Wm4[:].rearrange("p a b -> p (a b)"), constant=0.0)
        C.pool.op(nc.gpsimd.affine_select, writes=[Bw], out=Wm4[:], in_=Wm4[:], pattern=[[0, 4], [-1, 128]], base=0,
                  channel_multiplier=1, compare_op=ALU.is_gt, fill=NEG)
        C.pool.op(nc.gpsimd.memset, writes=[Bw], ap=Iw[:], constant=1.0)
        C.pool.op(nc.gpsimd.affine_select, writes=[Bw], out=Iw[:], in_=Iw[:], pattern=[[1, 144]], base=-120,
                  channel_multiplier=-1, compare_op=ALU.is_equal, fill=0.0)
        S.close()

    gs_scr = dscr("gs_scr", [4, 64, 6])
    if "D" in phases:
        Bm0 = Buf()
        C.pool.op(nc.gpsimd.memset, writes=[Bm0], ap=_w1flat[:, 0:544], constant=0.0)
        C.sp.wait(Bm0.w)
        tok = None
        with nc.allow_non_contiguous_dma(reason="tiny setup shuffles (page table spread, gate rows)"):
            for r8 in range(8):
                tok = C.dma(C.sp, sl_ds, idxraw[r8:128:8, :], bass.AP(ptab.tensor, 0, [[1, 16], [16, 16]]))
            tok = C.dma(C.sp, sl_ds, pm8f[:], c_pm8)
            for h in range(8):
                tok = C.dma(C.sp, sl_ds, mcst[96:127, h, :], toep(h, TOFF + 2017 - 16 * 96, TW - 16, 31, 4))
            for tau in range(16):
                for h in range(8):
                    tok = C.dma(C.sp, sl_ds, bspst[120:128, tau, h, :], toep(h, TOFF + 128 - tau, TW - 16, 8, 4))
            Bgs = Buf()
            for r in range(4):
                C.dma(C.sp, sl_ds, gs_scr[r], gat[0:64, 16, r:24:4], writes=[Bgs])
            for r in range(4):
                tok = C.dma(C.sp, sl_ds, gS[4 * r:4 * r + 4, :, :, :].rearrange("p b a g -> p b (a g)"),
                            bass.AP(gs_scr.tensor, r * 384, [[6, 4], [24, 16], [1, 6]]), reads=[Bgs])
        BDS.w = tok

    def d_setup_post(idx, idxf, Mcs, Bsp):
        C.dve.op(nc.vector.tensor_copy, reads=[BDS], writes=[BDS], out=idxf[:], in_=idxraw[:])
        C.dve.op(nc.vector.tensor_scalar, reads=[BDS], writes=[BDS], out=idxf[:], in0=idxf[:], scalar1=8.0, scalar2=pm8f[:, 0:1], op0=ALU.mult, op1=ALU.add)
        C.dve.op(nc.vector.tensor_copy, reads=[BDS], writes=[BDS], out=idx[:], in_=idxf[:])
        for g in range(2):
            C.dve.op(nc.vector.tensor_copy, reads=[BDS], writes=[BDS], out=Mcs[g][:], in_=mcst[:, 4 * g:4 * g + 4, :])
            C.dve.op(nc.vector.tensor_copy, reads=[BDS], writes=[BDS], out=Bsp[g][:], in_=bspst[:, :, 4 * g:4 * g + 4, :])

    def compress(S_ps, BS_ps, rhs_fn, Bsrc, hid, Bhid, Kdst, Vdst, Bdst):
        for j in range(2):
            p = S_ps[j]; Bp = BS_ps[j]
            for l in range(32):
                C.pe.op(nc.tensor.matmul, reads=[Bsrc], writes=[Bp], mark=(l == 31), out=p[:, 0:127], lhsT=W1bd[j][:, l, :],
                        rhs=rhs_fn(j, l), start=(l == 0), stop=(l == 31))
            C.act.op(nc.scalar.activation, reads=[Bp], writes=[Bhid[j]], out=hid[j][:, 0:127], in_=p[:, 0:127], func=AF.Silu,
                     bias=cvec[:, j:j + 1], scale=1.0)
        for g in range(2):
            p = S_ps[g]; Bp = BS_ps[g]
            C.pe.op(nc.tensor.matmul, reads=[Bhid[0]], writes=[Bp], out=p[0:64, 0:127], lhsT=W2k[:, g, :], rhs=hid[0][:, 0:127],
                    start=True, stop=True)
            C.dve.op(nc.vector.tensor_copy, reads=[Bp], writes=[Bdst], out=Kdst[g], in_=p[0:64, 0:127])
        p = S_ps[0]; Bp = BS_ps[0]
        C.pe.op(nc.tensor.matmul, reads=[Bhid[1]], writes=[Bp], out=p[0:127, 0:128], lhsT=hid[1][:, 0:127], rhs=W2v[:, :],
                start=True, stop=True)
        C.act.op(nc.scalar.copy, reads=[Bp], writes=[Bdst], out=Vdst, in_=p[0:127, 0:128].rearrange("p (g d) -> p g d", g=2))

    if "C" in phases:
        S = C.scope()
        psS = [S.psum("psS%d" % i, [128, 512], F32) for i in range(3)]; BpsS = [Buf() for _ in range(3)]
        psO = [S.psum("psO%d" % i, [128, 512], F32) for i in range(4)]; BpsO = [Buf() for _ in range(4)]
        psMX = S.psum("psMX", [128, 512], F32)
        psM = psMX[:, 0:128]; BpsM = Buf()
        psX = psMX[:, 128:384].bitcast(BF16).rearrange("p (j t) -> p j t", t=128); BpsX = Buf()
        hid = [S.sbuf("hid%d" % j, [128, 128], BF16) for j in range(2)]; Bhid = [Buf(), Buf()]
        Pb = [S.sbuf("Pb%d" % i, [128, 512], BF16) for i in range(4)]; BPb = [Buf() for _ in range(4)]
        oacc = S.sbuf("oacc", [128, 512], F32); Boacc = Buf()
        oab = S.sbuf("oab", [128, 512], BF16); Boab = Buf()
        sm = S.sbuf("sm", [128, 64], F32)
        imp = S.sbuf("imp", [128, 32], F32); sc = S.sbuf("sc", [128, 32], F32); mb = [S.sbuf("mb%d" % g, [128, 32], BF16) for g in range(2)]; Bmb = [Buf(), Buf()]
        Bsel = Buf(); BMB = [Buf(), Buf()]; Bc = Buf(); BoAT = Buf()
        compress(psS, BpsS, lambda j, l: kcvT[j][:, l:l + 16 * 126 + 1:16], Bc, hid, Bhid,
                 [Kcaug[g][0:64, 0:127] for g in range(2)], Vc[0:127, :, 0:64], Bc)
        sring = [0]; pring2 = [0]
        q3 = [qaug[g][:, :].rearrange("p (r t) -> p r t", r=4) for g in range(2)]

        def qk_exp(lhsT, K, M, qt, g, extra, reads):
            b = sring[0] % 3; sring[0] += 1
            C.pe.op(nc.tensor.matmul, reads=reads, writes=[BpsS[b]], mark=(extra is None), out=psS[b][0:M, :], lhsT=lhsT,
                    rhs=q3[g][0:K, :, qt * 128:(qt + 1) * 128], start=True, stop=(extra is None))
            if extra is not None:
                el, er = extra
                C.pe.op(nc.tensor.matmul, reads=reads, writes=[BpsS[b]], out=psS[b][0:M, :], lhsT=el, rhs=er, start=False, stop=True)
            pi = pring2[0] % 4; pring2[0] += 1
            C.act.op(nc.scalar.activation, reads=[BpsS[b]], writes=[BPb[pi]], out=Pb[pi][0:M, :], in_=psS[b][0:M, :], func=AF.Exp)
            return Pb[pi], BPb[pi]

        def combine(bi, W, qt, g, br, first):
            O = psO[bi]
            O3 = O[:, 0:4 * W].rearrange("p (r w) -> p r w", r=4)
            C.dve.op(nc.vector.tensor_scalar, reads=[BpsO[bi]], writes=[Bsel], out=sm[:, 0:4], in0=O3[:, :, 64], scalar1=1e-30,
                     scalar2=None, op0=ALU.add)
            C.dve.op(nc.vector.reciprocal, reads=[Bsel], writes=[Bsel], out=sm[:, 0:4], in_=sm[:, 0:4])
            C.dve.op(nc.vector.tensor_tensor, reads=[Bsel], writes=[Bsel], out=sm[:, 4:8], in0=sm[:, 0:4],
                     in1=gat[:, qt, br * 8 + g * 4:br * 8 + g * 4 + 4], op=ALU.mult)
            for r in range(4):
                dst = oacc[:, (g * 4 + r) * 64:(g * 4 + r + 1) * 64]
                if first:
                    C.dve.op(nc.vector.tensor_scalar, reads=[BpsO[bi], Bsel], writes=[Boacc], out=dst, in0=O[:, r * W:r * W + 64],
                             scalar1=sm[:, 4 + r:5 + r], scalar2=None, op0=ALU.mult)
                else:
                    C.dve.op(nc.vector.scalar_tensor_tensor, reads=[BpsO[bi], Bsel], writes=[Boacc], out=dst, in0=O[:, r * W:r * W + 64],
                             scalar=sm[:, 4 + r:5 + r], in1=dst, op0=ALU.mult, op1=ALU.add)

        SKEW = 2
        tiles = []

        def sel_part1(qt, g, bi):
            O3 = psO[bi][:, 0:388].rearrange("p (r w) -> p r w", r=4)
            C.dve.op(nc.vector.tensor_scalar, reads=[BpsO[bi]], writes=[Bsel], out=sm[:, 8:12], in0=O3[:, :, 64], scalar1=1e-30,
                     scalar2=None, op0=ALU.add)
            C.dve.op(nc.vector.reciprocal, reads=[Bsel], writes=[Bsel], out=sm[:, 8:12], in_=sm[:, 8:12])
            for r in range(4):
                if r == 0:
                    C.dve.op(nc.vector.tensor_scalar, reads=[BpsO[bi], Bsel], writes=[Bsel], out=imp[:], in0=psO[bi][:, 65:97],
                             scalar1=sm[:, 8:9], scalar2=None, op0=ALU.mult)
                else:
                    C.dve.op(nc.vector.scalar_tensor_tensor, reads=[BpsO[bi], Bsel], writes=[Bsel], out=imp[:],
                             in0=psO[bi][:, r * 97 + 65:r * 97 + 97], scalar=sm[:, 8 + r:9 + r], in1=imp[:], op0=ALU.mult, op1=ALU.add)
            C.dve.op(nc.vector.tensor_tensor, reads=[Bsel], writes=[Bsel], out=sc[:], in0=imp[:], in1=selc[:, 0, qt, :], op=ALU.mult)
            C.dve.op(nc.vector.tensor_tensor, reads=[Bsel], writes=[Bsel], out=sc[:], in0=sc[:], in1=selc[:, 1, qt, :], op=ALU.add)
            C.dve.op(nc.vector.max, reads=[Bsel], writes=[Bsel], out=sm[:, 16:24], in_=sc[:])
            C.dve.op(nc.vector.tensor_scalar, reads=[Bsel], writes=[Bmb[g]], out=mb[g][:], in0=sc[:], scalar1=sm[:, 23:24], scalar2=NEG,
                     op0=ALU.is_lt, op1=ALU.mult)
            combine(bi, 97, qt, g, 0, True)

        def sel_part2(qt, g):
            C.pe.op(nc.tensor.matmul, reads=[Bmb[g]], writes=[BpsM], out=psM[64:96, 0:128], lhsT=mb[g][:, :], rhs=identb[:, :], start=True, stop=True)
            C.act.op(nc.scalar.copy, reads=[BpsM], writes=[BMB[g]], out=q3[g][64:96, :, qt * 128:(qt + 1) * 128],
                     in_=bass.AP(psM[64:96, 0:128].tensor, psM[64:96, 0:128].offset, [list(psM[64:96, 0:128].ap[0]), [0, 4], [1, 128]]))

        def fin1(qt):
            C.act.op(nc.scalar.copy, reads=[Boacc], writes=[Boab], out=oab[:], in_=oacc[:])

        def fin2(qt):
            for j in range(4):
                C.pe.op(nc.tensor.transpose, reads=[Boab], writes=[BpsX], mark=(j == 3), out=psX[:, j, :], in_=oab[:, j * 128:(j + 1) * 128],
                        identity=identb[:, :])
            C.dve.op(nc.vector.tensor_copy, reads=[BpsX], writes=[BoAT], out=oAT[:, :, qt * 128:(qt + 1) * 128], in_=psX[:, :, :])

        for qt in range(16):
            nwin = min(qt + 1, 5)
            for g in range(2):
                ncv = min(8 * (qt + 1), 127)
                c0 = 128 - 8 * qt
                tiles.append(dict(qt=qt, g=g, lhsT=Kcaug[g][:, 0:ncv], M=ncv, extra=(Iw[0:16, c0:c0 + ncv], Mcmp[g][:, :, :]), reads=[Bc, BMB[g]],
                                  bank=g, W=97, rhs=Vc[0:ncv, g, :], first=True, last=True,
                                  hooks=[(0, lambda qt=qt, g=g: sel_part1(qt, g, g)), (min(3, nwin), lambda qt=qt, g=g: sel_part2(qt, g))]))
            for g in range(2):
                kts = list(range(max(0, qt - 4), qt + 1))
                for i, kt in enumerate(kts):
                    d = qt - kt
                    extra = None
                    if d <= 1:
                        extra = (identb[:, :], BT[g][d][:, :, :])
                    elif d == 4:
                        extra = (identb[:, :], Wm4[:, :, :])
                    tiles.append(dict(qt=qt, g=g, lhsT=Kwaug[g][:, kt * 128:(kt + 1) * 128], M=128, extra=extra, reads=[BMB[g]], bank=2, W=65,
                                      rhs=Vw[:, kt, g, :], first=(i == 0), last=(i == len(kts) - 1),
                                      hooks=([(0, lambda qt=qt, g=g: combine(2, 65, qt, g, 2, False))] if i == len(kts) - 1 else [])))
                for kt in range(qt + 1):
                    d = qt - kt
                    extra = (identb[:, :], BT[g][d][:, :, :]) if d <= 1 else None
                    hooks = []
                    if kt == qt:
                        hooks.append((0, lambda qt=qt, g=g: combine(3, 65, qt, g, 1, False)))
                        if g == 1:
                            hooks.append((0, lambda qt=qt: fin1(qt)))
                            hooks.append((2, lambda qt=qt: fin2(qt)))
                    tiles.append(dict(qt=qt, g=g, lhsT=Ksaug[g][:, kt * 128:(kt + 1) * 128], M=128, extra=extra, reads=[BMB[g]], bank=3, W=65,
                                      rhs=Vs[:, kt, g, :], first=(kt == 0), last=(kt == qt), hooks=hooks))
        pending = {}
        inflight = {}
        nt_ = len(tiles)
        for step in range(nt_ + SKEW + 4):
            j = step - SKEW
            if 0 <= j < nt_:
                T = tiles[j]
                P, BP = inflight.pop(j)
                bi = T["bank"]; W = T["W"]; M = T["M"]
                for r in range(4):
                    C.pe.op(nc.tensor.matmul, reads=[BP] + T["reads"], writes=[BpsO[bi]], mark=(r == 3), out=psO[bi][:, r * W:(r + 1) * W],
                            lhsT=P[0:M, r * 128:(r + 1) * 128], rhs=T["rhs"], start=(T["first"] and r == 0), stop=(T["last"] and r == 3),
                            skip_group_check=True)
                for (dl, fn) in T["hooks"]:
                    pending.setdefault(step + dl, []).append(fn)
            for fn in pending.pop(step, []):
                fn()
            if step < nt_:
                T = tiles[step]
                inflight[step] = qk_exp(T["lhsT"], 97, T["M"], T["qt"], T["g"], T["extra"], T["reads"])
        S.close()
    Bsmp = Buf()
    for g in range(2):
        C.dve.op(nc.vector.tensor_copy, writes=[Bsmp], out=qS[g][:], in_=qaug[g][:, :].rearrange("p (r t) -> p r t", r=4)[:, :, 2048:NTOK])
        C.dve.op(nc.vector.tensor_copy, writes=[Bsmp], out=KsS[g][:], in_=Ksaug[g][:, 2048:NTOK])
        C.dve.op(nc.vector.tensor_copy, writes=[Bsmp], out=KwS[g][:], in_=Kwaug[g][:, 2048:NTOK])
    SAB1.close()


    if "D" in phases:
        S = C.scope()
        oS_scr = dscr("oS_scr", [64, 512])
        pgb = [S.sbuf("pg%d" % c, [128, 16, 128], F32) for c in range(4)]; Bpg = [Buf() for _ in range(4)]
        wkvb = [S.sbuf("wkvb%d" % j, [128, 4, 128], F32) for j in range(2)]; Bwkv_ = [Buf(), Buf()]
        pgbf = [S.sbuf("pgbf%d" % c, [128, 16, 128], BF16) for c in range(2)]; Bpgf = [Buf() for _ in range(3)]
        pgbf.append(wst[1][:, :, :].rearrange("p a b -> p (a b)")[:, 1024:2048].bitcast(BF16).rearrange("p (a b) -> p a b", b=128))
        cT = [S.sbuf("cT%d" % j, [128, 2048], BF16) for j in range(2)]; BcT = Buf()
        KsT = [S.sbuf("KsT%d" % g, [97, 2048], BF16) for g in range(2)]; BKsT = Buf()
        KwT = [S.sbuf("KwT%d" % g, [97, 512], BF16) for g in range(2)]; BKwT = Buf()
        Vsb = S.sbuf("Vsb", [128, 16, 2, 65], BF16); BVsb = Buf()
        Vwb = S.sbuf("Vwb", [128, 4, 2, 65], BF16); BVwb = Buf()
        VnS = S.sbuf("VnS", [4, 16, 2, 2, 65], BF16)
        vnst1 = wst[0][0:4, :, :].rearrange("p a b -> p (a b)").rearrange("p (b c) -> p b c", c=128)
        Ws = S.sbuf("Ws", [128, 4, 4], BF16)
        oS = wst[0][0:16, :, :].rearrange("p a (b c) -> p (a b) c", c=64).rearrange("p (b g) c -> p b g c", g=2); BoS = Buf()
        Rsel = S.sbuf("Rsel", [16, 4], F32)
        selS = S.sbuf("selS", [4, 2, 32], F32)
        hid = [S.sbuf("hidD%d" % j, [128, 128], BF16) for j in range(2)]; Bhid = [Buf(), Buf()]
        Pd = [S.sbuf("Pd%d" % i, [128, 288], BF16) for i in range(3)]; BPd = [Buf() for _ in range(3)]
        smd = S.sbuf("smd", [16, 64], F32); impn = S.sbuf("impn", [16, 32], F32)
        sc4 = S.sbuf("sc4", [4, 32], F32); mb4 = S.sbuf("mb4", [4, 32], BF16)
        Ball_s = [S.sbuf("Ball_s%d" % g, [128, 17, 4, 4], BF16) for g in range(2)]
        Ball_w = [S.sbuf("Ball_w%d" % g, [128, 5, 4, 4], BF16) for g in range(2)]
        oSt = wst[1][0:64, :, :].rearrange("p a (b c) -> p (a b) c", c=64).rearrange("p a c -> p (a c)")[:, 0:512]; oSb = S.sbuf("oSb", [64, 512], BF16)
        psTr = [S.psum("psTr%d" % i, [128, 512], F32) for i in range(2)]; BpsTr = [Buf(), Buf()]
        psC = [S.psum("psC%d" % i, [128, 512], F32) for i in range(2)]; BpsC = [Buf(), Buf()]
        psTr4 = [psTr[0], psTr[1], psC[0], psC[1]]; BpsTr4 = [BpsTr[0], BpsTr[1], BpsC[0], BpsC[1]]
        psSd = [S.psum("psSd%d" % i, [128, 512], F32) for i in range(2)]; BpsSd = [Buf(), Buf()]
        psOd = S.psum("psOd", [128, 512], F32); BpsOd = Buf(); BpsOw = Buf()
        psMd = S.psum("psMd", [128, 512], F32); BpsMd = Buf()
        sl_pg = [C.slot("pg%d" % c) for c in range(4)]; sl_wk = [C.slot("wk0"), C.slot("wk1")]
        Bk = Buf(); BqS = Buf()
        for j in range(2):
            C.dma(C.sp, sl_misc, o_swin[j][:, 0:508, :], st_win[j][:, 4:512, :])
        idx = S.sbuf("idx", [128, 16], I32); idxf = S.sbuf("idxf", [128, 16], F32)
        Mcs = [S.sbuf("Mcs%d" % g, [128, 4, 4], BF16) for g in range(2)]
        Bsp = [S.sbuf("Bsp%d" % g, [128, 16, 4, 4], BF16) for g in range(2)]
        d_setup_post(idx, idxf, Mcs, Bsp)
        for g in range(2):
            C.pool.op(nc.gpsimd.memset, writes=[Bk], ap=KsT[g][64:96, :], constant=1.0)
            E3 = KsT[g][64:96, :].rearrange("p (a m) -> p a m", m=128)
            C.pool.op(nc.gpsimd.affine_select, writes=[Bk], out=E3, in_=E3, pattern=[[0, 16], [1, 128]], base=0,
                      channel_multiplier=-4, compare_op=ALU.is_ge, fill=0.0)
            C.pool.op(nc.gpsimd.affine_select, writes=[Bk], out=E3, in_=E3, pattern=[[0, 16], [-1, 128]], base=3,
                      channel_multiplier=4, compare_op=ALU.is_ge, fill=0.0)
            C.pool.op(nc.gpsimd.memset, writes=[Bk], ap=KsT[g][96:97, :], constant=1.0)
            C.pool.op(nc.gpsimd.memset, writes=[Bk], ap=KwT[g][64:96, :], constant=0.0)
            C.pool.op(nc.gpsimd.memset, writes=[Bk], ap=KwT[g][96:97, :], constant=1.0)
        C.pool.op(nc.gpsimd.memset, writes=[Bk], ap=Vsb[:].rearrange("p a b c -> p (a b c)"), constant=1.0)
        C.pool.op(nc.gpsimd.memset, writes=[Bk], ap=Vwb[:].rearrange("p a b c -> p (a b c)"), constant=1.0)
        C.pool.op(nc.gpsimd.memset, writes=[Bk], ap=VnS[:].rearrange("p a b c d -> p (a b c d)"), constant=1.0)
        Bvn = Buf()
        for jj, j in enumerate((3, 5)):
            C.dma(C.sp, sl_misc, vnst1, o_skv[j].rearrange("(b i) c -> i b c", i=4), writes=[Bvn])
            C.dve.op(nc.vector.tensor_copy, reads=[Bvn, Bk], writes=[Bk], out=VnS[:, :, jj, :, 0:64],
                     in_=vnst1.rearrange("p b (g d) -> p b g d", g=2))
            Bvn.r.append(Bk.w)
        C.pool.op(nc.gpsimd.memset, writes=[Bk], ap=Ws[:].rearrange("p a b -> p (a b)"), constant=0.0)
        C.pool.op(nc.gpsimd.affine_select, writes=[Bk], out=Ws[:], in_=Ws[:], pattern=[[0, 4], [-1, 4]], base=0, channel_multiplier=1,
                  compare_op=ALU.is_gt, fill=NEG)
        for g in range(2):
            C.pool.op(nc.gpsimd.memset, writes=[Bk], ap=Ball_s[g][:].rearrange("p a b c -> p (a b c)"), constant=0.0)
            C.pool.op(nc.gpsimd.memset, writes=[Bk], ap=Ball_w[g][:].rearrange("p a b c -> p (a b c)"), constant=0.0)
            C.dve.op(nc.vector.tensor_copy, reads=[BDS, Bk], writes=[Bk], out=Ball_s[g][:, 0:16, :, :], in_=Bsp[g][:, :, :, :])
            C.dve.op(nc.vector.tensor_copy, reads=[Bk], writes=[Bk], out=Ball_s[g][0:4, 16, :, :], in_=BT[g][0][0:4, :, 0:4])
            C.dve.op(nc.vector.tensor_copy, reads=[Bk], writes=[Bk], out=Ball_w[g][:, 0, :, :], in_=Ws[:, :, :])
            C.dve.op(nc.vector.tensor_copy, reads=[Bk], writes=[Bk], out=Ball_w[g][:, 3, :, :], in_=BT[g][1][:, :, 0:4])
            C.dve.op(nc.vector.tensor_copy, reads=[Bk], writes=[Bk], out=Ball_w[g][0:4, 4, :, :], in_=BT[g][0][0:4, :, 0:4])
        C.dve.op(nc.vector.tensor_copy, reads=[Bid], writes=[Bk], out=Rsel[:], in_=identf[0:16, 0:4])
        for r in range(1, 4):
            C.dve.op(nc.vector.tensor_tensor, reads=[Bid, Bk], writes=[Bk], out=Rsel[:], in0=Rsel[:], in1=identf[0:16, 4 * r:4 * r + 4], op=ALU.add)
        C.pool.op(nc.gpsimd.memset, writes=[Bk], ap=selS[:, 0, :], constant=1.0)
        C.pool.op(nc.gpsimd.memset, writes=[Bk], ap=selS[:, 1, :], constant=0.0)
        for j in (0, 31):
            C.pool.op(nc.gpsimd.memset, writes=[Bk], ap=selS[:, 0, j:j + 1], constant=0.0)
            C.pool.op(nc.gpsimd.memset, writes=[Bk], ap=selS[:, 1, j:j + 1], constant=1e9)
        tr = [0]
        prd = [0]

        def transposes(src_fn, nparts_out, ntiles, dst_fn, Bsrc, Bdst, bf=False):
            for k0 in range(0, ntiles, 4):
                b = tr[0] % 4; tr[0] += 1
                pt = psTr4[b][:, 0:256].bitcast(BF16) if bf else psTr4[b]
                for k in range(k0, k0 + 4):
                    C.pe.op(nc.tensor.transpose, reads=[Bsrc, Bid], writes=[BpsTr4[b]], mark=(k == k0 + 3),
                            out=pt[0:nparts_out, (k - k0) * 128:(k - k0 + 1) * 128], in_=src_fn(k), identity=(identb[:, :] if bf else identf[:, :]))
                if b % 2 == 0:
                    C.act.op(nc.scalar.copy, reads=[BpsTr4[b]], writes=[Bdst], out=dst_fn(k0), in_=pt[0:nparts_out, :])
                else:
                    C.dve.op(nc.vector.tensor_copy, reads=[BpsTr4[b]], writes=[Bdst], out=dst_fn(k0), in_=pt[0:nparts_out, :])

        for bb in range(16):
            for c in range(4):
                for t_ in Bpg[c].r:
                    C.pool.wait(t_)
                C.pool.wait(Bpg[c].w); C.pool.wait(BDS.w); sl_pg[c].pre_issue(C.pool)
                ins = nc.gpsimd.indirect_dma_start(out=pgb[c][:, :, :].rearrange("p a b -> p (a b)"), out_offset=None, in_=caches[c][:, :],
                                                   in_offset=bass.IndirectOffsetOnAxis(ap=idx[:, bb:bb + 1], axis=0))
                C.ndma += 1
                Bpg[c].w = sl_pg[c].issued(ins); Bpg[c].r = []
            for j in range(2):
                C.dma(C.sp, sl_wk[j], wkvb[j][:], st_win[j][bb].rearrange("(t p) c -> p t c", p=128), writes=[Bwkv_[j]])
            for c in range(3):
                if c == 1:
                    C.dve.op(nc.vector.tensor_copy, reads=[Bpg[c]], writes=[Bpgf[c]], out=pgbf[c][:, :, :], in_=pgb[c][:, :, :])
                else:
                    C.act.op(nc.scalar.copy, reads=[Bpg[c]], writes=[Bpgf[c]], out=pgbf[c][:, :, :], in_=pgb[c][:, :, :])
            for j in range(2):
                transposes(lambda k, j=j: pgbf[j][:, k, :], 128, 16, lambda k0, j=j: cT[j][:, k0 * 128:(k0 + 4) * 128], Bpgf[j], BcT, bf=True)
            compress(psC, BpsC, lambda j, l: cT[j][:, (l % 16) * 128 + l // 16:(l % 16) * 128 + l // 16 + 127], BcT, hid, Bhid, [Kcaug[g][0:64, 0:127] for g in range(2)], Vc[0:127, :, 0:64], Bk)
            for g in range(2):
                transposes(lambda k, g=g: pgbf[2][:, k, g * 64:(g + 1) * 64], 64, 16, lambda k0, g=g: KsT[g][0:64, k0 * 128:(k0 + 4) * 128],
                           Bpgf[2], BKsT, bf=True)
                transposes(lambda k, g=g: wkvb[0][:, k, g * 64:(g + 1) * 64], 64, 4, lambda k0, g=g: KwT[g][0:64, :], Bwkv_[0], BKwT)
            C.act.op(nc.scalar.copy, reads=[Bpg[3]], writes=[BVsb], out=Vsb[:, :, :, 0:64],
                     in_=pgb[3][:, :, :].rearrange("p a (g d) -> p a g d", g=2))
            C.dve.op(nc.vector.tensor_copy, reads=[Bwkv_[1]], writes=[BVwb], out=Vwb[:, :, :, 0:64],
                     in_=wkvb[1][:, :, :].rearrange("p a (g d) -> p a g d", g=2))
            for g in range(2):
                q16 = qS[g][:, :, 4 * bb:4 * bb + 4]

                def branch(tiles, W, bank_col, ball=None):
                    sb = prd[0] % 2; pi = prd[0] % 3; prd[0] += 1
                    nt_ = len(tiles)
                    if ball is not None:
                        Mb = ball.shape[0]
                        C.pe.op(nc.tensor.matmul, reads=[Bk], writes=[BpsSd[sb]], mark=False, out=psSd[sb][0:Mb, 0:nt_ * 16], lhsT=identb[0:Mb, 0:Mb],
                                rhs=ball, start=True, stop=False)
                    for ti, (lh, M, extra, vr, rd) in enumerate(tiles):
                        if ball is not None:
                            C.pe.op(nc.tensor.matmul, reads=rd, writes=[BpsSd[sb]], mark=(ti == nt_ - 1), out=psSd[sb][0:M, ti * 16:(ti + 1) * 16],
                                    lhsT=lh, rhs=q16, start=False, stop=(ti == nt_ - 1), skip_group_check=True)
                            continue
                        C.pe.op(nc.tensor.matmul, reads=rd, writes=[BpsSd[sb]], mark=(extra is None and ti == len(tiles) - 1),
                                out=psSd[sb][0:M, ti * 16:(ti + 1) * 16], lhsT=lh, rhs=q16, start=True, stop=(extra is None))
                        if extra is not None:
                            C.pe.op(nc.tensor.matmul, reads=[Bk], writes=[BpsSd[sb]], mark=(ti == len(tiles) - 1),
                                    out=psSd[sb][0:M, ti * 16:(ti + 1) * 16], lhsT=extra[0], rhs=extra[1], start=False, stop=True)
                    C.act.op(nc.scalar.activation, reads=[BpsSd[sb]], writes=[BPd[pi]], out=Pd[pi][:, 0:nt_ * 16], in_=psSd[sb][:, 0:nt_ * 16], func=AF.Exp)
                    for ti, (lh, M, extra, vr, rd) in enumerate(tiles):
                        C.pe.op(nc.tensor.matmul, reads=[BPd[pi]] + rd, writes=[BpsOd], mark=(ti == nt_ - 1),
                                out=psOd[0:16, bank_col:bank_col + W], lhsT=Pd[pi][0:M, ti * 16:(ti + 1) * 16], rhs=vr,
                                start=(ti == 0), stop=(ti == nt_ - 1), skip_group_check=True)

                def combine_s(bank_col, br, first, Bo=None):
                    Bo = Bo or BpsOd
                    C.dve.op(nc.vector.tensor_scalar, reads=[Bo], writes=[Bk], out=smd[:, 0:1], in0=psOd[0:16, bank_col + 64:bank_col + 65],
                             scalar1=1e-30, scalar2=None, op0=ALU.add)
                    C.dve.op(nc.vector.reciprocal, reads=[Bk], writes=[Bk], out=smd[:, 0:1], in_=smd[:, 0:1])
                    C.dve.op(nc.vector.tensor_tensor, reads=[Bk], writes=[Bk], out=smd[:, 1:2], in0=smd[:, 0:1], in1=gS[:, bb, br, g:g + 1], op=ALU.mult)
                    dst = oS[:, bb, g, :]
                    if first:
                        C.dve.op(nc.vector.tensor_scalar, reads=[Bo, Bk], writes=[BoS], out=dst, in0=psOd[0:16, bank_col:bank_col + 64],
                                 scalar1=smd[:, 1:2], scalar2=None, op0=ALU.mult)
                    else:
                        C.dve.op(nc.vector.scalar_tensor_tensor, reads=[Bo, Bk], writes=[BoS], out=dst, in0=psOd[0:16, bank_col:bank_col + 64],
                                 scalar=smd[:, 1:2], in1=dst, op0=ALU.mult, op1=ALU.add)

                sb = prd[0] % 2; pi = prd[0] % 3; prd[0] += 1
                wt = [(KwT[g][:, kt * 128:(kt + 1) * 128], 128, Vwb[:, kt, g, :]) for kt in range(4)]
                wt.append((KwS[g][:, 4 * bb:4 * bb + 4], 4, VnS[0:4, bb, 1, g, :]))
                C.pe.op(nc.tensor.matmul, reads=[Bk], writes=[BpsSd[sb]], mark=False, out=psSd[sb][:, 16:96], lhsT=identb[:, :],
                        rhs=Ball_w[g][:, :, :, :], start=True, stop=False)
                for ti, (lh, M, vr) in enumerate(wt):
                    C.pe.op(nc.tensor.matmul, reads=[BKwT, BqS, Bk], writes=[BpsSd[sb]], mark=False, out=psSd[sb][0:M, 16 + ti * 16:32 + ti * 16],
                            lhsT=lh, rhs=q16, start=False, stop=False, skip_group_check=True)
                C.pe.op(nc.tensor.matmul, reads=[Bk, BqS], writes=[BpsSd[sb]], mark=False, out=psSd[sb][0:127, 0:16], lhsT=Kcaug[g][:, 0:127],
                        rhs=q16, start=False, stop=False, skip_group_check=True)
                C.pe.op(nc.tensor.matmul, reads=[Bk], writes=[BpsSd[sb]], out=psSd[sb][0:127, 0:16], lhsT=identb[0:127, 0:127],
                        rhs=Mcs[g][0:127, :, :], start=False, stop=True, skip_group_check=True)
                C.act.op(nc.scalar.activation, reads=[BpsSd[sb]], writes=[BPd[pi]], out=Pd[pi][:, 0:96], in_=psSd[sb][:, 0:96], func=AF.Exp)
                C.pe.op(nc.tensor.matmul, reads=[BPd[pi], Bk], writes=[BpsOd], out=psOd[0:16, 0:97], lhsT=Pd[pi][0:127, 0:16], rhs=Vc[0:127, g, :],
                        start=True, stop=True, skip_group_check=True)
                for ti, (lh, M, vr) in enumerate(wt):
                    C.pe.op(nc.tensor.matmul, reads=[BPd[pi], BVwb, Bk], writes=[BpsOw], mark=(ti == 4), out=psOd[0:16, 128:193],
                            lhsT=Pd[pi][0:M, 16 + ti * 16:32 + ti * 16], rhs=vr, start=False, stop=(ti == 4), skip_group_check=True)
                C.dve.op(nc.vector.tensor_scalar, reads=[BpsOd], writes=[Bk], out=smd[:, 8:9], in0=psOd[0:16, 64:65], scalar1=1e-30, scalar2=None,
                         op0=ALU.add)
                C.dve.op(nc.vector.reciprocal, reads=[Bk], writes=[Bk], out=smd[:, 8:9], in_=smd[:, 8:9])
                C.dve.op(nc.vector.tensor_scalar, reads=[BpsOd, Bk], writes=[Bk], out=impn[:], in0=psOd[0:16, 65:97], scalar1=smd[:, 8:9],
                         scalar2=None, op0=ALU.mult)
                C.pe.op(nc.tensor.matmul, reads=[Bk], writes=[BpsMd], out=psMd[0:4, 0:32], lhsT=Rsel[:, :], rhs=impn[:, :], start=True, stop=True)
                C.dve.op(nc.vector.tensor_tensor, reads=[BpsMd, Bk], writes=[Bk], out=sc4[:], in0=psMd[0:4, 0:32], in1=selS[:, 0, :], op=ALU.mult)
                C.dve.op(nc.vector.tensor_tensor, reads=[Bk], writes=[Bk], out=sc4[:], in0=sc4[:], in1=selS[:, 1, :], op=ALU.add)
                C.dve.op(nc.vector.max, reads=[Bk], writes=[Bk], out=smd[0:4, 16:24], in_=sc4[:])
                C.dve.op(nc.vector.tensor_scalar, reads=[Bk], writes=[Bk], out=mb4[:], in0=sc4[:], scalar1=smd[0:4, 22:23], scalar2=NEG,
                         op0=ALU.is_lt, op1=ALU.mult)
                C.pe.op(nc.tensor.matmul, reads=[Bk], writes=[BpsMd], out=psMd[64:96, 64:68], lhsT=mb4[:, :], rhs=identb[0:4, 0:4], start=True, stop=True)
                src = psMd[64:96, 64:68]
                C.act.op(nc.scalar.copy, reads=[BpsMd], writes=[BqS], out=qS[g][64:96, :, 4 * bb:4 * bb + 4],
                         in_=bass.AP(src.tensor, src.offset, [list(src.ap[0]), [0, 4], [1, 4]]))
                combine_s(0, 0, True)
                combine_s(128, 2, False, BpsOw)
                tiles = []
                for kt in range(16):
                    extra = (identb[:, :], Bsp[g][:, kt, :, :])
                    tiles.append((KsT[g][:, kt * 128:(kt + 1) * 128], 128, extra, Vsb[:, kt, g, :], [BKsT, BVsb, BqS]))
                tiles.append((KsS[g][:, 4 * bb:4 * bb + 4], 4, (identb[0:4, 0:4], BT[g][0][0:4, :, 0:4]), VnS[0:4, bb, 0, g, :], [Bk, BqS]))
                branch(tiles, 65, 256, Ball_s[g][:, :, :, :])
                combine_s(256, 1, False)
        Bsh = Buf()
        for r in range(4):
            for g in range(2):
                C.dma(C.sp, sl_misc, bass.AP(oS_scr.tensor, g * 256 + r * 64, [[512, 4], [2048, 16], [1, 64]]), oS[4 * r:4 * r + 4, :, g, :],
                      reads=[BoS], writes=[Bsh])
        C.dma(C.sp, sl_misc, oSt, oS_scr, reads=[Bsh], writes=[Bsh])
        C.dve.op(nc.vector.tensor_copy, reads=[Bsh], writes=[Bsh], out=oSb[:], in_=oSt)
        psX3 = psTr[0][:].bitcast(BF16).rearrange("p (j t) -> p j t", t=128)
        for j in range(4):
            C.pe.op(nc.tensor.transpose, reads=[Bsh, Bid], writes=[BpsTr[0]], mark=(j == 3), out=psX3[:, j, 0:64], in_=oSb[0:64, j * 128:(j + 1) * 128],
                    identity=identb[0:64, 0:64])
        C.dve.op(nc.vector.tensor_copy, reads=[BpsTr[0]], writes=[Bsh], out=oAT[:, :, 2048:NTOK], in_=psX3[:, 0:4, 0:64])
        S.close()
    SAB2.close()


    if "E" in phases:
        S = C.scope()
        qdT = S.sbuf("qdT", [128, 4, NTOK], BF16); kdT = S.sbuf("kdT", [128, 4, NTOK], BF16)
        vtok = S.sbuf("vtok", [128, 16, 512], BF16); vS = [S.sbuf("vS%d" % i, [4, 512], BF16) for i in range(2)]; BvS = [Buf(), Buf()]
        wq = S.sbuf("wq", [128, KC, 512], BF16); wf = S.sbuf("wf", [128, KC, 512], BF16); wib = S.sbuf("wib", [128, KC, 512], BF16)
        lbt = S.sbuf("lbt", [128, 2, 4], F32); oml = S.sbuf("oml", [128, 4], F32); noml = S.sbuf("noml", [128, 4], F32)
        rmask = S.sbuf("rmask", [128, 512], F32); rmask4 = S.sbuf("rmask4", [128, 64], F32)
        eGl = S.sbuf("eGl", [128, 4, 32], F32); eGls = S.sbuf("eGls", [128, 4, 16], F32)
        Sst = S.sbuf("Sst", [128, 4, 128], F32); Sbf = S.sbuf("Sbf", [128, 4, 128], BF16)
        hgbc = S.sbuf("hgbc", [128, 512], F32); hgcol = S.sbuf("hgcol", [128, 4], F32)
        tri = S.sbuf("tri", [128, 64], F32)
        tmpf2 = [[S.sbuf("tmpf%d_%d" % (i, k), [128, 512], F32) for i in range(5)] for k in range(2)]
        Btmp2 = [[Buf() for _ in range(5)] for k in range(2)]
        pbank = [S.psum("psE%d" % i, [128, 512], F32) for i in range(8)]; Bpbank = [Buf() for _ in range(8)]
        ps2 = pbank[0:2]; Bps2 = Bpbank[0:2]
        psA = pbank[2][:, 0:256].rearrange("p (h t) -> p h t", h=4); BpsA = Bpbank[2]
        psK = pbank[3][:, 0:256].bitcast(BF16).rearrange("p (h t) -> p h t", h=4); BpsK = Bpbank[3]
        psOo = pbank[4][:, :].rearrange("p (h t) -> p h t", h=4); BpsOo = Bpbank[4]
        psD = [pbank[5 + i][:, :].rearrange("p (h t) -> p h t", h=4) for i in range(2)]; BpsD = Bpbank[5:7]
        psX2 = pbank[7][:, 0:256].bitcast(BF16).rearrange("p (h t) -> p h t", h=4); BpsX2 = Bpbank[7]
        Bw = Buf(); Bc = Buf(); Bqk = Buf(); Bv = Buf(); BoBT = Buf()
        Bwh = [Buf(), Buf()]
        for half in range(2):
            load_w(wf[:, :, half * 256:(half + 1) * 256], Bwh[half], w_in[:, C_FB + half * 256:C_FB + (half + 1) * 256], 256)
            load_w(wq[:, :, half * 256:(half + 1) * 256], Bwh[half], w_in[:, C_QB + half * 256:C_QB + (half + 1) * 256], 256)
        load_w(wib, Bw, w_in[:, C_IB:C_IB + 512], 512)
        with nc.allow_non_contiguous_dma(reason="tiny strided load of the HGRN lower-bound logits"):
            C.dma(C.sp, sl_misc, lbt[:], hg_lower.rearrange("a (h d) -> d a h", d=128), writes=[Bc])
            C.dma(C.sp, sl_misc, hgcol[:], hg_norm_g.rearrange("a (h d) -> d (a h)", d=128), writes=[Bc])
        C.dma(C.sp, sl_misc, hgbc[:], dram_bcast(hg_norm_g, 128), writes=[Bc])
        C.dma(C.sp, sl_misc, tri[:], c_tri, writes=[Bc])
        C.dve.op(nc.vector.tensor_tensor, reads=[Bc], writes=[Bc], out=oml[:], in0=lbt[:, 0, :], in1=lbt[:, 1, :], op=ALU.subtract)
        C.act.op(nc.scalar.activation, reads=[Bc], writes=[Bc], out=oml[:], in_=oml[:], func=AF.Exp)
        C.dve.op(nc.vector.tensor_scalar, reads=[Bc], writes=[Bc], out=oml[:], in0=oml[:], scalar1=1.0, scalar2=None, op0=ALU.add)
        C.dve.op(nc.vector.reciprocal, reads=[Bc], writes=[Bc], out=oml[:], in_=oml[:])
        C.dve.op(nc.vector.tensor_scalar, reads=[Bc], writes=[Bc], out=noml[:], in0=oml[:], scalar1=-1.0, scalar2=None, op0=ALU.mult)
        C.pool.op(nc.gpsimd.memset, writes=[Bc], ap=rmask[:], constant=1.0)
        C.pool.op(nc.gpsimd.memset, writes=[Bc], ap=rmask[:, 0:512:64], constant=0.0)
        C.pool.op(nc.gpsimd.memset, writes=[Bc], ap=rmask4[:], constant=1.0)
        C.pool.op(nc.gpsimd.memset, writes=[Bc], ap=rmask4[:, 0:64:4], constant=0.0)
        C.pool.op(nc.gpsimd.memset, writes=[Bc], ap=Sst[:].rearrange("p a b -> p (a b)"), constant=0.0)
        C.pool.op(nc.gpsimd.memset, writes=[Bc], ap=Sbf[:].rearrange("p a b -> p (a b)"), constant=0.0)
        for h in range(4):
            for st in range(5):
                sneg, lnf, Gt, eG, eGn = tmpf2[(h * 5 + st) % 2]; Btmp = Btmp2[(h * 5 + st) % 2]
                n = 512 if st < 4 else 64
                cs = slice(st * 512, st * 512 + n)
                pr_ = (h * 5 + st) % 4
                ps2 = pbank[2 * pr_:2 * pr_ + 2]; Bps2 = Bpbank[2 * pr_:2 * pr_ + 2]
                for (wt, b) in ((wf, 0), (wq, 1)):
                    for kc in range(KC):
                        C.pe.op(nc.tensor.matmul, reads=[Bwh[h // 2]], writes=[Bps2[b]], mark=(kc == KC - 1), out=ps2[b][:, 0:n],
                                lhsT=wt[:, kc, h * 128:(h + 1) * 128], rhs=xnT[:, kc, cs], start=(kc == 0), stop=(kc == KC - 1))
                C.act.op(nc.scalar.activation, reads=[Bps2[0]], writes=[Btmp[0]], out=sneg[:, 0:n], in_=ps2[0][:, 0:n], func=AF.Exp)
                C.dve.op(nc.vector.tensor_scalar, reads=[Btmp[0]], writes=[Btmp[0]], out=sneg[:, 0:n], in0=sneg[:, 0:n], scalar1=1.0,
                         scalar2=None, op0=ALU.add)
                C.dve.op(nc.vector.reciprocal, reads=[Btmp[0]], writes=[Btmp[0]], out=sneg[:, 0:n], in_=sneg[:, 0:n])
                C.act.op(nc.scalar.activation, reads=[Btmp[0], Bc], writes=[Btmp[1]], out=lnf[:, 0:n], in_=sneg[:, 0:n], func=AF.Ln,
                         scale=noml[:, h:h + 1], bias=1.0)
                C.dve.op(nc.vector.tensor_tensor_scan, reads=[Btmp[1], Bc], writes=[Btmp[2]], out=Gt[:, 0:n],
                         data0=(rmask[:, 0:n] if st < 4 else rmask4[:, 0:n]), data1=lnf[:, 0:n], initial=0.0, op0=ALU.mult, op1=ALU.add)
                C.act.op(nc.scalar.activation, reads=[Btmp[2]], writes=[Btmp[3]], out=eG[:, 0:n], in_=Gt[:, 0:n], func=AF.Exp)
                C.act.op(nc.scalar.activation, reads=[Btmp[2]], writes=[Btmp[4]], out=eGn[:, 0:n], in_=Gt[:, 0:n], func=AF.Exp, scale=-1.0)
                C.dve.op(nc.vector.tensor_tensor, reads=[Bps2[1], Btmp[3]], writes=[Bqk], out=qdT[:, h, cs], in0=ps2[1][:, 0:n], in1=eG[:, 0:n],
                         op=ALU.mult)
                C.dve.op(nc.vector.scalar_tensor_tensor, reads=[Btmp[0], Btmp[4], Bc], writes=[Bqk], out=kdT[:, h, cs], in0=sneg[:, 0:n],
                         scalar=oml[:, h:h + 1], in1=eGn[:, 0:n], op0=ALU.mult, op1=ALU.mult)
                if st < 4:
                    C.dve.op(nc.vector.tensor_copy, reads=[Btmp[3]], writes=[Bqk], out=eGl[:, h, st * 8:(st + 1) * 8], in_=eG[:, 63:512:64])
                else:
                    C.dve.op(nc.vector.tensor_copy, reads=[Btmp[3]], writes=[Bqk], out=eGls[:, h, :], in_=eG[:, 3:64:4])
        ps2 = pbank[0:2]; Bps2 = Bpbank[0:2]
        for t in range(16):
            b = t % 2
            for kc in range(KC):
                C.pe.op(nc.tensor.matmul, reads=[Bw], writes=[Bps2[b]], mark=(kc == KC - 1), out=ps2[b][:, :],
                        lhsT=xnT[:, kc, t * 128:(t + 1) * 128], rhs=wib[:, kc, :], start=(kc == 0), stop=(kc == KC - 1))
            if b == 0:
                C.act.op(nc.scalar.copy, reads=[Bps2[b]], writes=[Bv], out=vtok[:, t, :], in_=ps2[b][:, :])
            else:
                C.dve.op(nc.vector.tensor_copy, reads=[Bps2[b]], writes=[Bv], out=vtok[:, t, :], in_=ps2[b][:, :])
        Am = S.sbuf("Am", [128, 4, 64], BF16); BAm = Buf()
        kdtok = S.sbuf("kdtok", [128, 4, 128], BF16); Bkdtok = Buf()
        ob = S.sbuf("ob", [128, 512], BF16); Bob = Buf()
        sq = S.sbuf("sq", [128, 128], BF16); BS = Buf(); BSbf = Buf()
        nrm = S.sbuf("nrm", [128, 16], F32); Bn = Buf()
        stmp = S.sbuf("stmp", [128, 4, 128], F32)
        psOo_r = [psOo, pbank[0][:, :].rearrange("p (h t) -> p h t", h=4)]; BpsOo_r = [BpsOo, Bpbank[0]]
        psX2_r = [psX2, pbank[1][:, 0:256].bitcast(BF16).rearrange("p (h t) -> p h t", h=4)]; BpsX2_r = [BpsX2, Bpbank[1]]
        for t in range(16):
            psOo = psOo_r[t % 2]; BpsOo = BpsOo_r[t % 2]; psX2 = psX2_r[t % 2]; BpsX2 = BpsX2_r[t % 2]
            for h in range(4):
                for c in range(2):
                    cs = slice(t * 128 + c * 64, t * 128 + c * 64 + 64)
                    C.pe.op(nc.tensor.matmul, reads=[Bqk], writes=[BpsA], mark=(h == 3 and c == 1), out=psA[c * 64:(c + 1) * 64, h, :],
                            lhsT=kdT[:, h, cs], rhs=qdT[:, h, cs], start=True, stop=True)
                C.pe.op(nc.tensor.transpose, reads=[Bqk, Bid], writes=[BpsK], mark=(h == 3), out=psK[:, h, :],
                        in_=kdT[:, h, t * 128:(t + 1) * 128], identity=identb[:, :])
            C.dve.op(nc.vector.tensor_tensor, reads=[BpsA, Bc], writes=[BAm], out=Am[:], in0=psA[:],
                     in1=bass.AP(tri[:].tensor, tri[:].offset, [list(tri[:].ap[0]), [0, 4], [1, 64]]), op=ALU.mult)
            C.act.op(nc.scalar.copy, reads=[BpsK], writes=[Bkdtok], out=kdtok[:], in_=psK[:])
            for c in range(2):
                rs_ = slice(c * 64, (c + 1) * 64)
                d = psD[c]
                for h in range(4):
                    cs = slice(t * 128 + c * 64, t * 128 + c * 64 + 64)
                    C.pe.op(nc.tensor.matmul, reads=[Bqk, BSbf], writes=[BpsOo], mark=False, out=psOo[rs_, h, :], lhsT=qdT[:, h, cs],
                            rhs=Sbf[:, h, :], start=True, stop=False)
                    C.pe.op(nc.tensor.matmul, reads=[BAm, Bv], writes=[BpsOo], mark=(c == 1 and h == 3), out=psOo[rs_, h, :],
                            lhsT=Am[rs_, h, :], rhs=vtok[rs_, t, h * 128:(h + 1) * 128], start=False, stop=True)
                    C.pe.op(nc.tensor.matmul, reads=[Bkdtok, Bv], writes=[BpsD[c]], mark=(h == 3), out=d[:, h, :], lhsT=kdtok[rs_, h, :],
                            rhs=vtok[rs_, t, h * 128:(h + 1) * 128], start=True, stop=True)
                C.dve.op(nc.vector.tensor_tensor, reads=[BpsD[c], BS], writes=[BS], out=stmp[:], in0=d[:], in1=Sst[:], op=ALU.add)
                eg = eGl[:, :, 2 * t + c]
                C.dve.op(nc.vector.tensor_tensor, reads=[BS, Bqk], writes=[BS], out=Sst[:], in0=stmp[:],
                         in1=bass.AP(eg.tensor, eg.offset, [list(eg.ap[0]), list(eg.ap[1]), [0, 128]]), op=ALU.mult)
                C.act.op(nc.scalar.copy, reads=[BS], writes=[BSbf], out=Sbf[:], in_=Sst[:])
            for h in range(4):
                C.act.op(nc.scalar.activation, reads=[BpsOo], writes=[Bn], out=sq[:], in_=psOo[:, h, :], func=AF.Square,
                         accum_out=nrm[:, h:h + 1])
            C.act.op(nc.scalar.activation, reads=[Bn], writes=[Bn], out=nrm[:, 4:8], in_=nrm[:, 0:4], func=AF.Sqrt, scale=1.0 / 128, bias=EPS)
            C.dve.op(nc.vector.reciprocal, reads=[Bn], writes=[Bn], out=nrm[:, 8:12], in_=nrm[:, 4:8])
            for h in range(4):
                C.dve.op(nc.vector.scalar_tensor_tensor, reads=[BpsOo, Bn, Bc], writes=[Bob], out=ob[:, h * 128:(h + 1) * 128],
                         in0=psOo[:, h, :], scalar=nrm[:, 8 + h:9 + h], in1=hgbc[:, h * 128:(h + 1) * 128], op0=ALU.mult, op1=ALU.mult)
            for j in range(4):
                C.pe.op(nc.tensor.transpose, reads=[Bob, Bid], writes=[BpsX2], mark=(j == 3), out=psX2[:, j, :],
                        in_=ob[:, j * 128:(j + 1) * 128], identity=identb[:, :])
            C.act.op(nc.scalar.copy, reads=[BpsX2], writes=[BoBT], out=oBT[:, :, t * 128:(t + 1) * 128], in_=psX2[:, :, :])
        C.dma(C.sp, sl_misc, o_phg.rearrange("h d e -> d h e"), Sst[:], reads=[BS])
        psOo = psOo_r[0]; BpsOo = BpsOo_r[0]; psX2 = psX2_r[0]; BpsX2 = BpsX2_r[0]
        S0 = [S.sbuf("S0_%d" % i, [128, 4, 128], F32) for i in range(2)]; BS0 = [Buf(), Buf()]
        S0b2 = [S.sbuf("S0b%d" % i, [128, 4, 128], BF16) for i in range(2)]; BS0b2 = [Buf(), Buf()]
        Sn = [S.sbuf("Sn%d" % i, [128, 4, 128], F32) for i in range(2)]; BSn = [Buf(), Buf()]
        Am4 = S.sbuf("Am4", [4, 4, 4], BF16); kd4 = S.sbuf("kd4", [4, 4, 128], BF16); BA4 = Buf(); Bk4 = Buf()
        oTs = S.sbuf("oTs", [128, 4, 64], F32); BoTs = Buf()
        onesb = S.sbuf("onesb", [128, 128], BF16)
        sqs = S.sbuf("sqs", [128, 256], BF16); rss = S.sbuf("rss", [128, 256], F32)
        C.pool.op(nc.gpsimd.memset, writes=[Bc], ap=onesb[:], constant=1.0)
        sl_s0 = [C.slot("s0a"), C.slot("s0b")]; sl_sn = [C.slot("sna"), C.slot("snb")]
        for bb in range(16):
            s = bb % 2
            cs = slice(2048 + 4 * bb, 2048 + 4 * bb + 4)
            S0b = S0b2[s]; BS0b = BS0b2[s]
            if bb == 0:
                C.dma(C.sp, sl_s0[0], S0[0][:], st_hg[0].rearrange("h d e -> d h e"), writes=[BS0[0]])
            if bb + 1 < 16:
                C.dma(C.sp, sl_s0[1 - s], S0[1 - s][:], st_hg[bb + 1].rearrange("h d e -> d h e"), writes=[BS0[1 - s]])
            for kc in range(KC):
                C.pe.op(nc.tensor.matmul, reads=[Bw], writes=[Bps2[s]], mark=(kc == KC - 1), out=ps2[s][0:4, :],
                        lhsT=xnT[:, kc, 2048 + 4 * bb:2048 + 4 * bb + 4], rhs=wib[:, kc, :], start=(kc == 0), stop=(kc == KC - 1))
            C.act.op(nc.scalar.copy, reads=[Bps2[s]], writes=[BvS[s]], out=vS[s][0:4, :], in_=ps2[s][0:4, :])
            C.act.op(nc.scalar.copy, reads=[BS0[s]], writes=[BS0b], out=S0b[:], in_=S0[s][:])
            for h in range(4):
                C.pe.op(nc.tensor.matmul, reads=[Bqk], writes=[BpsA], mark=(h == 3), out=psA[0:4, h, 0:4], lhsT=kdT[:, h, cs],
                        rhs=qdT[:, h, cs], start=True, stop=True)
                C.pe.op(nc.tensor.transpose, reads=[Bqk, Bid], writes=[BpsK], mark=(h == 3), out=psK[0:4, h, :], in_=kdT[:, h, cs],
                        identity=identb[:, :])
            C.dve.op(nc.vector.tensor_tensor, reads=[BpsA, Bc], writes=[BA4], out=Am4[:], in0=psA[0:4, :, 0:4],
                     in1=bass.AP(tri[0:4, 0:4].tensor, tri[0:4, 0:4].offset, [list(tri[0:4, 0:4].ap[0]), [0, 4], [1, 4]]), op=ALU.mult)
            C.act.op(nc.scalar.copy, reads=[BpsK], writes=[Bk4], out=kd4[:], in_=psK[0:4, :, :])
            for h in range(4):
                C.pe.op(nc.tensor.matmul, reads=[Bqk, BS0b], writes=[BpsOo], mark=False, out=psOo[:, h, 0:4], lhsT=S0b[:, h, :],
                        rhs=qdT[:, h, cs], start=True, stop=False)
                C.pe.op(nc.tensor.matmul, reads=[BA4, BvS[s], BS0b], writes=[BpsOo], mark=(h == 3), out=psOo[:, h, 0:4],
                        lhsT=vS[s][0:4, h * 128:(h + 1) * 128], rhs=Am4[0:4, h, :], start=False, stop=True)
                C.pe.op(nc.tensor.matmul, reads=[Bk4, BvS[s]], writes=[BpsD[s]], mark=(h == 3), out=psD[s][:, h, :], lhsT=kd4[0:4, h, :],
                        rhs=vS[s][0:4, h * 128:(h + 1) * 128], start=True, stop=True)
            C.dve.op(nc.vector.tensor_tensor, reads=[BpsD[s], BS0[s]], writes=[BS], out=stmp[:], in0=psD[s][:], in1=S0[s][:], op=ALU.add)
            eg = eGls[:, :, bb]
            C.dve.op(nc.vector.tensor_tensor, reads=[BS, Bqk], writes=[BSn[s]], out=Sn[s][:], in0=stmp[:],
                     in1=bass.AP(eg.tensor, eg.offset, [list(eg.ap[0]), list(eg.ap[1]), [0, 128]]), op=ALU.mult)
            C.dma(C.sp, sl_sn[s], o_shg[bb].rearrange("h d e -> d h e"), Sn[s][:], reads=[BSn[s]])
            C.act.op(nc.scalar.copy, reads=[BpsOo], writes=[BoTs], out=oTs[:, :, 4 * bb:4 * bb + 4], in_=psOo[:, :, 0:4])
        C.act.op(nc.scalar.activation, reads=[BoTs], writes=[Bn], out=sqs[:], in_=oTs[:].rearrange("p a b -> p (a b)"), func=AF.Square)
        C.pe.op(nc.tensor.matmul, reads=[Bn, Bc], writes=[Bps2[0]], out=ps2[0][:, 0:256], lhsT=onesb[:, :], rhs=sqs[:, :], start=True, stop=True)
        C.act.op(nc.scalar.activation, reads=[Bps2[0]], writes=[Bn], out=rss[:], in_=ps2[0][:, 0:256], func=AF.Sqrt, scale=1.0 / 128, bias=EPS)
        C.dve.op(nc.vector.reciprocal, reads=[Bn], writes=[Bn], out=rss[:], in_=rss[:])
        C.dve.op(nc.vector.tensor_tensor, reads=[Bn, BoTs], writes=[BoTs], out=oTs[:].rearrange("p a b -> p (a b)"),
                 in0=oTs[:].rearrange("p a b -> p (a b)"), in1=rss[:], op=ALU.mult)
        for h in range(4):
            C.dve.op(nc.vector.tensor_scalar, reads=[BoTs, Bc], writes=[BoBT], out=oBT[:, h, 2048:NTOK], in0=oTs[:, h, :],
                     scalar1=hgcol[:, h:h + 1], scalar2=None, op0=ALU.mult)
        S.close()

    if "F" in phases:
        S = C.scope()
        y2T = S.sbuf("y2T", [128, KC, NTOK], BF16)
        wba = S.sbuf("wba", [128, 4, D], BF16); wbb = S.sbuf("wbb", [128, 4, D], BF16)
        wo = S.sbuf("wo", [128, KC, D], BF16)
        wch = [S.sbuf("wchF%d" % i, [128, KC, 256], BF16) for i in range(3)]; Bwch = [Buf() for _ in range(3)]
        fgb = S.sbuf("fgb", [128, D], F32)
        sg = [S.sbuf("sg%d" % i, [128, 512], F32) for i in range(2)]; Bsg = [Buf(), Buf()]
        t1 = S.sbuf("t1", [128, 512], F32); Bt1 = Buf()
        xr = [S.sbuf("xr%d" % i, [128, D], F32) for i in range(3)]; Bxr = [Buf() for _ in range(3)]
        hh = [S.sbuf("hh%d" % i, [128, D], F32) for i in range(2)]; Bhh = [Buf(), Buf()]
        junkf = S.sbuf("junkf", [128, D], BF16); Bj = Buf()
        nf = S.sbuf("nf", [128, NT, 4], F32)
        ps = [S.psum("psF%d" % i, [128, 512], F32) for i in range(6)]; Bps = [Buf() for _ in range(6)]
        Bw = Buf(); By = Buf(); Bo = Buf()
        sl_xr = [C.slot("xr%d" % i) for i in range(3)]; sl_y = [C.slot("y%d" % i) for i in range(3)]
        C.dma(C.sp, sl_misc, fgb[:], dram_bcast(final_g, 128), writes=[Bw])
        wi = [0]

        def next_w(c0, n):
            s = wi[0] % 3; wi[0] += 1
            load_w(wch[s], Bwch[s], w_in[:, c0:c0 + n], n)
            return wch[s], Bwch[s]
        pr = [0]
        for (c0, oT) in ((C_ZA, oAT), (C_ZB, oBT)):
            for piece in range(2):
                wt, Bwt = next_w(c0 + piece * 256, 256)
                for jj in range(2):
                    j = piece * 2 + jj
                    for st in range(5):
                        n = 512 if st < 4 else 64
                        cs = slice(st * 512, st * 512 + n)
                        b = pr[0] % 6; pr[0] += 1
                        for kc in range(KC):
                            C.pe.op(nc.tensor.matmul, reads=[Bwt], writes=[Bps[b]], mark=(kc == KC - 1), out=ps[b][:, 0:n],
                                    lhsT=wt[:, kc, jj * 128:(jj + 1) * 128], rhs=xnT[:, kc, cs], start=(kc == 0), stop=(kc == KC - 1))
                        s = b % 2
                        C.act.op(nc.scalar.activation, reads=[Bps[b]], writes=[Bsg[s]], out=sg[s][:, 0:n], in_=ps[b][:, 0:n], func=AF.Silu)
                        C.dve.op(nc.vector.tensor_tensor, reads=[Bsg[s], Bo], writes=[Bo], out=oT[:, j, cs], in0=oT[:, j, cs], in1=sg[s][:, 0:n],
                                 op=ALU.mult)
        load_w(wba, Bw, w_ba, D, kcn=4)
        load_w(wbb, Bw, w_bb, D, kcn=4)
        load_w(wo, Bw, w_out, D)
        for piece in range(4):
            wtA, BwA = next_w(C_MA + piece * 256, 256)
            wtB, BwB = next_w(C_MB + piece * 256, 256)
            for jj in range(2):
                cc = piece * 2 + jj
                for st in range(5):
                    n = 512 if st < 4 else 64
                    cs = slice(st * 512, st * 512 + n)
                    bA, bB, bMA, bMB = [(pr[0] + i) % 6 for i in range(4)]; pr[0] += 4
                    for (b, wsrc, oT) in ((bA, wba, oAT), (bB, wbb, oBT)):
                        for k in range(4):
                            C.pe.op(nc.tensor.matmul, reads=[Bw, Bo], writes=[Bps[b]], mark=(k == 3), out=ps[b][:, 0:n],
                                    lhsT=wsrc[:, k, cc * 128:(cc + 1) * 128], rhs=oT[:, k, cs], start=(k == 0), stop=(k == 3))
                    for (b, wt, Bwt) in ((bMA, wtA, BwA), (bMB, wtB, BwB)):
                        for kc in range(KC):
                            C.pe.op(nc.tensor.matmul, reads=[Bwt], writes=[Bps[b]], mark=(kc == KC - 1), out=ps[b][:, 0:n],
                                    lhsT=wt[:, kc, jj * 128:(jj + 1) * 128], rhs=xnT[:, kc, cs], start=(kc == 0), stop=(kc == KC - 1))
                    for (i, bm) in enumerate((bMA, bMB)):
                        C.act.op(nc.scalar.activation, reads=[Bps[bm]], writes=[Bsg[i]], out=sg[i][:, 0:n], in_=ps[bm][:, 0:n], func=AF.Sigmoid)
                    C.dve.op(nc.vector.tensor_tensor, reads=[Bsg[0], Bps[bA]], writes=[Bt1], out=t1[:, 0:n], in0=sg[0][:, 0:n], in1=ps[bA][:, 0:n],
                             op=ALU.mult)
                    C.dve.op(nc.vector.tensor_tensor, reads=[Bsg[1], Bps[bB]], writes=[Bsg[1]], out=sg[1][:, 0:n], in0=sg[1][:, 0:n],
                             in1=ps[bB][:, 0:n], op=ALU.mult)
                    C.dve.op(nc.vector.tensor_tensor, reads=[Bsg[1], Bt1], writes=[By], out=y2T[:, cc, cs], in0=sg[1][:, 0:n], in1=t1[:, 0:n],
                             op=ALU.add)
        def xload(t):
            C.dma(C.sp, sl_xr[t % 3], xr[t % 3][0:rows(t), :], x_all[t * 128:t * 128 + rows(t), :], writes=[Bxr[t % 3]])
        xload(0); xload(1)
        for t in range(NT):
            r = rows(t); s = t % 2
            if t + 2 < NT:
                xload(t + 2)
            b0 = pr[0] % 6; b1 = (pr[0] + 1) % 6; pr[0] += 2
            for (b, half) in ((b0, 0), (b1, 1)):
                for c2 in range(KC):
                    C.pe.op(nc.tensor.matmul, reads=[By, Bw], writes=[Bps[b]], mark=(c2 == KC - 1), out=ps[b][0:r, :],
                            lhsT=y2T[:, c2, t * 128:t * 128 + r], rhs=wo[:, c2, half * 512:(half + 1) * 512], start=(c2 == 0), stop=(c2 == KC - 1))
                C.dve.op(nc.vector.tensor_tensor, reads=[Bps[b], Bxr[t % 3]], writes=[Bhh[s]], out=hh[s][0:r, half * 512:(half + 1) * 512],
                         in0=ps[b][0:r, :], in1=xr[t % 3][0:r, half * 512:(half + 1) * 512], op=ALU.add)
            C.act.op(nc.scalar.activation, reads=[Bhh[s]], writes=[Bj], out=junkf[0:r, :], in_=hh[s][0:r, :], func=AF.Square,
                     accum_out=nf[0:r, t, 0:1])
            C.act.op(nc.scalar.activation, reads=[Bj], writes=[Bj], out=nf[0:r, t, 1:2], in_=nf[0:r, t, 0:1], func=AF.Sqrt, scale=1.0 / D, bias=EPS)
            C.dve.op(nc.vector.reciprocal, reads=[Bj], writes=[Bj], out=nf[0:r, t, 2:3], in_=nf[0:r, t, 1:2])
            C.dve.op(nc.vector.scalar_tensor_tensor, reads=[Bhh[s], Bj, Bw], writes=[Bxr[t % 3]], out=xr[t % 3][0:r, :], in0=hh[s][0:r, :],
                     scalar=nf[0:r, t, 2:3], in1=fgb[0:r, :], op0=ALU.mult, op1=ALU.mult)
            C.dma(C.sp, sl_y[t % 3], o_y[t * 128:t * 128 + r, :], xr[t % 3][0:r, :], reads=[Bxr[t % 3]])
        S.close()

    C.finish()
    return nc


def _t5_bucket(rel):
    n = np.maximum(rel, 0)
    nf = np.maximum(n, 1).astype(np.float32)
    large = 16 + (np.log(nf / np.float32(16)) / np.float32(np.log(np.float32(8.0))) * np.float32(16)).astype(np.int32)
    large = np.minimum(large, 31)
    return np.where(n < 16, n, large)


def make_consts():
    c = {}
    ohx = np.zeros((33, TW), np.float32)
    for m in range(TW):
        rel = m - TOFF
        if rel >= 0:
            ohx[int(_t5_bucket(np.array(rel))), m] += 1.0
            ohx[31, m] -= 1.0
        else:
            ohx[32, m] = NEG
    c["c_ohx"] = ohx
    cs = np.arange(127) * 16; ce = cs + 32
    bs = np.arange(32) * 64; be = bs + 64
    ov = np.clip(np.minimum(ce[:, None], be[None, :]) - np.maximum(cs[:, None], bs[None, :]), 0, None) / 16
    c["c_cover"] = ov.astype(np.float32)
    selc = np.zeros((128, 2, 16, 32), np.float32)
    for qt in range(16):
        t = qt * 128 + np.arange(128)[:, None]
        j = np.arange(32)[None, :]
        valid = (j * 64) <= t
        cur = t // 64
        forced = (j == 0) | (j == cur) | (j == cur - 1)
        selc[:, 0, qt, :] = (valid & ~forced).astype(np.float32)
        selc[:, 1, qt, :] = np.where(valid, np.where(forced, 1e9, 0.0), -1.0)
    c["c_selc"] = selc
    p = np.arange(128)[:, None] % 64
    c["c_tri"] = (p <= np.arange(64)[None, :]).astype(np.float32)
    c["c_pm8"] = (np.arange(128) % 8).astype(np.float32).reshape(128, 1)
    return c


_NC_CACHE = {}
PHASES = ("A", "B", "T", "C", "D", "E", "F")


def kernel(**inp):
    f = lambda k: np.ascontiguousarray(np.asarray(inp[k]))
    xp = f("x_prompt"); xsm = f("x_sample")
    consts = make_consts()
    if "nc" not in _NC_CACHE:
        _NC_CACHE["nc"] = build(PHASES)
    nc = _NC_CACHE["nc"]
    shared = {
        "w_in": f("w_in")[0], "norm_g": f("norm_g"), "final_g": f("final_g").reshape(1, D), "rel_bias": f("rel_bias"),
        "pos_k": f("cmp_pos_k")[0], "pos_v": f("cmp_pos_v")[0], "w1_k": f("cmp_w1_k")[0], "w1_v": f("cmp_w1_v")[0],
        "w2_k": f("cmp_w2_k")[0], "w2_v": f("cmp_w2_v")[0], "hg_lower": f("hg_lower"), "hg_norm_g": f("hg_norm_g"),
        "w_ba": f("w_branch_a")[0], "w_bb": f("w_branch_b")[0], "w_out": f("w_out")[0],
        "c_ck": f("cache_cmp_k").reshape(2560 * 8, 2048), "c_cv": f("cache_cmp_v").reshape(2560 * 8, 2048),
        "c_sk": f("cache_sel_k").reshape(2560 * 8, 2048), "c_sv": f("cache_sel_v").reshape(2560 * 8, 2048),
    }
    shared.update(consts)
    swk = f("state_win_k")[0].reshape(128, 512, 128); swv = f("state_win_v")[0].reshape(128, 512, 128)
    shg = f("state_hgrn")[0]
    pt = f("page_table").astype(np.int32)
    in_maps = []
    for c in range(8):
        m = dict(shared)
        m["x_all"] = np.concatenate([xp[c], xsm[16 * c:16 * c + 16].reshape(64, D)], axis=0)
        m["st_wk"] = swk[16 * c:16 * c + 16]; m["st_wv"] = swv[16 * c:16 * c + 16]
        m["st_hg"] = shg[16 * c:16 * c + 16]
        m["ptab"] = pt[16 * c:16 * c + 16].reshape(1, 256)
        in_maps.append(m)
    res = run_bass_kernel_spmd(nc, in_maps, core_ids=list(range(8)))
    R = res.results
    y_p = np.stack([R[c]["o_y"][0:2048] for c in range(8)], 0)
    y_s = np.concatenate([R[c]["o_y"][2048:].reshape(16, 4, D) for c in range(8)], 0)
    outs = [y_p, y_s]
    for j in range(4):
        outs.append(np.stack([R[c]["o_pkv"][j].reshape(2048, 2, 64) for c in range(8)], 0)[None])
    for j in range(2):
        outs.append(np.stack([R[c]["o_pwin"][j].reshape(512, 2, 64) for c in range(8)], 0)[None])
    outs.append(np.stack([R[c]["o_phg"] for c in range(8)], 0)[None])
    for j in range(4):
        outs.append(np.concatenate([R[c]["o_skv"][j].reshape(16, 4, 2, 64) for c in range(8)], 0)[None])
    for j in range(2):
        outs.append(np.concatenate([R[c]["o_swin"][j].reshape(16, 512, 2, 64) for c in range(8)], 0)[None])
    outs.append(np.concatenate([R[c]["o_shg"] for c in range(8)], 0)[None])
    return tuple(np.ascontiguousarray(o.astype(np.float32)) for o in outs)
```
